# Optimizing a Trainium2 kernel written in Bass

```python
import jax, jax.numpy as jnp
from jax import lax
import numpy as np


D_MODEL = 2048
BATCH = 4
SEQ = 4096
DEPTH = 2

HEAD_DIM = 128
CONV_CH = 1024
CONV_GROUPS = CONV_CH // HEAD_DIM
CONV_WIDTH = 31
MOBA_HEADS = 8
MOBA_DIM = MOBA_HEADS * HEAD_DIM
MOBA_BLOCK = 256
MOBA_TOPK = 3
MOBA_Q_CHUNK = 32
SB_HEADS = 16
SB_DIM = SB_HEADS * HEAD_DIM
SB_Q_BLOCK = 128
D_FF = 5632
FFN_CONV_WIDTH = 3
RMS_EPS = 1e-6
LN_EPS = 1e-5
EVEN_IN = 2 * CONV_CH + 3 * MOBA_DIM
EVEN_MIX = CONV_CH + MOBA_DIM
N_EVEN = (DEPTH + 1) // 2
N_ODD = DEPTH // 2

kernel_name = "hybrid_conformer_moba_stickbreak_block"


def rms_norm(x, g):
    xf = x.astype(jnp.float32)
    y = xf * lax.rsqrt(jnp.mean(xf * xf, axis=-1, keepdims=True) + RMS_EPS)
    return y.astype(x.dtype) * g


def causal_depthwise_conv(x, w, b):
    width = w.shape[0]
    xp = jnp.pad(x, ((0, 0), (width - 1, 0), (0, 0)))
    y = lax.conv_general_dilated(xp, w[:, None, :].astype(x.dtype), window_strides=(1,), padding='VALID',
                                 dimension_numbers=('NWC', 'WIO', 'NWC'), feature_group_count=x.shape[-1])
    return y + b.astype(x.dtype)


def conformer_conv(u, w_dw, b_dw, ln_g, ln_b):
    a, g = jnp.split(u, 2, axis=-1)
    h = causal_depthwise_conv(a * jax.nn.sigmoid(g), w_dw, b_dw)
    hf = h.astype(jnp.float32)
    mu = jnp.mean(hf, axis=-1, keepdims=True)
    var = jnp.mean(jnp.square(hf - mu), axis=-1, keepdims=True)
    h = ((hf - mu) * lax.rsqrt(var + LN_EPS)).astype(h.dtype) * ln_g + ln_b
    return jax.nn.silu(h)


def moba_attention(q, k, v):
    bsz, nh, t_len, dh = q.shape
    nb = -(-t_len // MOBA_BLOCK)
    pad = ((0, 0), (0, 0), (0, nb * MOBA_BLOCK - t_len), (0, 0))
    kb = jnp.pad(k, pad).reshape(bsz, nh, nb, MOBA_BLOCK, dh)
    vb = jnp.pad(v, pad).reshape(bsz, nh, nb, MOBA_BLOCK, dh)
    kmean = jnp.mean(kb.astype(jnp.float32), axis=3).astype(k.dtype)
    topk = min(MOBA_TOPK, nb)
    scale = dh ** -0.5
    n_chunks = t_len // MOBA_Q_CHUNK
    qc = q.reshape(bsz, nh, n_chunks, MOBA_Q_CHUNK, dh).transpose(2, 0, 1, 3, 4)
    gather = jax.vmap(jax.vmap(lambda blocks, ix: blocks[ix]))
    blk_ids = jnp.arange(nb)
    slot_ids = jnp.arange(topk)

    def chunk(args):
        ci, qi = args
        t0 = ci * MOBA_Q_CHUNK
        own = t0 // MOBA_BLOCK
        qpos = t0 + jnp.arange(MOBA_Q_CHUNK)
        gate = jnp.einsum('bhqd,bhnd->bhqn', qi, kmean).astype(jnp.float32)
        gate = jnp.where(blk_ids < own, gate, -jnp.inf)
        _, idx = lax.top_k(gate, topk)
        slot_ok = slot_ids < own
        ksel = gather(kb, idx)
        vsel = gather(vb, idx)
        s_sel = jnp.einsum('bhqd,bhqjsd->bhqjs', qi, ksel).astype(jnp.float32) * scale
        s_sel = jnp.where(slot_ok[:, None], s_sel, -jnp.inf).reshape(bsz, nh, MOBA_Q_CHUNK, topk * MOBA_BLOCK)
        kown = lax.dynamic_index_in_dim(kb, own, axis=2, keepdims=False)
        vown = lax.dynamic_index_in_dim(vb, own, axis=2, keepdims=False)
        kpos = own * MOBA_BLOCK + jnp.arange(MOBA_BLOCK)
        s_own = jnp.einsum('bhqd,bhsd->bhqs', qi, kown).astype(jnp.float32) * scale
        s_own = jnp.where(kpos[None, :] <= qpos[:, None], s_own, -jnp.inf)
        p = jax.nn.softmax(jnp.concatenate([s_sel, s_own], axis=-1), axis=-1).astype(v.dtype)
        p_sel = p[..., :topk * MOBA_BLOCK].reshape(bsz, nh, MOBA_Q_CHUNK, topk, MOBA_BLOCK)
        p_own = p[..., topk * MOBA_BLOCK:]
        return (jnp.einsum('bhqjs,bhqjsd->bhqd', p_sel, vsel)
                + jnp.einsum('bhqs,bhsd->bhqd', p_own, vown))

    out = lax.map(chunk, (jnp.arange(n_chunks), qc))
    return out.transpose(1, 2, 0, 3, 4).reshape(bsz, nh, t_len, dh)


def stick_breaking_attention(q, k, v):
    bsz, nh, t_len, dh = q.shape
    nblk = t_len // SB_Q_BLOCK
    qb = q.reshape(bsz, nh, nblk, SB_Q_BLOCK, dh).transpose(2, 0, 1, 3, 4)
    kpos = jnp.arange(t_len)
    scale = dh ** -0.5

    def block(args):
        bi, qi = args
        qpos = bi * SB_Q_BLOCK + jnp.arange(SB_Q_BLOCK)
        z = jnp.einsum('bhqd,bhsd->bhqs', qi, k).astype(jnp.float32) * scale
        strict = kpos[None, :] < qpos[:, None]
        log_1m = jnp.where(strict, jax.nn.log_sigmoid(-z), 0.0)
        between = lax.cumsum(log_1m, axis=log_1m.ndim - 1, reverse=True) - log_1m
        a = jnp.where(strict, jnp.exp(jax.nn.log_sigmoid(z) + between), 0.0).astype(v.dtype)
        return jnp.einsum('bhqs,bhsd->bhqd', a, v)

    out = lax.map(block, (jnp.arange(nblk), qb))
    return out.transpose(1, 2, 0, 3, 4).reshape(bsz, nh, t_len, dh)


def split_heads(t, n_heads):
    b, s, _ = t.shape
    return t.reshape(b, s, n_heads, HEAD_DIM).transpose(0, 2, 1, 3)


def merge_heads(t):
    b, h, s, d = t.shape
    return t.transpose(0, 2, 1, 3).reshape(b, s, h * d)


def even_mixer(h, w_in, conv_w, conv_b, ln_g, ln_b, w_out):
    proj = h @ w_in
    u_conv = proj[..., :2 * CONV_CH]
    q, k, v = jnp.split(proj[..., 2 * CONV_CH:], 3, axis=-1)
    y_a = conformer_conv(u_conv, conv_w, conv_b, ln_g, ln_b)
    y_b = merge_heads(moba_attention(split_heads(q, MOBA_HEADS), split_heads(k, MOBA_HEADS),
                                     split_heads(v, MOBA_HEADS)))
    return jnp.concatenate([y_a, y_b], axis=-1) @ w_out


def odd_mixer(h, w_qkv, w_o):
    q, k, v = jnp.split(h @ w_qkv, 3, axis=-1)
    y = stick_breaking_attention(split_heads(q, SB_HEADS), split_heads(k, SB_HEADS), split_heads(v, SB_HEADS))
    return merge_heads(y) @ w_o


def conv_glu_ffn(h, w_up, w_gate, conv_w, conv_b, w_down):
    u = causal_depthwise_conv(h @ w_up, conv_w, conv_b)
    return (jax.nn.silu(u) * (h @ w_gate)) @ w_down


def setup_inputs(seed: int = 0) -> dict:
    key = jax.random.key(seed)
    ks = jax.random.split(key, 20)
    f32 = jnp.float32
    nrm = lambda k, shape, s: jax.random.normal(k, shape, f32) * s
    return {
        "x": nrm(ks[0], (BATCH, SEQ, D_MODEL), 1.0),
        "mix_norm": 1.0 + nrm(ks[1], (DEPTH, D_MODEL), 0.05),
        "ffn_norm": 1.0 + nrm(ks[2], (DEPTH, D_MODEL), 0.05),
        "even_w_in": nrm(ks[3], (N_EVEN, D_MODEL, EVEN_IN), D_MODEL ** -0.5),
        "even_conv_w": nrm(ks[4], (N_EVEN, CONV_WIDTH, CONV_CH), CONV_WIDTH ** -0.5),
        "even_conv_b": nrm(ks[5], (N_EVEN, CONV_CH), 0.01),
        "even_ln_g": 1.0 + nrm(ks[6], (N_EVEN, CONV_CH), 0.05),
        "even_ln_b": nrm(ks[7], (N_EVEN, CONV_CH), 0.01),
        "even_w_out": nrm(ks[8], (N_EVEN, EVEN_MIX, D_MODEL), EVEN_MIX ** -0.5),
        "odd_w_qkv": nrm(ks[9], (N_ODD, D_MODEL, 3 * SB_DIM), D_MODEL ** -0.5),
        "odd_w_o": nrm(ks[10], (N_ODD, SB_DIM, D_MODEL), SB_DIM ** -0.5),
        "ffn_w_up": nrm(ks[11], (DEPTH, D_MODEL, D_FF), D_MODEL ** -0.5),
        "ffn_w_gate": nrm(ks[12], (DEPTH, D_MODEL, D_FF), D_MODEL ** -0.5),
        "ffn_conv_w": nrm(ks[13], (DEPTH, FFN_CONV_WIDTH, D_FF), FFN_CONV_WIDTH ** -0.5),
        "ffn_conv_b": nrm(ks[14], (DEPTH, D_FF), 0.01),
        "ffn_w_down": nrm(ks[15], (DEPTH, D_FF, D_MODEL), D_FF ** -0.5),
        "final_norm": 1.0 + nrm(ks[16], (D_MODEL,), 0.05),
    }


def reference(x, mix_norm, ffn_norm, even_w_in, even_conv_w, even_conv_b, even_ln_g, even_ln_b, even_w_out,
              odd_w_qkv, odd_w_o, ffn_w_up, ffn_w_gate, ffn_conv_w, ffn_conv_b, ffn_w_down, final_norm):
    for layer in range(DEPTH):
        h = rms_norm(x, mix_norm[layer])
        j = layer // 2
        if layer % 2 == 0:
            x = x + even_mixer(h, even_w_in[j], even_conv_w[j], even_conv_b[j], even_ln_g[j], even_ln_b[j],
                               even_w_out[j])
        else:
            x = x + odd_mixer(h, odd_w_qkv[j], odd_w_o[j])
        h = rms_norm(x, ffn_norm[layer])
        x = x + conv_glu_ffn(h, ffn_w_up[layer], ffn_w_gate[layer], ffn_conv_w[layer], ffn_conv_b[layer],
                             ffn_w_down[layer])
    return rms_norm(x, final_norm)
```

```python
import numpy as np
import ml_dtypes
from contextlib import ExitStack
import concourse.bass as bass
import concourse.mybir as mybir
from concourse.bass_utils import run_bass_kernel_spmd

F32 = mybir.dt.float32
BF16 = mybir.dt.bfloat16
AF = mybir.ActivationFunctionType
ALU = mybir.AluOpType
AX = mybir.AxisListType

D = 2048
KC = 16
NW = 4096
NOWN = 2048
NCR = 2176
NU = 2304
DFF = 5632
FC = 44
HD = 128
RMS_EPS = 1e-6
LN_EPS = 1e-5
CR_TILES = [(0, 512), (512, 512), (1024, 512), (1536, 512), (2048, 128)]
W_TILES = [(i * 512, 512) for i in range(8)]
SBUF_WORDS = 49 * 1024


class Res:
    __slots__ = ("name", "last_w", "readers", "sem", "sem_total", "last_dma")

    def __init__(self, name):
        self.name = name
        self.last_w = None
        self.readers = []
        self.sem = None
        self.sem_total = 0
        self.last_dma = None


class Ins:
    __slots__ = ("eng", "fn", "deps", "is_dma", "res", "sem", "sem_val", "need_sig", "sig_val")

    def __init__(self, eng, fn):
        self.eng = eng
        self.fn = fn
        self.deps = []
        self.is_dma = False
        self.res = None
        self.sem = None
        self.sem_val = 0
        self.need_sig = False
        self.sig_val = 0


class Prog:
    ENG = ["pe", "act", "dve", "pool", "sp"]
    COMPUTE = ["pe", "act", "dve", "pool"]

    def __init__(self, nc, es):
        self.nc = nc
        self.es = es
        self.streams = {e: [] for e in self.ENG}
        self.esem = {e: es.enter_context(nc.semaphore("prog_" + e)) for e in self.COMPUTE}
        self.dma_res = []
        self.sem_pool = []
        self.nsem = 0
        self.big = es.enter_context(nc.sbuf_tensor("bigbuf", [128, SBUF_WORDS], F32))
        self.top = 0
        self.floor = 0
        self.banks = [es.enter_context(nc.psum_tensor("bank%d" % i, [128, 512], F32)) for i in range(8)]
        self.rbanks = [Res("bank%d" % i) for i in range(8)]
        self.nbank = 0
        self.rot = list(range(8))
        self.free_sems = []

    def alloc(self, n, dtype):
        words = (n * (2 if dtype == BF16 else 4) + 3) // 4
        words = (words + 7) // 8 * 8
        a = self.big[:, self.top:self.top + words]
        self.top += words
        assert self.top <= SBUF_WORDS, "SBUF overflow %d" % self.top
        if dtype == BF16:
            return a.bitcast(BF16)[:, 0:n]
        return a[:, 0:n]

    def reset(self):
        self.top = self.floor
        self.rot = list(range(8))

    def bank(self):
        i = self.rot[self.nbank % len(self.rot)]
        self.nbank += 1
        return self.banks[i], self.rbanks[i]

    def fixed(self, i):
        return self.banks[i], self.rbanks[i]

    def res(self, name):
        return Res(name)

    def add(self, eng, fn, reads=(), writes=(), dma=None, ndma=1, extra=()):
        ins = Ins(eng, fn)
        deps = []
        for r in reads:
            if r.last_w is not None:
                deps.append(r.last_w)
        for w in writes:
            lw = w.last_w
            if lw is not None:
                if not (eng == "pe" and lw.eng == "pe" and not lw.is_dma and not w.readers):
                    deps.append(lw)
            for rd in w.readers:
                deps.append(rd)
        deps.extend(extra)
        if dma is not None:
            ins.is_dma = True
            ins.res = dma
            if dma.sem is None:
                if self.sem_pool:
                    dma.sem, dma.sem_total = self.sem_pool.pop(0)
                else:
                    dma.sem = self.es.enter_context(self.nc.semaphore("d%d" % self.nsem))
                    dma.sem_total = 0
                    self.nsem += 1
                self.dma_res.append(dma)
            if dma.last_dma is not None:
                deps.append(dma.last_dma)
            dma.sem_total += 16 * ndma
            ins.sem = dma.sem
            ins.sem_val = dma.sem_total
            dma.last_dma = ins
        for r in reads:
            r.readers.append(ins)
        for w in writes:
            w.last_w = ins
            w.readers = []
        seen = set()
        for d in deps:
            if d is ins or id(d) in seen:
                continue
            seen.add(id(d))
            ins.deps.append(d)
        self.streams[eng].append(ins)
        return ins

    def barrier(self):
        lasts = []
        for e in self.ENG:
            for ins in reversed(self.streams[e]):
                if not ins.is_dma and ins.fn is not None:
                    lasts.append(ins)
                    break
        dmas = [r.last_dma for r in self.dma_res if r.last_dma is not None]
        for e in self.ENG:
            self.add(e, None, extra=lasts + dmas)
        for r in self.dma_res:
            self.sem_pool.append((r.sem, r.sem_total))
            r.sem = None
            r.last_dma = None
        self.dma_res = []

    def mm(self, out, lhsT, rhs, start, stop, reads, writes):
        return self.add("pe", lambda e: e.matmul(out, lhsT=lhsT, rhs=rhs, start=start, stop=stop), reads, writes)

    def tr(self, out, in_, ident, reads, writes):
        return self.add("pe", lambda e: e.transpose(out, in_, ident), reads, writes)

    def act(self, out, in_, func, reads, writes, scale=1.0, bias=None, accum_out=None):
        def f(e):
            kw = {}
            if bias is not None:
                kw["bias"] = bias
            if accum_out is not None:
                kw["accum_out"] = accum_out
            return e.activation(out=out, in_=in_, func=func, scale=scale, **kw)
        return self.add("act", f, reads, writes)

    def tt(self, eng, out, in0, in1, op, reads, writes):
        return self.add(eng, lambda e: e.tensor_tensor(out=out, in0=in0, in1=in1, op=op), reads, writes)

    def ts(self, eng, out, in0, s1, s2, op0, op1, reads, writes):
        if op1 is None:
            return self.add(eng, lambda e: e.tensor_scalar(out=out, in0=in0, scalar1=s1, scalar2=None, op0=op0), reads, writes)
        return self.add(eng, lambda e: e.tensor_scalar(out=out, in0=in0, scalar1=s1, scalar2=s2, op0=op0, op1=op1), reads, writes)

    def stt(self, out, in0, scalar, in1, op0, op1, reads, writes):
        return self.add("dve", lambda e: e.scalar_tensor_tensor(out=out, in0=in0, scalar=scalar, in1=in1, op0=op0, op1=op1), reads, writes)

    def copy(self, eng, out, in_, reads, writes):
        if eng == "act":
            return self.add("act", lambda e: e.activation(out=out, in_=in_, func=AF.Copy), reads, writes)
        return self.add(eng, lambda e: e.tensor_copy(out=out, in_=in_), reads, writes)

    def memset(self, eng, ap, val, writes):
        return self.add(eng, lambda e: e.memset(ap, val), (), writes)

    def dma(self, q, out, in_, reads, writes, res, **kw):
        return self.add(q, lambda e: e.dma_start(out=out, in_=in_, **kw), reads, writes, dma=res)

    def _prepare(self):
        for e in self.ENG:
            for ins in self.streams[e]:
                for d in ins.deps:
                    if not d.is_dma:
                        d.need_sig = True
        for e in self.ENG:
            c = 0
            for ins in self.streams[e]:
                if not ins.is_dma and ins.need_sig:
                    assert ins.fn is not None
                    c += 1
                    ins.sig_val = c

    def _emit_one(self, e, eng):
        waited = {}
        for ins in self.streams[e]:
            need = {}
            for d in ins.deps:
                if d.is_dma:
                    key = ("d", id(d.sem))
                    sem = d.sem
                    val = d.sem_val
                else:
                    key = ("e", d.eng)
                    sem = self.esem[d.eng]
                    val = d.sig_val
                if waited.get(key, 0) >= val:
                    continue
                if key not in need or need[key][1] < val:
                    need[key] = (sem, val)
            for key, (sem, val) in need.items():
                eng.wait_ge(sem, val)
                waited[key] = val
            if ins.fn is None:
                continue
            r = ins.fn(eng)
            if ins.is_dma:
                r.then_inc(ins.sem, 16)
            elif ins.need_sig:
                r.then_inc(self.esem[ins.eng], 1)


def build_program(nc, body):
    with ExitStack() as es:
        p = Prog(nc, es)
        body(p)
        p.barrier()
        p._prepare()
        block = es.enter_context(nc.Block())

        def sect(name):
            def f(eng):
                p._emit_one(name, eng)
            return f

        block.tensor(sect("pe"))
        block.scalar(sect("act"))
        block.vector(sect("dve"))
        block.gpsimd(sect("pool"))
        block.sync(sect("sp"))
    return nc


class Consts:
    pass


def setup_consts(p, small_dram):
    c = Consts()
    c.r = Res("consts")
    c.ones_bf = p.alloc(128, BF16)
    c.ones_f = p.alloc(128, F32)
    c.ident_bf = p.alloc(128, BF16)
    c.zeros = p.alloc(512, F32)
    c.causal_ge = p.alloc(128, F32)
    c.eps_rms = p.alloc(1, F32)
    c.eps_ln = p.alloc(1, F32)
    identf = p.alloc(128, F32)
    c.ffn_halo = p.alloc(FC * 2, F32).rearrange("p (j k) -> p j k", j=FC)
    c.r_halo = Res("ffn_halo")
    p.memset("pool", c.ones_bf, 1.0, [c.r])
    p.memset("pool", c.ones_f, 1.0, [c.r])
    p.memset("pool", c.zeros, 0.0, [c.r])
    p.memset("pool", c.eps_rms, RMS_EPS, [c.r])
    p.memset("pool", c.eps_ln, LN_EPS, [c.r])
    p.memset("pool", c.causal_ge, 0.0, [c.r])
    p.add("pool", lambda e: e.affine_select(out=c.causal_ge, in_=c.causal_ge, pattern=[[1, 128]], base=0,
                                            channel_multiplier=-1, compare_op=ALU.is_ge, fill=-1e5), [c.r], [c.r])
    p.memset("pool", identf, 0.0, [c.r])
    p.add("pool", lambda e: e.affine_select(out=identf, in_=identf, pattern=[[1, 128]], base=0,
                                            channel_multiplier=-1, compare_op=ALU.not_equal, fill=1.0), [c.r], [c.r])
    p.copy("pool", c.ident_bf, identf, [c.r], [c.r])
    c.small = {}
    for name, ap in small_dram.items():
        n = ap.shape[1]
        t = p.alloc(n, F32)
        p.dma("sp", t, ap, [], [c.r], c.r)
        c.small[name] = t
    p.floor = p.top
    return c


def norm_bufs(p):
    xt = [p.alloc(KC * 256, F32).rearrange("p (k t) -> p k t", k=KC) for _ in range(2)]
    sq = [p.alloc(KC * 256, BF16).rearrange("p (k t) -> p k t", k=KC) for _ in range(2)]
    rs = [p.alloc(256, F32) for _ in range(2)]
    r_xt = [Res("n_xt%d" % i) for i in range(2)]
    r_sq = [Res("n_sq%d" % i) for i in range(2)]
    r_rs = [Res("n_rs%d" % i) for i in range(2)]
    return xt, sq, rs, r_xt, r_sq, r_rs


def norm_phase(p, c, x_dram, col0, ntok, g, hT, rh, hcol0, bufs=None):
    xv = x_dram.rearrange("(k p) t -> p k t", p=128)
    xt, sq, rs, r_xt, r_sq, r_rs = bufs if bufs is not None else norm_bufs(p)
    assert ntok % 128 == 0
    t = 0
    it = 0
    while t < ntok:
        n = min(256, ntok - t)
        s = it % 2
        it += 1
        p.dma("sp", xt[s][:, :, 0:n], xv[:, :, col0 + t:col0 + t + n], [], [r_xt[s]], r_xt[s])
        p.act(sq[s][:, :, 0:n], xt[s][:, :, 0:n], AF.Square, [r_xt[s]], [r_sq[s]])
        ps, rps = p.bank()
        for k in range(KC):
            p.mm(ps[:, 0:n], c.ones_bf, sq[s][:, k, 0:n], k == 0, k == KC - 1, [r_sq[s], c.r], [rps])
        p.act(rs[s][:, 0:n], ps[:, 0:n], AF.Sqrt, [rps, c.r], [r_rs[s]], scale=1.0 / D, bias=c.eps_rms)
        p.add("dve", lambda e, o=rs[s][:, 0:n]: e.reciprocal(out=o, in_=o), [r_rs[s]], [r_rs[s]])
        hc = hcol0 + t
        rr = rh[hc // 256]
        for k in range(KC):
            p.stt(hT[:, k, hc:hc + n], xt[s][:, k, 0:n], g[:, k:k + 1], rs[s][:, 0:n], ALU.mult, ALU.mult,
                  [r_xt[s], r_rs[s], c.r], [rr])
        t += n


def rh_reads(rh, c0, n):
    return [rh[i] for i in range(c0 // 256, (c0 + n - 1) // 256 + 1)]


class WStream:
    def __init__(self, p, nslots, kc, cb, name):
        self.p = p
        self.kc = kc
        self.cb = cb
        self.slots = [p.alloc(kc * cb, BF16) for _ in range(nslots)]
        self.res = [Res("%s_w%d" % (name, i)) for i in range(nslots)]
        self.n = 0

    def load(self, blk_ap):
        s = self.n % len(self.slots)
        self.n += 1
        if blk_ap.dtype == BF16:
            self.p.dma("sp", self.slots[s], blk_ap, [], [self.res[s]], self.res[s])
        else:
            self.p.dma("pool", self.slots[s], blk_ap, [], [self.res[s]], self.res[s], max_dma_last_dim=8192)
        for _ in range(2):
            if getattr(self.p, "bg", None):
                o, i_, nm = self.p.bg.pop(0)
                self.p.dma("pool", o, i_, [], [], Res(nm), max_dma_last_dim=8192)
        return self.slots[s].rearrange("p (k f) -> p k f", k=self.kc), self.res[s]


def linear_fm(p, ws, blocks, hT, rh, tok_tiles, consume, tok_outer=False):
    nfo = ws.cb // 128
    pending = None
    loaded = []
    for bi, (bap, tag) in enumerate(blocks):
        if bi == 0:
            loaded.append(ws.load(bap))
        if bi + 1 < len(blocks):
            loaded.append(ws.load(blocks[bi + 1][0]))
        wv, rw = loaded[bi]
        order = ([(ti, fo) for ti in range(len(tok_tiles)) for fo in range(nfo)] if tok_outer
                 else [(ti, fo) for fo in range(nfo) for ti in range(len(tok_tiles))])
        for ti, fo in order:
            c0, n = tok_tiles[ti]
            ps, rps = p.bank()
            for k in range(ws.kc):
                p.mm(ps[:, 0:n], wv[:, k, fo * 128:(fo + 1) * 128], hT[:, k, c0:c0 + n], k == 0, k == ws.kc - 1,
                     [rw] + rh_reads(rh, c0, n), [rps])
            consume(tag, fo, ti, c0, n, ps, rps)


def linear_tm(p, ws, blocks, hT, rh, tok0, ntok, consume):
    loaded = []
    for bi, (bap, tag) in enumerate(blocks):
        if bi == 0:
            loaded.append(ws.load(bap))
        if bi + 1 < len(blocks):
            loaded.append(ws.load(blocks[bi + 1][0]))
        wv, rw = loaded[bi]
        for c in range(tok0, tok0 + ntok, 128):
            ps, rps = p.bank()
            for k in range(ws.kc):
                p.mm(ps[:, 0:ws.cb], hT[:, k, c:c + 128], wv[:, k, :], k == 0, k == ws.kc - 1,
                     [rw] + rh_reads(rh, c, 128), [rps])
            consume(tag, c, ps, rps)


class Stage:
    def __init__(self, p, nslots, n, dtype, name):
        self.t = [p.alloc(n, dtype) for _ in range(nslots)]
        self.r = [Res("%s%d" % (name, i)) for i in range(nslots)]
        self.i = 0

    def next(self):
        s = self.i % len(self.t)
        self.i += 1
        return self.t[s], self.r[s]


def inproj_phase(p, c, d):
    p.reset()
    NCH = 2048
    hT = p.alloc(KC * NCH, BF16).rearrange("p (k t) -> p k t", k=KC)
    rh = {i: Res("hT%d" % i) for i in range(NCH // 256)}
    ws = WStream(p, 2, KC, 512, "inp")
    st_bf = Stage(p, 4, 512, BF16, "st_bf")
    st_sg = Stage(p, 2, 512, F32, "st_sg")
    st_s = Stage(p, 2, 512, F32, "st_s")
    g = c.small["mixn0"]
    w = d["w_in"]
    tiles = [(0, 512), (512, 512), (1024, 512), (1536, 512)]
    nb_ = norm_bufs(p)
    rz = Res("zpad")
    for jj in range(8):
        p.dma("sp", d["sT"][jj * 128:(jj + 1) * 128, NW:NW + 32], c.zeros[:, 0:32], [c.r], [], rz)
    for base in (0, NCH):
        def cons_q(tag, fo, ti, c0, n, ps, rps, base=base):
            t, r = st_bf.next()
            p.copy("act", t[:, 0:n], ps[:, 0:n], [rps], [r])
            f0 = (tag[1] * 4 + fo) * 128
            dst = d["qT"] if tag[0] == "q" else d["kT"]
            p.dma("sp", dst[f0:f0 + 128, base + c0:base + c0 + n], t[:, 0:n], [r], [], r)

        def cons_v(tag, cc, ps, rps, base=base):
            t, r = st_bf.next()
            p.copy("dve", t[:, 0:512], ps[:, 0:512], [rps], [r])
            f0 = tag[1] * 512
            p.dma("sp", d["V"][base + cc:base + cc + 128, f0:f0 + 512], t[:, 0:512], [r], [], r)

        glu = {}

        def cons_u(tag, fo, ti, c0, n, ps, rps, base=base, glu=glu):
            glu[fo] = (ps, rps)
            if fo == 3:
                for j in range(2):
                    pa, ra = glu[j]
                    pg, rg = glu[2 + j]
                    sg, rsg = st_sg.next()
                    p.act(sg[:, 0:n], pg[:, 0:n], AF.Sigmoid, [rg], [rsg])
                    s, rs_ = st_s.next()
                    p.tt("dve", s[:, 0:n], pa[:, 0:n], sg[:, 0:n], ALU.mult, [ra, rsg], [rs_])
                    f0 = (tag[1] * 2 + j) * 128
                    p.dma("sp", d["sT"][f0:f0 + 128, base + c0:base + c0 + n], s[:, 0:n], [rs_], [], rs_)

        norm_phase(p, c, d["xT"], base, NCH, g, hT, rh, 0, nb_)
        linear_fm(p, ws, [(w[i], ("u", i)) for i in range(4)], hT, rh, tiles, cons_u, tok_outer=True)
        linear_fm(p, ws, [(w[4 + i], ("q", i)) for i in range(2)], hT, rh, tiles, cons_q)
        linear_fm(p, ws, [(w[6 + i], ("k", i)) for i in range(2)], hT, rh, tiles, cons_q)
        linear_tm(p, ws, [(w[8 + i], ("v", i)) for i in range(2)], hT, rh, 0, NCH, cons_v)
    p.barrier()


def conformer_gen(p, c, d):
    W = 31
    sb = [p.alloc(512 + 32, F32) for _ in range(3)]
    r_sb = [Res("cf_s%d" % i) for i in range(3)]
    h = p.alloc(8 * 512, F32).rearrange("p (j t) -> p j t", j=8)
    sq = p.alloc(8 * 512, F32).rearrange("p (j t) -> p j t", j=8)
    r_h = [Res("cf_h%d" % j) for j in range(8)]
    r_sq = [Res("cf_sq%d" % j) for j in range(8)]
    mean = p.alloc(512, F32)
    msq = p.alloc(512, F32)
    rstd = p.alloc(512, F32)
    r_mean, r_msq, r_rstd = Res("cf_mean"), Res("cf_msq"), Res("cf_rstd")
    tmp = [p.alloc(512, F32) for _ in range(2)]
    r_tmp = [Res("cf_tmp%d" % i) for i in range(2)]
    yst = Stage(p, 2, 512, BF16, "cf_y")
    cw, cb, lg, lb = c.small["conv_w"], c.small["conv_b"], c.small["ln_g"], c.small["ln_b"]
    cwv = cw.rearrange("p (j k) -> p j k", j=8)
    ld = 0
    for (c0, n) in W_TILES:
        for j in range(8):
            s = ld % 3
            ld += 1
            p.dma("sp", sb[s][:, 0:n + 30], d["sT"][j * 128:(j + 1) * 128, c0:c0 + n + 30], [], [r_sb[s]], r_sb[s])
            hj = h[:, j, 0:n]
            p.ts("dve", hj, sb[s][:, 30:30 + n], cwv[:, j, 0:1], cb[:, j:j + 1], ALU.mult, ALU.add,
                 [r_sb[s], c.r], [r_h[j]])
            for k in range(1, W):
                p.stt(hj, sb[s][:, 30 - k:30 - k + n], cwv[:, j, k:k + 1], hj, ALU.mult, ALU.add,
                      [r_sb[s], r_h[j], c.r], [r_h[j]])
                if k % 2 == 0:
                    yield
            p.act(sq[:, j, 0:n], hj, AF.Square, [r_h[j]], [r_sq[j]])
        ps1, rp1 = p.bank()
        for j in range(8):
            p.mm(ps1[:, 0:n], c.ones_f, h[:, j, 0:n], j == 0, j == 7, [r_h[j], c.r], [rp1])
        ps2, rp2 = p.bank()
        for j in range(8):
            p.mm(ps2[:, 0:n], c.ones_f, sq[:, j, 0:n], j == 0, j == 7, [r_sq[j], c.r], [rp2])
        p.add("act", lambda e, o=mean[:, 0:n], i=ps1[:, 0:n]: e.activation(out=o, in_=i, func=AF.Copy, scale=1.0 / 1024),
              [rp1], [r_mean])
        p.tt("dve", msq[:, 0:n], mean[:, 0:n], mean[:, 0:n], ALU.mult, [r_mean], [r_msq])
        p.stt(rstd[:, 0:n], ps2[:, 0:n], 1.0 / 1024, msq[:, 0:n], ALU.mult, ALU.subtract, [rp2, r_msq], [r_rstd])
        p.act(rstd[:, 0:n], rstd[:, 0:n], AF.Sqrt, [r_rstd, c.r], [r_rstd], bias=c.eps_ln)
        p.add("dve", lambda e, o=rstd[:, 0:n]: e.reciprocal(out=o, in_=o), [r_rstd], [r_rstd])
        yield
        for j in range(8):
            if j % 2 == 0:
                yield
            t = tmp[j % 2][:, 0:n]
            rt = r_tmp[j % 2]
            p.tt("dve", t, h[:, j, 0:n], mean[:, 0:n], ALU.subtract, [r_h[j], r_mean], [rt])
            p.tt("dve", t, t, rstd[:, 0:n], ALU.mult, [rt, r_rstd], [rt])
            y, ry = yst.next()
            p.act(y[:, 0:n], t, AF.Silu, [rt, c.r], [ry], scale=lg[:, j:j + 1], bias=lb[:, j:j + 1])
            p.dma("sp", d["yT"][j * 128:(j + 1) * 128, c0:c0 + n], y[:, 0:n], [ry], [], ry)
    yield


def run_interleaved(items, S, mk, stagger=True, side=None):
    free = list(range(S))
    active = []
    it = iter(items)
    done = False
    sweeps = 0
    while True:
        while free and not done and (not stagger or not active or sweeps >= 2):
            sweeps = 0
            x = next(it, None)
            if x is None:
                done = True
                break
            sl = free.pop(0)
            active.append((mk(x, sl), sl))
            if stagger and free and not done:
                break
        if not active:
            break
        sweeps += 1
        if side is not None and side[0] is not None:
            try:
                next(side[0])
            except StopIteration:
                side[0] = None
        for g, sl in list(active):
            try:
                next(g)
            except StopIteration:
                active.remove((g, sl))
                free.append(sl)


NSTREAM = 3


def moba_phase(p, c, d, with_conformer=True):
    p.reset()
    S = 3
    side = [conformer_gen(p, c, d)] if with_conformer else [None]
    if with_conformer:
        next(side[0])
    p.rot = list(range(8 - S))
    BIG = 1.0e30
    NEGB = 3.0e4
    scale = HD ** -0.5
    NQT = NW // 128
    qT = [p.alloc(NW, BF16) for _ in range(2)]
    kT = [p.alloc(NW, BF16) for _ in range(2)]
    Vh = [p.alloc(32 * 128, BF16).rearrange("p (s d) -> p s d", s=32) for _ in range(2)]
    r_q = [Res("mb_q%d" % i) for i in range(2)]
    r_k = [Res("mb_k%d" % i) for i in range(2)]
    r_v = [Res("mb_v%d" % i) for i in range(2)]
    kmf = [p.alloc(16, F32) for _ in range(2)]
    kmb = [p.alloc(16, BF16) for _ in range(2)]
    r_km = [Res("mb_km%d" % i) for i in range(2)]
    vbias = p.alloc(16, F32)
    r_vb = Res("mb_vb")
    bv = c.small["blkvalid"]
    p.ts("dve", vbias, bv, -1.0, BIG, ALU.add, ALU.mult, [c.r], [r_vb])
    yst = [p.alloc(NW, BF16) for _ in range(2)]
    r_y = [Res("mb_y%d" % i) for i in range(2)]
    streams = []
    for si in range(S):
        st = dict(
            gm=p.alloc(16, F32), top8=p.alloc(8, F32), sel=p.alloc(16, F32), r_g=Res("mb_g%d" % si),
            pexp=[p.alloc(512, BF16) for _ in range(2)], r_pe=[Res("mb_p%d_%d" % (si, i)) for i in range(2)],
            dtmp=p.alloc(128, F32), r_dt=Res("mb_dt%d" % si),
            pT=[p.alloc(512, BF16) for _ in range(2)], r_pT=[Res("mb_pT%d_%d" % (si, i)) for i in range(2)],
            rsum=p.alloc(24, F32), r_rs=Res("mb_rs%d" % si), rinv=p.alloc(1, F32),
            ob=p.alloc(128, BF16), r_ob=Res("mb_ob%d" % si), acc=8 - S + si, n=0)
        streams.append(st)

    def qgen(h, hs, qt, st):
        i0 = qt * 128
        nb = i0 // 256
        qtile = qT[hs][:, i0:i0 + 128]
        gm, top8, sel, r_g = st["gm"], st["top8"], st["sel"], st["r_g"]
        npast = 15 - nb
        p.memset("pool", gm, -BIG, [r_g])
        if npast > 0:
            psg, rpg = p.bank()
            p.mm(psg[:, 0:16], qtile, kmb[hs], True, True, [r_q[hs], r_km[hs]], [rpg])
            p.tt("dve", gm[:, nb + 1:16], psg[:, nb + 1:16], vbias[:, nb + 1:16], ALU.add, [rpg, r_vb], [r_g])
        p.add("dve", lambda e, o=top8, i=gm: e.max(out=o, in_=i), [r_g], [r_g])
        p.ts("dve", sel, gm, top8[:, 2:3], None, ALU.is_ge, None, [r_g], [r_g])
        p.tt("dve", sel, sel, bv, ALU.mult, [r_g, c.r], [r_g])
        p.ts("dve", sel, sel, -1.0, NEGB, ALU.add, ALU.mult, [r_g], [r_g])
        yield
        segs = [(i0, 128, "diag")]
        k = i0 + 128
        if qt % 2 == 0:
            segs.append((k, 128, "own"))
            k += 128
        while k < NW:
            segs.append((k, 256, k // 256))
            k += 256
        tiles = []
        cur = []
        curn = 0
        for sg in segs:
            if curn + sg[1] > 512:
                tiles.append(cur)
                cur = []
                curn = 0
            cur.append(sg)
            curn += sg[1]
        tiles.append(cur)
        po, rpo = p.fixed(st["acc"])
        nsub_total = sum(s_[1] for s_ in segs) // 128
        sub_done = 0
        rcol = 0
        rsm, rrs = st["rsum"], st["r_rs"]
        for tl in tiles:
            k0 = tl[0][0]
            nk = sum(s_[1] for s_ in tl)
            pz, rpz = p.bank()
            p.mm(pz[:, 0:nk], qtile, kT[hs][:, k0:k0 + nk], True, True, [r_q[hs], r_k[hs]], [rpz])
            px = st["n"] % 2
            st["n"] += 1
            pexp, r_pe = st["pexp"][px], st["r_pe"][px]
            pT, r_pT = st["pT"][px], st["r_pT"][px]
            yield
            col = 0
            for (ks, kn, kind) in tl:
                if kind == "diag":
                    dt_ = st["dtmp"]
                    p.tt("dve", dt_, pz[:, col:col + 128], c.causal_ge, ALU.add, [rpz, c.r], [st["r_dt"]])
                    p.act(pexp[:, col:col + 128], dt_, AF.Exp, [st["r_dt"]], [r_pe, rrs], scale=scale,
                          accum_out=rsm[:, rcol:rcol + 1])
                elif kind == "own":
                    p.act(pexp[:, col:col + kn], pz[:, col:col + kn], AF.Exp, [rpz], [r_pe, rrs], scale=scale,
                          accum_out=rsm[:, rcol:rcol + 1])
                else:
                    p.act(pexp[:, col:col + kn], pz[:, col:col + kn], AF.Exp, [rpz, r_g], [r_pe, rrs],
                          scale=scale, bias=sel[:, kind:kind + 1], accum_out=rsm[:, rcol:rcol + 1])
                rcol += 1
                col += kn
            yield
            pt_ps, rpt = p.bank()
            ptv = pt_ps[:, 0:256].bitcast(BF16)
            nsub = nk // 128
            for j in range(nsub):
                p.tr(ptv[:, j * 128:(j + 1) * 128], pexp[:, j * 128:(j + 1) * 128], c.ident_bf, [r_pe, c.r], [rpt])
            yield
            p.copy("dve" if (st["n"] % 2) else "act", pT[:, 0:nk], ptv[:, 0:nk], [rpt], [r_pT])
            yield
            for j in range(nsub):
                sbi = (k0 // 128) + j
                p.mm(po[:, 0:128], pT[:, j * 128:(j + 1) * 128], Vh[hs][:, sbi, :], sub_done == 0,
                     sub_done == nsub_total - 1, [r_pT, r_v[hs]], [rpo])
                sub_done += 1
            yield
        rinv, ob, r_ob = st["rinv"], st["ob"], st["r_ob"]
        p.add("dve", lambda e, o=rinv, i=rsm[:, 0:rcol]: e.tensor_reduce(out=o, in_=i, axis=AX.X, op=ALU.add),
              [rrs], [rrs])
        p.add("dve", lambda e, o=rinv: e.reciprocal(out=o, in_=o), [rrs], [rrs])
        p.ts("dve", ob, po[:, 0:128], rinv[:, 0:1], None, ALU.mult, None, [rpo, rrs], [r_ob])
        pt2, rpt2 = p.bank()
        pt2v = pt2[:, 0:64].bitcast(BF16)
        p.tr(pt2v, ob, c.ident_bf, [r_ob, c.r], [rpt2])
        p.copy("act", yst[hs][:, i0:i0 + 128], pt2v, [rpt2], [r_y[hs]])
        yield

    for h in range(8):
        hs = h % 2
        p.dma("sp", qT[hs], d["qT"][h * 128:(h + 1) * 128, 0:NW], [], [r_q[hs]], r_q[hs])
        p.dma("sp", kT[hs], d["kT"][h * 128:(h + 1) * 128, :], [], [r_k[hs]], r_k[hs])
        p.dma("sp", Vh[hs], d["V"][:, h * 128:(h + 1) * 128].rearrange("(s q) e -> q s e", q=128), [], [r_v[hs]], r_v[hs])
        p.add("dve", lambda e, o=kmf[hs], i=kT[hs].rearrange("p (n s) -> p n s", n=16): e.tensor_reduce(out=o, in_=i, axis=AX.X, op=ALU.add),
              [r_k[hs]], [r_km[hs]])
        p.ts("dve", kmb[hs], kmf[hs], 1.0 / 256, None, ALU.mult, None, [r_km[hs]], [r_km[hs]])
        run_interleaved(range(NQT), S, lambda qt, sl, h=h, hs=hs: qgen(h, hs, qt, streams[sl]), side=side)
        p.dma("sp", d["yT"][1024 + h * 128:1024 + (h + 1) * 128, 0:NW], yst[hs], [r_y[hs]], [], r_y[hs])
    while side[0] is not None:
        try:
            next(side[0])
        except StopIteration:
            side[0] = None
    p.barrier()


def load_hT_from_dram(p, src, nrow_chunks, t0, ntok, name):
    hT = p.alloc(nrow_chunks * ntok, BF16).rearrange("p (k t) -> p k t", k=nrow_chunks)
    rh = {i: Res("%s%d" % (name, i)) for i in range((ntok + 255) // 256)}
    one = Res(name + "_ld")
    for k in range(nrow_chunks):
        p.dma("sp", hT[:, k, :], src[k * 128:(k + 1) * 128, t0:t0 + ntok], [], list(rh.values()), one)
    return hT, rh


def outproj_phase(p, c, d, yname, wname, xin, xout, chunks):
    for (t0, tiles) in chunks:
        p.reset()
        ntok = sum(n for _, n in tiles)
        hT, rh = load_hT_from_dram(p, d[yname], KC, t0, ntok, "op_y")
        ws = WStream(p, 2, KC, 512, "op")
        xst = Stage(p, 3, 512, F32, "op_x")
        w = d[wname]

        def cons(tag, fo, ti, c0, n, ps, rps, t0=t0, xst=xst):
            f0 = (tag * 4 + fo) * 128
            t, r = xst.next()
            p.dma("sp", t[:, 0:n], d[xin][f0:f0 + 128, t0 + c0:t0 + c0 + n], [], [r], r)
            p.tt("dve", t[:, 0:n], ps[:, 0:n], t[:, 0:n], ALU.add, [rps, r], [r])
            p.dma("sp", d[xout][f0:f0 + 128, t0 + c0:t0 + c0 + n], t[:, 0:n], [r], [], r)

        linear_fm(p, ws, [(w[i], i) for i in range(4)], hT, rh, tiles, cons)
        p.barrier()


def ffn_phase(p, c, d, L, xin, xout, chunks):
    g = c.small["ffnn%d" % L]
    fw, fb = c.small["fconv_w%d" % L], c.small["fconv_b%d" % L]
    fwv = fw.rearrange("p (j k) -> p j k", j=FC)
    valid = c.small["valid"]
    wu, wg = d["w_up%d" % L], d["w_gate%d" % L]
    nblk = DFF // 256
    for (t0, tiles, halo, halves) in chunks:
        p.reset()
        ntok = sum(n for _, n in tiles)
        hT = p.alloc(KC * ntok, BF16).rearrange("p (k t) -> p k t", k=KC)
        rh = {i: Res("ff_h%d" % i) for i in range((ntok + 255) // 256)}
        mark = p.top
        norm_phase(p, c, d[xin], t0, ntok, g, hT, rh, 0)
        p.barrier()
        p.top = mark
        wsu = WStream(p, 2, KC, 256, "ffu")
        wsg = WStream(p, 2, KC, 256, "ffg")
        NB = ntok + 2
        upb = [p.alloc(NB, F32) for _ in range(2)]
        gb = [p.alloc(ntok, F32) for _ in range(2)]
        u = [p.alloc(ntok, F32) for _ in range(2)]
        ab = [p.alloc(ntok, BF16) for _ in range(2)]
        r_up = [Res("ff_up%d" % i) for i in range(2)]
        r_gb = [Res("ff_gb%d" % i) for i in range(2)]
        r_u = [Res("ff_u%d" % i) for i in range(2)]
        r_ab = [Res("ff_ab%d" % i) for i in range(2)]
        lu = [wsu.load(wu[0])]
        lg_ = [wsg.load(wg[0])]
        for b in range(nblk):
            if b + 1 < nblk:
                lu.append(wsu.load(wu[b + 1]))
                lg_.append(wsg.load(wg[b + 1]))
            for fo in range(2):
                j = b * 2 + fo
                s = j % 2
                wv, rw = lu[b]
                for (c0, n) in tiles:
                    ps, rps = p.bank()
                    for k in range(KC):
                        p.mm(ps[:, 0:n], wv[:, k, fo * 128:(fo + 1) * 128], hT[:, k, c0:c0 + n], k == 0, k == KC - 1,
                             [rw] + rh_reads(rh, c0, n), [rps])
                    p.copy("act", upb[s][:, c0:c0 + n], ps[:, 0:n], [rps], [r_up[s]])
                wv, rw = lg_[b]
                for (c0, n) in tiles:
                    ps, rps = p.bank()
                    for k in range(KC):
                        p.mm(ps[:, 0:n], wv[:, k, fo * 128:(fo + 1) * 128], hT[:, k, c0:c0 + n], k == 0, k == KC - 1,
                             [rw] + rh_reads(rh, c0, n), [rps])
                    p.copy("act", gb[s][:, c0:c0 + n], ps[:, 0:n], [rps], [r_gb[s]])
                if halo == "cr":
                    p.ts("dve", upb[s][:, NOWN:NOWN + 2], upb[s][:, NOWN:NOWN + 2], valid[:, 0:1], None, ALU.mult, None,
                         [r_up[s], c.r], [r_up[s]])
                    p.memset("pool", upb[s][:, ntok:NB], 0.0, [r_up[s]])
                elif halo == "save":
                    p.memset("pool", upb[s][:, ntok:NB], 0.0, [r_up[s]])
                    p.copy("pool", c.ffn_halo[:, j, :], upb[s][:, 0:2], [r_up[s]], [c.r_halo])
                else:
                    p.ts("dve", upb[s][:, ntok:NB], c.ffn_halo[:, j, :], valid[:, 0:1], None, ALU.mult, None,
                         [c.r_halo, c.r], [r_up[s]])
                us = u[s]
                p.ts("dve", us, upb[s][:, 0:ntok], fwv[:, j, 2:3], fb[:, j:j + 1], ALU.mult, ALU.add, [r_up[s], c.r], [r_u[s]])
                p.stt(us, upb[s][:, 1:ntok + 1], fwv[:, j, 1:2], us, ALU.mult, ALU.add, [r_up[s], r_u[s], c.r], [r_u[s]])
                p.stt(us, upb[s][:, 2:ntok + 2], fwv[:, j, 0:1], us, ALU.mult, ALU.add, [r_up[s], r_u[s], c.r], [r_u[s]])
                p.act(us, us, AF.Silu, [r_u[s]], [r_u[s]])
                p.tt("pool", ab[s], us, gb[s], ALU.mult, [r_u[s], r_gb[s]], [r_ab[s]])
                p.dma("sp", d["actT"][j * 128:(j + 1) * 128, t0:t0 + ntok], ab[s], [r_ab[s]], [], r_ab[s])
        p.barrier()
        for hv in halves:
            p.reset()
            h0 = hv[0][0]
            nt = sum(n for _, n in hv)
            aT = p.alloc(FC * nt, BF16).rearrange("p (k t) -> p k t", k=FC)
            ra = {i: Res("fd_a%d" % i) for i in range((nt + 255) // 256)}
            one = Res("fd_ld")
            for k in range(FC):
                p.dma("sp", aT[:, k, :], d["actT"][k * 128:(k + 1) * 128, t0 + h0:t0 + h0 + nt], [], list(ra.values()), one)
            ws = WStream(p, 3, FC, 128, "ffd")
            xst = Stage(p, 3, 512, F32, "fd_x")
            wd = d["wdb%d" % L]
            rel_tiles = [(c0 - h0, n) for (c0, n) in hv]

            def cons(tag, fo, ti, c0, n, ps, rps, tb=t0 + h0, xst=xst):
                f0 = tag * 128
                t, r = xst.next()
                p.dma("sp", t[:, 0:n], d[xin][f0:f0 + 128, tb + c0:tb + c0 + n], [], [r], r)
                p.tt("dve", t[:, 0:n], ps[:, 0:n], t[:, 0:n], ALU.add, [rps, r], [r])
                p.dma("sp", d[xout][f0:f0 + 128, tb + c0:tb + c0 + n], t[:, 0:n], [r], [], r)

            linear_fm(p, ws, [(wd[i], i) for i in range(16)], aT, ra, rel_tiles, cons)
            p.barrier()


def qkv_phase(p, c, d):
    p.reset()
    hT = p.alloc(KC * NCR, BF16).rearrange("p (k t) -> p k t", k=KC)
    rh = {i: Res("qk_h%d" % i) for i in range((NCR + 255) // 256)}
    nb_ = norm_bufs(p)
    ws = WStream(p, 2, KC, 512, "qkv")
    st_bf = Stage(p, 4, 512, BF16, "qk_st")
    w = d["w_qkv"]
    valid = c.small["valid"]

    def mk(dst, base):
        def cons(tag, fo, ti, c0, n, ps, rps):
            t, r = st_bf.next()
            p.copy("act", t[:, 0:n], ps[:, 0:n], [rps], [r])
            f0 = (tag * 4 + fo) * 128
            p.dma("sp", d[dst][f0:f0 + 128, base + c0:base + c0 + n], t[:, 0:n], [r], [], r)
        return cons

    def mk_v(base):
        def cons_v(tag, cc, ps, rps):
            t, r = st_bf.next()
            if base + cc >= NOWN:
                p.ts("dve", t[:, 0:512], ps[:, 0:512], valid[:, 0:1], None, ALU.mult, None, [rps, c.r], [r])
            else:
                p.copy("dve", t[:, 0:512], ps[:, 0:512], [rps], [r])
            p.dma("sp", d["V1"][base + cc:base + cc + 128, tag * 512:(tag + 1) * 512], t[:, 0:512], [r], [], r)
        return cons_v

    norm_phase(p, c, d["x2T"], 0, NCR, c.small["mixn1"], hT, rh, 0, nb_)
    linear_fm(p, ws, [(w[i], i) for i in range(4)], hT, rh, CR_TILES, mk("q1T", 0))
    linear_fm(p, ws, [(w[4 + i], i) for i in range(4)], hT, rh, CR_TILES, mk("k1T", 0))
    linear_tm(p, ws, [(w[8 + i], i) for i in range(4)], hT, rh, 0, NCR, mk_v(0))
    nrest = NW - NCR
    norm_phase(p, c, d["x2T"], NCR, nrest, c.small["mixn1"], hT, rh, 0, nb_)
    rest_tiles = [(0, 512), (512, 512), (1024, 512), (1536, 384)]
    linear_fm(p, ws, [(w[4 + i], i) for i in range(4)], hT, rh, rest_tiles, mk("k1T", NCR))
    linear_tm(p, ws, [(w[8 + i], i) for i in range(4)], hT, rh, 0, nrest, mk_v(NCR))
    p.barrier()


def sb_phase(p, c, d):
    p.reset()
    S = 4
    p.rot = list(range(8 - S))
    scale = HD ** -0.5
    NQT = NCR // 128
    qT = [p.alloc(NCR, BF16) for _ in range(2)]
    kT = [p.alloc(NW, BF16) for _ in range(2)]
    Vh = [p.alloc(32 * 128, BF16).rearrange("p (s d) -> p s d", s=32) for _ in range(2)]
    r_q = [Res("sb_q%d" % i) for i in range(2)]
    r_k = [Res("sb_k%d" % i) for i in range(2)]
    r_v = [Res("sb_v%d" % i) for i in range(2)]
    yst = [p.alloc(NCR, BF16) for _ in range(2)]
    r_y = [Res("sb_y%d" % i) for i in range(2)]
    streams = []
    for si in range(S):
        streams.append(dict(
            om=[p.alloc(512, F32) for _ in range(2)], r_om=[Res("sb_om%d_%d" % (si, i)) for i in range(2)],
            Cx=[p.alloc(513, F32) for _ in range(2)], r_cx=[Res("sb_cx%d_%d" % (si, i)) for i in range(2)],
            ab=[p.alloc(512, BF16) for _ in range(2)], r_ab=[Res("sb_ab%d_%d" % (si, i)) for i in range(2)],
            aT=[p.alloc(512, BF16) for _ in range(2)], r_aT=[Res("sb_aT%d_%d" % (si, i)) for i in range(2)],
            acc=8 - S + si, n=0))
    regcache = {}

    def qgen(h, hs, qt, st):
        i0 = qt * 128
        qtile = qT[hs][:, i0:i0 + 128]
        nkeys = NW - i0
        po, rpo = p.fixed(st["acc"])
        nsub_total = nkeys // 128
        sub_done = 0
        k0 = i0
        prev = None
        while k0 < NW:
            nk = min(512, NW - k0)
            s = st["n"] % 2
            st["n"] += 1
            om, r_om = st["om"][s], st["r_om"][s]
            Cx, r_cx = st["Cx"][s], st["r_cx"][s]
            ab, r_ab = st["ab"][s], st["r_ab"][s]
            aT, r_aT = st["aT"][s], st["r_aT"][s]
            pz, rpz = p.bank()
            p.mm(pz[:, 0:nk], qtile, kT[hs][:, k0:k0 + nk], True, True, [r_q[hs], r_k[hs]], [rpz])
            yield
            p.act(om[:, 0:nk], pz[:, 0:nk], AF.Sigmoid, [rpz], [r_om], scale=-scale)
            if k0 == i0:
                def sel_f(e, o=om[:, 0:128]):
                    if "one" not in regcache:
                        regcache["one"] = e.to_reg(1.0)
                    return e.affine_select(out=o, in_=o, pattern=[[1, 128]], base=0, channel_multiplier=-1,
                                           compare_op=ALU.is_gt, fill=regcache["one"])
                p.add("pool", sel_f, [r_om], [r_om])
                p.memset("dve", Cx[:, 0:1], 1.0, [r_cx])
            else:
                pcx, prcx, pnk = prev
                p.copy("dve", Cx[:, 0:1], pcx[:, pnk:pnk + 1], [prcx], [r_cx])
            yield
            p.add("dve", lambda e, o=Cx[:, 1:nk + 1], a=om[:, 0:nk], z=c.zeros[:, 0:nk], ini=Cx[:, 0:1]:
                  e.tensor_tensor_scan(out=o, data0=a, data1=z, initial=ini, op0=ALU.mult, op1=ALU.add),
                  [r_om, r_cx, c.r], [r_cx])
            prev = (Cx, r_cx, nk)
            yield
            p.tt("pool", ab[:, 0:nk], Cx[:, 0:nk], Cx[:, 1:nk + 1], ALU.subtract, [r_cx], [r_ab])
            yield
            pt_ps, rpt = p.bank()
            ptv = pt_ps[:, 0:256].bitcast(BF16)
            nsub = nk // 128
            for j in range(nsub):
                p.tr(ptv[:, j * 128:(j + 1) * 128], ab[:, j * 128:(j + 1) * 128], c.ident_bf, [r_ab, c.r], [rpt])
            yield
            p.copy("act", aT[:, 0:nk], ptv[:, 0:nk], [rpt], [r_aT])
            yield
            for j in range(nsub):
                sbi = (k0 // 128) + j
                p.mm(po[:, 0:128], Vh[hs][:, sbi, :], aT[:, j * 128:(j + 1) * 128], sub_done == 0,
                     sub_done == nsub_total - 1, [r_aT, r_v[hs]], [rpo])
                sub_done += 1
            k0 += nk
            yield
        p.copy("dve", yst[hs][:, i0:i0 + 128], po[:, 0:128], [rpo], [r_y[hs]])
        yield

    for h in range(16):
        hs = h % 2
        p.dma("sp", qT[hs], d["q1T"][h * 128:(h + 1) * 128, 0:NCR], [], [r_q[hs]], r_q[hs])
        p.dma("sp", kT[hs], d["k1T"][h * 128:(h + 1) * 128, :], [], [r_k[hs]], r_k[hs])
        p.dma("sp", Vh[hs], d["V1"][:, h * 128:(h + 1) * 128].rearrange("(s q) e -> q s e", q=128), [], [r_v[hs]], r_v[hs])
        run_interleaved(range(NQT), S, lambda qt, sl, h=h, hs=hs: qgen(h, hs, qt, streams[sl]))
        p.dma("sp", d["y1T"][h * 128:(h + 1) * 128, 0:NCR], yst[hs], [r_y[hs]], [], r_y[hs])
    p.barrier()


def final_norm_phase(p, c, d, xin):
    p.reset()
    g = c.small["finaln"]
    xv = d[xin].rearrange("(k p) t -> p k t", p=128)
    ov = d["outT"].rearrange("(k p) t -> p k t", p=128)
    xt = [p.alloc(KC * 256, F32).rearrange("p (k t) -> p k t", k=KC) for _ in range(2)]
    sqf = [p.alloc(KC * 256, F32).rearrange("p (k t) -> p k t", k=KC) for _ in range(2)]
    rs = [p.alloc(256, F32) for _ in range(2)]
    r_xt = [Res("fn_xt%d" % i) for i in range(2)]
    r_sq = [Res("fn_sq%d" % i) for i in range(2)]
    r_rs = [Res("fn_rs%d" % i) for i in range(2)]
    for it in range(NOWN // 256):
        s = it % 2
        t = it * 256
        p.dma("sp", xt[s], xv[:, :, t:t + 256], [], [r_xt[s]], r_xt[s])
        p.act(sqf[s], xt[s], AF.Square, [r_xt[s]], [r_sq[s]])
        ps, rps = p.bank()
        for k in range(KC):
            p.mm(ps[:, 0:256], c.ones_f, sqf[s][:, k, :], k == 0, k == KC - 1, [r_sq[s], c.r], [rps])
        p.act(rs[s], ps[:, 0:256], AF.Sqrt, [rps, c.r], [r_rs[s]], scale=1.0 / D, bias=c.eps_rms)
        p.add("dve", lambda e, o=rs[s]: e.reciprocal(out=o, in_=o), [r_rs[s]], [r_rs[s]])
        for k in range(KC):
            p.stt(sqf[s][:, k, :], xt[s][:, k, :], g[:, k:k + 1], rs[s], ALU.mult, ALU.mult,
                  [r_xt[s], r_rs[s], c.r], [r_sq[s]])
        p.dma("sp", ov[:, :, t:t + 256], sqf[s], [r_sq[s]], [], r_sq[s])
    p.barrier()


SMALL = {"mixn0": 16, "ffnn0": 16, "mixn1": 16, "ffnn1": 16, "finaln": 16, "conv_w": 8 * 31, "conv_b": 8,
         "ln_g": 8, "ln_b": 8, "fconv_w0": FC * 3, "fconv_b0": FC, "fconv_w1": FC * 3, "fconv_b1": FC,
         "valid": 1, "blkvalid": 16}
W_SHAPES = {"w_in": [10, 128, KC * 512], "w_out": [4, 128, KC * 512], "w_up0": [22, 128, KC * 256],
            "w_gate0": [22, 128, KC * 256], "w_down0": [16, 128, FC * 128], "w_qkv": [12, 128, KC * 512],
            "w_o": [4, 128, KC * 512], "w_up1": [22, 128, KC * 256], "w_gate1": [22, 128, KC * 256],
            "w_down1": [16, 128, FC * 128]}
T4 = [(0, 512), (512, 512), (1024, 512), (1536, 512)]


def build_fused(debug=False):
    nc = bass.Bass("TRN2", target_bir_lowering=False)
    d = {}
    d["xT"] = nc.dram_tensor("xT", [D, NW], F32, kind="ExternalInput").ap()
    for k, shp in W_SHAPES.items():
        d[k] = nc.dram_tensor(k, shp, F32, kind="ExternalInput").ap()
    small = {}
    for k, n in SMALL.items():
        small[k] = nc.dram_tensor("s_" + k, [128, n], F32, kind="ExternalInput").ap()

    def scr(name, shape, dt, out=False):
        d[name] = nc.dram_tensor(name, shape, dt, kind="ExternalOutput" if (out or debug) else "Internal").ap()

    scr("sT", [1024, NW + 32], F32)
    scr("qT", [1024, NW], BF16)
    scr("kT", [1024, NW], BF16)
    scr("V", [NW, 1024], BF16)
    scr("yT", [D, NW], BF16)
    scr("x1T", [D, NW], F32)
    scr("actT", [DFF, NW], BF16)
    scr("x2T", [D, NW], F32)
    scr("q1T", [D, NCR], BF16)
    scr("k1T", [D, NW], BF16)
    scr("V1", [NW, D], BF16)
    scr("y1T", [D, NCR], BF16)
    scr("x3T", [D, NCR], F32)
    scr("x4T", [D, NCR], F32)
    scr("outT", [D, NOWN], F32, out=True)
    scr("wdb0", [16, 128, FC * 128], BF16)
    scr("wdb1", [16, 128, FC * 128], BF16)

    def body(p):
        c = setup_consts(p, small)
        p.bg = []
        for L in range(2):
            for i in range(16):
                p.bg.append((d["wdb%d" % L][i], d["w_down%d" % L][i], "wcv%d_%d" % (L, i)))
        inproj_phase(p, c, d)
        moba_phase(p, c, d)
        outproj_phase(p, c, d, "yT", "w_out", "xT", "x1T", [(0, T4), (2048, T4)])
        ffn_phase(p, c, d, 0, "x1T", "x2T",
                  [(2048, T4, "save", [T4[0:2], T4[2:4]]), (0, T4, "use", [T4[0:2], T4[2:4]])])
        qkv_phase(p, c, d)
        sb_phase(p, c, d)
        outproj_phase(p, c, d, "y1T", "w_o", "x2T", "x3T", [(0, CR_TILES)])
        ffn_phase(p, c, d, 1, "x3T", "x4T", [(0, CR_TILES, "cr", [CR_TILES[0:2], CR_TILES[2:5]])])
        final_norm_phase(p, c, d, "x4T")

    build_program(nc, body)
    return nc


def wblk(W, cb):
    K, Fo = W.shape
    kc = K // 128
    nb = Fo // cb
    return np.ascontiguousarray(W.reshape(kc, 128, nb, cb).transpose(2, 1, 0, 3).reshape(nb, 128, kc * cb))


def pvec(v):
    n = v.shape[0] // 128
    return np.ascontiguousarray(v.reshape(n, 128).T)


def prep(inputs):
    f = lambda a: np.asarray(a, dtype=np.float32)
    x = f(inputs["x"])
    w_in = f(inputs["even_w_in"])[0]
    cols = []
    for i in range(4):
        for j in (2 * i, 2 * i + 1):
            cols.append(np.arange(j * 128, (j + 1) * 128))
        for j in (2 * i, 2 * i + 1):
            cols.append(np.arange(1024 + j * 128, 1024 + (j + 1) * 128))
    cols.append(np.arange(2048, 5120))
    cols = np.concatenate(cols)

    def cw(a, n):
        return np.ascontiguousarray(a.T.reshape(n, 128, a.shape[0]).transpose(1, 0, 2).reshape(128, n * a.shape[0]))

    shared = {
        "w_in": wblk(w_in[:, cols], 512),
        "w_out": wblk(f(inputs["even_w_out"])[0], 512),
        "w_qkv": wblk(f(inputs["odd_w_qkv"])[0], 512),
        "w_o": wblk(f(inputs["odd_w_o"])[0], 512),
        "s_mixn0": pvec(f(inputs["mix_norm"])[0]),
        "s_mixn1": pvec(f(inputs["mix_norm"])[1]),
        "s_finaln": pvec(f(inputs["final_norm"])),
        "s_conv_w": cw(f(inputs["even_conv_w"])[0], 8),
        "s_conv_b": pvec(f(inputs["even_conv_b"])[0]),
        "s_ln_g": pvec(f(inputs["even_ln_g"])[0]),
        "s_ln_b": pvec(f(inputs["even_ln_b"])[0]),
    }
    for L in range(2):
        shared["w_up%d" % L] = wblk(f(inputs["ffn_w_up"])[L], 256)
        shared["w_gate%d" % L] = wblk(f(inputs["ffn_w_gate"])[L], 256)
        shared["w_down%d" % L] = wblk(f(inputs["ffn_w_down"])[L], 128)
        shared["s_ffnn%d" % L] = pvec(f(inputs["ffn_norm"])[L])
        shared["s_fconv_w%d" % L] = cw(f(inputs["ffn_conv_w"])[L], FC)
        shared["s_fconv_b%d" % L] = pvec(f(inputs["ffn_conv_b"])[L])
    maps = []
    for core in range(8):
        b, half = core // 2, core % 2
        win = np.zeros((NW, D), np.float32)
        if half == 1:
            win[:] = x[b]
        else:
            win[NOWN:] = x[b, :NOWN]
        m = dict(shared)
        m["xT"] = np.ascontiguousarray(win[::-1].T)
        m["s_valid"] = np.full((128, 1), float(half), np.float32)
        bv = np.ones((128, 16), np.float32)
        if half == 0:
            bv[:, 8:] = 0.0
        m["s_blkvalid"] = bv
        maps.append(m)
    return maps


def assemble(resB):
    out = np.zeros((4, 4096, D), np.float32)
    for core in range(8):
        b, half = core // 2, core % 2
        oT = resB[core]["outT"]
        o = oT.T[::-1]
        out[b, half * NOWN:(half + 1) * NOWN] = o
    return out


def kernel(**inputs):
    nc = build_fused()
    maps = prep(inputs)
    r = run_bass_kernel_spmd(nc, maps, core_ids=list(range(8)))
    return assemble(r.results)
```

```python
import numpy as np
import ml_dtypes
from contextlib import ExitStack
import concourse.bass as bass
import concourse.mybir as mybir
from concourse.bass_utils import run_bass_kernel_spmd

F32 = mybir.dt.float32
BF16 = mybir.dt.bfloat16
AF = mybir.ActivationFunctionType
ALU = mybir.AluOpType
AX = mybir.AxisListType

D = 2048
KC = 16
NW = 4096
NOWN = 2048
NCR = 2176
NU = 2304
DFF = 5632
FC = 44
HD = 128
RMS_EPS = 1e-6
LN_EPS = 1e-5
CR_TILES = [(0, 512), (512, 512), (1024, 512), (1536, 512), (2048, 128)]
W_TILES = [(i * 512, 512) for i in range(8)]
SBUF_WORDS = 49 * 1024


class Res:
    __slots__ = ("name", "last_w", "readers", "sem", "sem_total", "last_dma")

    def __init__(self, name):
        self.name = name
        self.last_w = None
        self.readers = []
        self.sem = None
        self.sem_total = 0
        self.last_dma = None


class Ins:
    __slots__ = ("eng", "fn", "deps", "is_dma", "res", "sem", "sem_val", "need_sig", "sig_val")

    def __init__(self, eng, fn):
        self.eng = eng
        self.fn = fn
        self.deps = []
        self.is_dma = False
        self.res = None
        self.sem = None
        self.sem_val = 0
        self.need_sig = False
        self.sig_val = 0


class Prog:
    ENG = ["pe", "act", "dve", "pool", "sp"]
    COMPUTE = ["pe", "act", "dve", "pool"]

    def __init__(self, nc, es):
        self.nc = nc
        self.es = es
        self.streams = {e: [] for e in self.ENG}
        self.esem = {e: es.enter_context(nc.semaphore("prog_" + e)) for e in self.COMPUTE}
        self.dma_res = []
        self.sem_pool = []
        self.nsem = 0
        self.big = es.enter_context(nc.sbuf_tensor("bigbuf", [128, SBUF_WORDS], F32))
        self.top = 0
        self.floor = 0
        self.banks = [es.enter_context(nc.psum_tensor("bank%d" % i, [128, 512], F32)) for i in range(8)]
        self.rbanks = [Res("bank%d" % i) for i in range(8)]
        self.nbank = 0
        self.rot = list(range(8))
        self.free_sems = []

    def alloc(self, n, dtype):
        words = (n * (2 if dtype == BF16 else 4) + 3) // 4
        words = (words + 7) // 8 * 8
        a = self.big[:, self.top:self.top + words]
        self.top += words
        assert self.top <= SBUF_WORDS, "SBUF overflow %d" % self.top
        if dtype == BF16:
            return a.bitcast(BF16)[:, 0:n]
        return a[:, 0:n]

    def reset(self):
        self.top = self.floor
        self.rot = list(range(8))

    def bank(self):
        i = self.rot[self.nbank % len(self.rot)]
        self.nbank += 1
        return self.banks[i], self.rbanks[i]

    def fixed(self, i):
        return self.banks[i], self.rbanks[i]

    def res(self, name):
        return Res(name)

    def add(self, eng, fn, reads=(), writes=(), dma=None, ndma=1, extra=()):
        ins = Ins(eng, fn)
        deps = []
        for r in reads:
            if r.last_w is not None:
                deps.append(r.last_w)
        for w in writes:
            lw = w.last_w
            if lw is not None:
                if not (eng == "pe" and lw.eng == "pe" and not lw.is_dma and not w.readers):
                    deps.append(lw)
            for rd in w.readers:
                deps.append(rd)
        deps.extend(extra)
        if dma is not None:
            ins.is_dma = True
            ins.res = dma
            if dma.sem is None:
                if self.sem_pool:
                    dma.sem, dma.sem_total = self.sem_pool.pop(0)
                else:
                    dma.sem = self.es.enter_context(self.nc.semaphore("d%d" % self.nsem))
                    dma.sem_total = 0
                    self.nsem += 1
                self.dma_res.append(dma)
            if dma.last_dma is not None:
                deps.append(dma.last_dma)
            dma.sem_total += 16 * ndma
            ins.sem = dma.sem
            ins.sem_val = dma.sem_total
            dma.last_dma = ins
        for r in reads:
            r.readers.append(ins)
        for w in writes:
            w.last_w = ins
            w.readers = []
        seen = set()
        for d in deps:
            if d is ins or id(d) in seen:
                continue
            seen.add(id(d))
            ins.deps.append(d)
        self.streams[eng].append(ins)
        return ins

    def barrier(self):
        lasts = []
        for e in self.ENG:
            for ins in reversed(self.streams[e]):
                if not ins.is_dma and ins.fn is not None:
                    lasts.append(ins)
                    break
        dmas = [r.last_dma for r in self.dma_res if r.last_dma is not None]
        for e in self.ENG:
            self.add(e, None, extra=lasts + dmas)
        for r in self.dma_res:
            self.sem_pool.append((r.sem, r.sem_total))
            r.sem = None
            r.last_dma = None
        self.dma_res = []

    def mm(self, out, lhsT, rhs, start, stop, reads, writes):
        return self.add("pe", lambda e: e.matmul(out, lhsT=lhsT, rhs=rhs, start=start, stop=stop), reads, writes)

    def tr(self, out, in_, ident, reads, writes):
        return self.add("pe", lambda e: e.transpose(out, in_, ident), reads, writes)

    def act(self, out, in_, func, reads, writes, scale=1.0, bias=None, accum_out=None):
        def f(e):
            kw = {}
            if bias is not None:
                kw["bias"] = bias
            if accum_out is not None:
                kw["accum_out"] = accum_out
            return e.activation(out=out, in_=in_, func=func, scale=scale, **kw)
        return self.add("act", f, reads, writes)

    def tt(self, eng, out, in0, in1, op, reads, writes):
        return self.add(eng, lambda e: e.tensor_tensor(out=out, in0=in0, in1=in1, op=op), reads, writes)

    def ts(self, eng, out, in0, s1, s2, op0, op1, reads, writes):
        if op1 is None:
            return self.add(eng, lambda e: e.tensor_scalar(out=out, in0=in0, scalar1=s1, scalar2=None, op0=op0), reads, writes)
        return self.add(eng, lambda e: e.tensor_scalar(out=out, in0=in0, scalar1=s1, scalar2=s2, op0=op0, op1=op1), reads, writes)

    def stt(self, out, in0, scalar, in1, op0, op1, reads, writes):
        return self.add("dve", lambda e: e.scalar_tensor_tensor(out=out, in0=in0, scalar=scalar, in1=in1, op0=op0, op1=op1), reads, writes)

    def copy(self, eng, out, in_, reads, writes):
        if eng == "act":
            return self.add("act", lambda e: e.activation(out=out, in_=in_, func=AF.Copy), reads, writes)
        return self.add(eng, lambda e: e.tensor_copy(out=out, in_=in_), reads, writes)

    def memset(self, eng, ap, val, writes):
        return self.add(eng, lambda e: e.memset(ap, val), (), writes)

    def dma(self, q, out, in_, reads, writes, res, **kw):
        return self.add(q, lambda e: e.dma_start(out=out, in_=in_, **kw), reads, writes, dma=res)

    def _prepare(self):
        for e in self.ENG:
            for ins in self.streams[e]:
                for d in ins.deps:
                    if not d.is_dma:
                        d.need_sig = True
        for e in self.ENG:
            c = 0
            for ins in self.streams[e]:
                if not ins.is_dma and ins.need_sig:
                    assert ins.fn is not None
                    c += 1
                    ins.sig_val = c

    def _emit_one(self, e, eng):
        waited = {}
        for ins in self.streams[e]:
            need = {}
            for d in ins.deps:
                if d.is_dma:
                    key = ("d", id(d.sem))
                    sem = d.sem
                    val = d.sem_val
                else:
                    key = ("e", d.eng)
                    sem = self.esem[d.eng]
                    val = d.sig_val
                if waited.get(key, 0) >= val:
                    continue
                if key not in need or need[key][1] < val:
                    need[key] = (sem, val)
            for key, (sem, val) in need.items():
                eng.wait_ge(sem, val)
                waited[key] = val
            if ins.fn is None:
                continue
            r = ins.fn(eng)
            if ins.is_dma:
                r.then_inc(ins.sem, 16)
            elif ins.need_sig:
                r.then_inc(self.esem[ins.eng], 1)


def build_program(nc, body):
    with ExitStack() as es:
        p = Prog(nc, es)
        body(p)
        p.barrier()
        p._prepare()
        block = es.enter_context(nc.Block())

        def sect(name):
            def f(eng):
                p._emit_one(name, eng)
            return f

        block.tensor(sect("pe"))
        block.scalar(sect("act"))
        block.vector(sect("dve"))
        block.gpsimd(sect("pool"))
        block.sync(sect("sp"))
    return nc


class Consts:
    pass


def setup_consts(p, small_dram):
    c = Consts()
    c.r = Res("consts")
    c.ones_bf = p.alloc(128, BF16)
    c.ones_f = p.alloc(128, F32)
    c.ident_bf = p.alloc(128, BF16)
    c.zeros = p.alloc(512, F32)
    c.causal_ge = p.alloc(128, F32)
    c.eps_rms = p.alloc(1, F32)
    c.eps_ln = p.alloc(1, F32)
    identf = p.alloc(128, F32)
    c.ffn_halo = p.alloc(FC * 2, F32).rearrange("p (j k) -> p j k", j=FC)
    c.r_halo = Res("ffn_halo")
    p.memset("pool", c.ones_bf, 1.0, [c.r])
    p.memset("pool", c.ones_f, 1.0, [c.r])
    p.memset("pool", c.zeros, 0.0, [c.r])
    p.memset("pool", c.eps_rms, RMS_EPS, [c.r])
    p.memset("pool", c.eps_ln, LN_EPS, [c.r])
    p.memset("pool", c.causal_ge, 0.0, [c.r])
    p.add("pool", lambda e: e.affine_select(out=c.causal_ge, in_=c.causal_ge, pattern=[[1, 128]], base=0,
                                            channel_multiplier=-1, compare_op=ALU.is_ge, fill=-1e5), [c.r], [c.r])
    p.memset("pool", identf, 0.0, [c.r])
    p.add("pool", lambda e: e.affine_select(out=identf, in_=identf, pattern=[[1, 128]], base=0,
                                            channel_multiplier=-1, compare_op=ALU.not_equal, fill=1.0), [c.r], [c.r])
    p.copy("pool", c.ident_bf, identf, [c.r], [c.r])
    c.small = {}
    for name, ap in small_dram.items():
        n = ap.shape[1]
        t = p.alloc(n, F32)
        p.dma("sp", t, ap, [], [c.r], c.r)
        c.small[name] = t
    p.floor = p.top
    return c


def norm_bufs(p):
    xt = [p.alloc(KC * 256, F32).rearrange("p (k t) -> p k t", k=KC) for _ in range(2)]
    sq = [p.alloc(KC * 256, BF16).rearrange("p (k t) -> p k t", k=KC) for _ in range(2)]
    rs = [p.alloc(256, F32) for _ in range(2)]
    r_xt = [Res("n_xt%d" % i) for i in range(2)]
    r_sq = [Res("n_sq%d" % i) for i in range(2)]
    r_rs = [Res("n_rs%d" % i) for i in range(2)]
    return xt, sq, rs, r_xt, r_sq, r_rs


def norm_phase(p, c, x_dram, col0, ntok, g, hT, rh, hcol0, bufs=None):
    xv = x_dram.rearrange("(k p) t -> p k t", p=128)
    xt, sq, rs, r_xt, r_sq, r_rs = bufs if bufs is not None else norm_bufs(p)
    assert ntok % 128 == 0
    t = 0
    it = 0
    while t < ntok:
        n = min(256, ntok - t)
        s = it % 2
        it += 1
        p.dma("sp", xt[s][:, :, 0:n], xv[:, :, col0 + t:col0 + t + n], [], [r_xt[s]], r_xt[s])
        p.act(sq[s][:, :, 0:n], xt[s][:, :, 0:n], AF.Square, [r_xt[s]], [r_sq[s]])
        ps, rps = p.bank()
        for k in range(KC):
            p.mm(ps[:, 0:n], c.ones_bf, sq[s][:, k, 0:n], k == 0, k == KC - 1, [r_sq[s], c.r], [rps])
        p.act(rs[s][:, 0:n], ps[:, 0:n], AF.Sqrt, [rps, c.r], [r_rs[s]], scale=1.0 / D, bias=c.eps_rms)
        p.add("dve", lambda e, o=rs[s][:, 0:n]: e.reciprocal(out=o, in_=o), [r_rs[s]], [r_rs[s]])
        hc = hcol0 + t
        rr = rh[hc // 256]
        for k in range(KC):
            p.stt(hT[:, k, hc:hc + n], xt[s][:, k, 0:n], g[:, k:k + 1], rs[s][:, 0:n], ALU.mult, ALU.mult,
                  [r_xt[s], r_rs[s], c.r], [rr])
        t += n


def rh_reads(rh, c0, n):
    return [rh[i] for i in range(c0 // 256, (c0 + n - 1) // 256 + 1)]


class WStream:
    def __init__(self, p, nslots, kc, cb, name):
        self.p = p
        self.kc = kc
        self.cb = cb
        self.slots = [p.alloc(kc * cb, BF16) for _ in range(nslots)]
        self.res = [Res("%s_w%d" % (name, i)) for i in range(nslots)]
        self.n = 0

    def load(self, blk_ap):
        s = self.n % len(self.slots)
        self.n += 1
        self.p.dma("pool", self.slots[s], blk_ap, [], [self.res[s]], self.res[s], max_dma_last_dim=8192)
        return self.slots[s].rearrange("p (k f) -> p k f", k=self.kc), self.res[s]


def linear_fm(p, ws, blocks, hT, rh, tok_tiles, consume, tok_outer=False):
    nfo = ws.cb // 128
    pending = None
    loaded = []
    for bi, (bap, tag) in enumerate(blocks):
        if bi == 0:
            loaded.append(ws.load(bap))
        if bi + 1 < len(blocks):
            loaded.append(ws.load(blocks[bi + 1][0]))
        wv, rw = loaded[bi]
        order = ([(ti, fo) for ti in range(len(tok_tiles)) for fo in range(nfo)] if tok_outer
                 else [(ti, fo) for fo in range(nfo) for ti in range(len(tok_tiles))])
        for ti, fo in order:
            c0, n = tok_tiles[ti]
            ps, rps = p.bank()
            for k in range(ws.kc):
                rr = rh(k, c0, n) if callable(rh) else rh_reads(rh, c0, n)
                p.mm(ps[:, 0:n], wv[:, k, fo * 128:(fo + 1) * 128], hT[:, k, c0:c0 + n], k == 0, k == ws.kc - 1,
                     [rw] + rr, [rps])
            consume(tag, fo, ti, c0, n, ps, rps)


def linear_tm(p, ws, blocks, hT, rh, tok0, ntok, consume):
    loaded = []
    for bi, (bap, tag) in enumerate(blocks):
        if bi == 0:
            loaded.append(ws.load(bap))
        if bi + 1 < len(blocks):
            loaded.append(ws.load(blocks[bi + 1][0]))
        wv, rw = loaded[bi]
        for c in range(tok0, tok0 + ntok, 128):
            ps, rps = p.bank()
            for k in range(ws.kc):
                p.mm(ps[:, 0:ws.cb], hT[:, k, c:c + 128], wv[:, k, :], k == 0, k == ws.kc - 1,
                     [rw] + rh_reads(rh, c, 128), [rps])
            consume(tag, c, ps, rps)


class Stage:
    def __init__(self, p, nslots, n, dtype, name):
        self.t = [p.alloc(n, dtype) for _ in range(nslots)]
        self.r = [Res("%s%d" % (name, i)) for i in range(nslots)]
        self.i = 0

    def next(self):
        s = self.i % len(self.t)
        self.i += 1
        return self.t[s], self.r[s]


def inproj_phase(p, c, d):
    p.reset()
    NCH = 2048
    hT = p.alloc(KC * NCH, BF16).rearrange("p (k t) -> p k t", k=KC)
    rh = {i: Res("hT%d" % i) for i in range(NCH // 256)}
    ws = WStream(p, 2, KC, 512, "inp")
    st_bf = Stage(p, 4, 512, BF16, "st_bf")
    st_sg = Stage(p, 2, 512, F32, "st_sg")
    st_s = Stage(p, 2, 512, F32, "st_s")
    g = c.small["mixn0"]
    w = d["w_in"]
    tiles = [(0, 512), (512, 512), (1024, 512), (1536, 512)]
    nb_ = norm_bufs(p)
    rz = Res("zpad")
    for jj in range(8):
        p.dma("sp", d["sT"][jj * 128:(jj + 1) * 128, NW:NW + 32], c.zeros[:, 0:32], [c.r], [], rz)
    for base in (0, NCH):
        def cons_q(tag, fo, ti, c0, n, ps, rps, base=base):
            t, r = st_bf.next()
            p.copy("act", t[:, 0:n], ps[:, 0:n], [rps], [r])
            f0 = (tag[1] * 4 + fo) * 128
            dst = d["qT"] if tag[0] == "q" else d["kT"]
            p.dma("sp", dst[f0:f0 + 128, base + c0:base + c0 + n], t[:, 0:n], [r], [], r)

        def cons_v(tag, cc, ps, rps, base=base):
            t, r = st_bf.next()
            p.copy("dve", t[:, 0:512], ps[:, 0:512], [rps], [r])
            f0 = tag[1] * 512
            p.dma("sp", d["V"][base + cc:base + cc + 128, f0:f0 + 512], t[:, 0:512], [r], [], r)

        glu = {}

        def cons_u(tag, fo, ti, c0, n, ps, rps, base=base, glu=glu):
            glu[fo] = (ps, rps)
            if fo == 3:
                for j in range(2):
                    pa, ra = glu[j]
                    pg, rg = glu[2 + j]
                    sg, rsg = st_sg.next()
                    p.act(sg[:, 0:n], pg[:, 0:n], AF.Sigmoid, [rg], [rsg])
                    s, rs_ = st_s.next()
                    p.tt("dve", s[:, 0:n], pa[:, 0:n], sg[:, 0:n], ALU.mult, [ra, rsg], [rs_])
                    f0 = (tag[1] * 2 + j) * 128
                    p.dma("sp", d["sT"][f0:f0 + 128, base + c0:base + c0 + n], s[:, 0:n], [rs_], [], rs_)

        norm_phase(p, c, d["xT"], base, NCH, g, hT, rh, 0, nb_)
        linear_fm(p, ws, [(w[i], ("u", i)) for i in range(4)], hT, rh, tiles, cons_u, tok_outer=True)
        linear_fm(p, ws, [(w[4 + i], ("q", i)) for i in range(2)], hT, rh, tiles, cons_q)
        linear_fm(p, ws, [(w[6 + i], ("k", i)) for i in range(2)], hT, rh, tiles, cons_q)
        linear_tm(p, ws, [(w[8 + i], ("v", i)) for i in range(2)], hT, rh, 0, NCH, cons_v)
    p.barrier()


def conformer_gen(p, c, d):
    W = 31
    sb = [p.alloc(512 + 32, F32) for _ in range(3)]
    r_sb = [Res("cf_s%d" % i) for i in range(3)]
    h = p.alloc(8 * 512, F32).rearrange("p (j t) -> p j t", j=8)
    sq = p.alloc(8 * 512, F32).rearrange("p (j t) -> p j t", j=8)
    r_h = [Res("cf_h%d" % j) for j in range(8)]
    r_sq = [Res("cf_sq%d" % j) for j in range(8)]
    mean = p.alloc(512, F32)
    msq = p.alloc(512, F32)
    rstd = p.alloc(512, F32)
    r_mean, r_msq, r_rstd = Res("cf_mean"), Res("cf_msq"), Res("cf_rstd")
    tmp = [p.alloc(512, F32) for _ in range(2)]
    r_tmp = [Res("cf_tmp%d" % i) for i in range(2)]
    yst = Stage(p, 2, 512, BF16, "cf_y")
    cw, cb, lg, lb = c.small["conv_w"], c.small["conv_b"], c.small["ln_g"], c.small["ln_b"]
    cwv = cw.rearrange("p (j k) -> p j k", j=8)
    ld = 0
    for (c0, n) in W_TILES:
        for j in range(8):
            s = ld % 3
            ld += 1
            p.dma("sp", sb[s][:, 0:n + 30], d["sT"][j * 128:(j + 1) * 128, c0:c0 + n + 30], [], [r_sb[s]], r_sb[s])
            hj = h[:, j, 0:n]
            p.ts("dve", hj, sb[s][:, 30:30 + n], cwv[:, j, 0:1], cb[:, j:j + 1], ALU.mult, ALU.add,
                 [r_sb[s], c.r], [r_h[j]])
            for k in range(1, W):
                p.stt(hj, sb[s][:, 30 - k:30 - k + n], cwv[:, j, k:k + 1], hj, ALU.mult, ALU.add,
                      [r_sb[s], r_h[j], c.r], [r_h[j]])
                if k % 2 == 0:
                    yield
            p.act(sq[:, j, 0:n], hj, AF.Square, [r_h[j]], [r_sq[j]])
        ps1, rp1 = p.bank()
        for j in range(8):
            p.mm(ps1[:, 0:n], c.ones_f, h[:, j, 0:n], j == 0, j == 7, [r_h[j], c.r], [rp1])
        ps2, rp2 = p.bank()
        for j in range(8):
            p.mm(ps2[:, 0:n], c.ones_f, sq[:, j, 0:n], j == 0, j == 7, [r_sq[j], c.r], [rp2])
        p.add("act", lambda e, o=mean[:, 0:n], i=ps1[:, 0:n]: e.activation(out=o, in_=i, func=AF.Copy, scale=1.0 / 1024),
              [rp1], [r_mean])
        p.tt("dve", msq[:, 0:n], mean[:, 0:n], mean[:, 0:n], ALU.mult, [r_mean], [r_msq])
        p.stt(rstd[:, 0:n], ps2[:, 0:n], 1.0 / 1024, msq[:, 0:n], ALU.mult, ALU.subtract, [rp2, r_msq], [r_rstd])
        p.act(rstd[:, 0:n], rstd[:, 0:n], AF.Sqrt, [r_rstd, c.r], [r_rstd], bias=c.eps_ln)
        p.add("dve", lambda e, o=rstd[:, 0:n]: e.reciprocal(out=o, in_=o), [r_rstd], [r_rstd])
        yield
        for j in range(8):
            if j % 2 == 0:
                yield
            t = tmp[j % 2][:, 0:n]
            rt = r_tmp[j % 2]
            p.tt("dve", t, h[:, j, 0:n], mean[:, 0:n], ALU.subtract, [r_h[j], r_mean], [rt])
            p.tt("dve", t, t, rstd[:, 0:n], ALU.mult, [rt, r_rstd], [rt])
            y, ry = yst.next()
            p.act(y[:, 0:n], t, AF.Silu, [rt, c.r], [ry], scale=lg[:, j:j + 1], bias=lb[:, j:j + 1])
            p.dma("sp", d["yT"][j * 128:(j + 1) * 128, c0:c0 + n], y[:, 0:n], [ry], [], ry)
    yield


def run_interleaved(items, S, mk, stagger=True, side=None):
    free = list(range(S))
    active = []
    it = iter(items)
    done = False
    sweeps = 0
    while True:
        while free and not done and (not stagger or not active or sweeps >= 2):
            sweeps = 0
            x = next(it, None)
            if x is None:
                done = True
                break
            sl = free.pop(0)
            active.append((mk(x, sl), sl))
            if stagger and free and not done:
                break
        if not active:
            break
        sweeps += 1
        if side is not None and side[0] is not None:
            try:
                next(side[0])
            except StopIteration:
                side[0] = None
        for g, sl in list(active):
            try:
                next(g)
            except StopIteration:
                active.remove((g, sl))
                free.append(sl)


NSTREAM = 3


def moba_phase(p, c, d, with_conformer=True):
    p.reset()
    S = 3
    side = [conformer_gen(p, c, d)] if with_conformer else [None]
    if with_conformer:
        next(side[0])
    p.rot = list(range(8 - S))
    BIG = 1.0e30
    NEGB = 3.0e4
    scale = HD ** -0.5
    NQT = NW // 128
    qT = [p.alloc(NW, BF16) for _ in range(2)]
    kT = [p.alloc(NW, BF16) for _ in range(2)]
    Vh = [p.alloc(32 * 128, BF16).rearrange("p (s d) -> p s d", s=32) for _ in range(2)]
    r_q = [Res("mb_q%d" % i) for i in range(2)]
    r_k = [Res("mb_k%d" % i) for i in range(2)]
    r_v = [Res("mb_v%d" % i) for i in range(2)]
    kmf = [p.alloc(16, F32) for _ in range(2)]
    kmb = [p.alloc(16, BF16) for _ in range(2)]
    r_km = [Res("mb_km%d" % i) for i in range(2)]
    vbias = p.alloc(16, F32)
    r_vb = Res("mb_vb")
    bv = c.small["blkvalid"]
    p.ts("dve", vbias, bv, -1.0, BIG, ALU.add, ALU.mult, [c.r], [r_vb])
    yst = [p.alloc(NW, BF16) for _ in range(2)]
    r_y = [Res("mb_y%d" % i) for i in range(2)]
    streams = []
    for si in range(S):
        st = dict(
            gm=p.alloc(16, F32), top8=p.alloc(8, F32), sel=p.alloc(16, F32), r_g=Res("mb_g%d" % si),
            pexp=[p.alloc(512, BF16) for _ in range(2)], r_pe=[Res("mb_p%d_%d" % (si, i)) for i in range(2)],
            dtmp=p.alloc(128, F32), r_dt=Res("mb_dt%d" % si),
            pT=[p.alloc(512, BF16) for _ in range(2)], r_pT=[Res("mb_pT%d_%d" % (si, i)) for i in range(2)],
            rsum=p.alloc(24, F32), r_rs=Res("mb_rs%d" % si), rinv=p.alloc(1, F32),
            ob=p.alloc(128, BF16), r_ob=Res("mb_ob%d" % si), acc=8 - S + si, n=0)
        streams.append(st)

    def qgen(h, hs, qt, st):
        i0 = qt * 128
        nb = i0 // 256
        qtile = qT[hs][:, i0:i0 + 128]
        gm, top8, sel, r_g = st["gm"], st["top8"], st["sel"], st["r_g"]
        npast = 15 - nb
        p.memset("pool", gm, -BIG, [r_g])
        if npast > 0:
            psg, rpg = p.bank()
            p.mm(psg[:, 0:16], qtile, kmb[hs], True, True, [r_q[hs], r_km[hs]], [rpg])
            p.tt("dve", gm[:, nb + 1:16], psg[:, nb + 1:16], vbias[:, nb + 1:16], ALU.add, [rpg, r_vb], [r_g])
        p.add("dve", lambda e, o=top8, i=gm: e.max(out=o, in_=i), [r_g], [r_g])
        p.ts("dve", sel, gm, top8[:, 2:3], None, ALU.is_ge, None, [r_g], [r_g])
        p.tt("dve", sel, sel, bv, ALU.mult, [r_g, c.r], [r_g])
        p.ts("dve", sel, sel, -1.0, NEGB, ALU.add, ALU.mult, [r_g], [r_g])
        yield
        segs = [(i0, 128, "diag")]
        k = i0 + 128
        if qt % 2 == 0:
            segs.append((k, 128, "own"))
            k += 128
        while k < NW:
            segs.append((k, 256, k // 256))
            k += 256
        tiles = []
        cur = []
        curn = 0
        for sg in segs:
            if curn + sg[1] > 512:
                tiles.append(cur)
                cur = []
                curn = 0
            cur.append(sg)
            curn += sg[1]
        tiles.append(cur)
        po, rpo = p.fixed(st["acc"])
        nsub_total = sum(s_[1] for s_ in segs) // 128
        sub_done = 0
        rcol = 0
        rsm, rrs = st["rsum"], st["r_rs"]
        for tl in tiles:
            k0 = tl[0][0]
            nk = sum(s_[1] for s_ in tl)
            pz, rpz = p.bank()
            p.mm(pz[:, 0:nk], qtile, kT[hs][:, k0:k0 + nk], True, True, [r_q[hs], r_k[hs]], [rpz])
            px = st["n"] % 2
            st["n"] += 1
            pexp, r_pe = st["pexp"][px], st["r_pe"][px]
            pT, r_pT = st["pT"][px], st["r_pT"][px]
            yield
            col = 0
            for (ks, kn, kind) in tl:
                if kind == "diag":
                    dt_ = st["dtmp"]
                    p.tt("dve", dt_, pz[:, col:col + 128], c.causal_ge, ALU.add, [rpz, c.r], [st["r_dt"]])
                    p.act(pexp[:, col:col + 128], dt_, AF.Exp, [st["r_dt"]], [r_pe, rrs], scale=scale,
                          accum_out=rsm[:, rcol:rcol + 1])
                elif kind == "own":
                    p.act(pexp[:, col:col + kn], pz[:, col:col + kn], AF.Exp, [rpz], [r_pe, rrs], scale=scale,
                          accum_out=rsm[:, rcol:rcol + 1])
                else:
                    p.act(pexp[:, col:col + kn], pz[:, col:col + kn], AF.Exp, [rpz, r_g], [r_pe, rrs],
                          scale=scale, bias=sel[:, kind:kind + 1], accum_out=rsm[:, rcol:rcol + 1])
                rcol += 1
                col += kn
            yield
            pt_ps, rpt = p.bank()
            ptv = pt_ps[:, 0:256].bitcast(BF16)
            nsub = nk // 128
            for j in range(nsub):
                p.tr(ptv[:, j * 128:(j + 1) * 128], pexp[:, j * 128:(j + 1) * 128], c.ident_bf, [r_pe, c.r], [rpt])
            yield
            p.copy("dve" if (st["n"] % 2) else "act", pT[:, 0:nk], ptv[:, 0:nk], [rpt], [r_pT])
            yield
            for j in range(nsub):
                sbi = (k0 // 128) + j
                p.mm(po[:, 0:128], pT[:, j * 128:(j + 1) * 128], Vh[hs][:, sbi, :], sub_done == 0,
                     sub_done == nsub_total - 1, [r_pT, r_v[hs]], [rpo])
                sub_done += 1
            yield
        rinv, ob, r_ob = st["rinv"], st["ob"], st["r_ob"]
        p.add("dve", lambda e, o=rinv, i=rsm[:, 0:rcol]: e.tensor_reduce(out=o, in_=i, axis=AX.X, op=ALU.add),
              [rrs], [rrs])
        p.add("dve", lambda e, o=rinv: e.reciprocal(out=o, in_=o), [rrs], [rrs])
        p.ts("dve", ob, po[:, 0:128], rinv[:, 0:1], None, ALU.mult, None, [rpo, rrs], [r_ob])
        pt2, rpt2 = p.bank()
        pt2v = pt2[:, 0:64].bitcast(BF16)
        p.tr(pt2v, ob, c.ident_bf, [r_ob, c.r], [rpt2])
        p.copy("act", yst[hs][:, i0:i0 + 128], pt2v, [rpt2], [r_y[hs]])
        yield

    for h in range(8):
        hs = h % 2
        p.dma("sp", qT[hs], d["qT"][h * 128:(h + 1) * 128, 0:NW], [], [r_q[hs]], r_q[hs])
        p.dma("sp", kT[hs], d["kT"][h * 128:(h + 1) * 128, :], [], [r_k[hs]], r_k[hs])
        p.dma("sp", Vh[hs], d["V"][:, h * 128:(h + 1) * 128].rearrange("(s q) e -> q s e", q=128), [], [r_v[hs]], r_v[hs])
        p.add("dve", lambda e, o=kmf[hs], i=kT[hs].rearrange("p (n s) -> p n s", n=16): e.tensor_reduce(out=o, in_=i, axis=AX.X, op=ALU.add),
              [r_k[hs]], [r_km[hs]])
        p.ts("dve", kmb[hs], kmf[hs], 1.0 / 256, None, ALU.mult, None, [r_km[hs]], [r_km[hs]])
        run_interleaved(range(NQT), S, lambda qt, sl, h=h, hs=hs: qgen(h, hs, qt, streams[sl]), side=side)
        p.dma("sp", d["yT"][1024 + h * 128:1024 + (h + 1) * 128, 0:NW], yst[hs], [r_y[hs]], [], r_y[hs])
    while side[0] is not None:
        try:
            next(side[0])
        except StopIteration:
            side[0] = None
    p.barrier()


def load_hT_from_dram(p, src, nrow_chunks, t0, ntok, name):
    hT = p.alloc(nrow_chunks * ntok, BF16).rearrange("p (k t) -> p k t", k=nrow_chunks)
    rk = [Res("%s%d" % (name, k)) for k in range(nrow_chunks)]
    for k in range(nrow_chunks):
        p.dma("sp", hT[:, k, :], src[k * 128:(k + 1) * 128, t0:t0 + ntok], [], [rk[k]], rk[k])
    return hT, (lambda k, c0, n: [rk[k]])


def outproj_phase(p, c, d, yname, wname, xin, xout, chunks):
    for (t0, tiles) in chunks:
        p.reset()
        ntok = sum(n for _, n in tiles)
        hT, rh = load_hT_from_dram(p, d[yname], KC, t0, ntok, "op_y")
        ws = WStream(p, 2, KC, 512, "op")
        xst = Stage(p, 3, 512, F32, "op_x")
        w = d[wname]

        def cons(tag, fo, ti, c0, n, ps, rps, t0=t0, xst=xst):
            f0 = (tag * 4 + fo) * 128
            t, r = xst.next()
            p.dma("sp", t[:, 0:n], d[xin][f0:f0 + 128, t0 + c0:t0 + c0 + n], [], [r], r)
            p.tt("dve", t[:, 0:n], ps[:, 0:n], t[:, 0:n], ALU.add, [rps, r], [r])
            p.dma("sp", d[xout][f0:f0 + 128, t0 + c0:t0 + c0 + n], t[:, 0:n], [r], [], r)

        linear_fm(p, ws, [(w[i], i) for i in range(4)], hT, rh, tiles, cons)
        p.barrier()


def ffn_phase(p, c, d, L, xin, xout, chunks):
    g = c.small["ffnn%d" % L]
    fw, fb = c.small["fconv_w%d" % L], c.small["fconv_b%d" % L]
    fwv = fw.rearrange("p (j k) -> p j k", j=FC)
    valid = c.small["valid"]
    wu, wg = d["w_up%d" % L], d["w_gate%d" % L]
    nblk = DFF // 256
    for (t0, tiles, halo, halves) in chunks:
        p.reset()
        ntok = sum(n for _, n in tiles)
        hT = p.alloc(KC * ntok, BF16).rearrange("p (k t) -> p k t", k=KC)
        rh = {i: Res("ff_h%d" % i) for i in range((ntok + 255) // 256)}
        mark = p.top
        norm_phase(p, c, d[xin], t0, ntok, g, hT, rh, 0)
        p.barrier()
        p.top = mark
        wsu = WStream(p, 2, KC, 256, "ffu")
        wsg = WStream(p, 2, KC, 256, "ffg")
        NB = ntok + 2
        upb = [p.alloc(NB, F32) for _ in range(2)]
        gb = [p.alloc(ntok, F32) for _ in range(2)]
        u = [p.alloc(ntok, F32) for _ in range(2)]
        ab = [p.alloc(ntok, BF16) for _ in range(2)]
        r_up = [Res("ff_up%d" % i) for i in range(2)]
        r_gb = [Res("ff_gb%d" % i) for i in range(2)]
        r_u = [Res("ff_u%d" % i) for i in range(2)]
        r_ab = [Res("ff_ab%d" % i) for i in range(2)]
        lu = [wsu.load(wu[0])]
        lg_ = [wsg.load(wg[0])]
        for b in range(nblk):
            if b + 1 < nblk:
                lu.append(wsu.load(wu[b + 1]))
                lg_.append(wsg.load(wg[b + 1]))
            for fo in range(2):
                j = b * 2 + fo
                s = j % 2
                wv, rw = lu[b]
                for (c0, n) in tiles:
                    ps, rps = p.bank()
                    for k in range(KC):
                        p.mm(ps[:, 0:n], wv[:, k, fo * 128:(fo + 1) * 128], hT[:, k, c0:c0 + n], k == 0, k == KC - 1,
                             [rw] + rh_reads(rh, c0, n), [rps])
                    p.copy("act", upb[s][:, c0:c0 + n], ps[:, 0:n], [rps], [r_up[s]])
                wv, rw = lg_[b]
                for (c0, n) in tiles:
                    ps, rps = p.bank()
                    for k in range(KC):
                        p.mm(ps[:, 0:n], wv[:, k, fo * 128:(fo + 1) * 128], hT[:, k, c0:c0 + n], k == 0, k == KC - 1,
                             [rw] + rh_reads(rh, c0, n), [rps])
                    p.copy("act", gb[s][:, c0:c0 + n], ps[:, 0:n], [rps], [r_gb[s]])
                if halo == "cr":
                    p.ts("dve", upb[s][:, NOWN:NOWN + 2], upb[s][:, NOWN:NOWN + 2], valid[:, 0:1], None, ALU.mult, None,
                         [r_up[s], c.r], [r_up[s]])
                    p.memset("pool", upb[s][:, ntok:NB], 0.0, [r_up[s]])
                elif halo == "save":
                    p.memset("pool", upb[s][:, ntok:NB], 0.0, [r_up[s]])
                    p.copy("pool", c.ffn_halo[:, j, :], upb[s][:, 0:2], [r_up[s]], [c.r_halo])
                else:
                    p.ts("dve", upb[s][:, ntok:NB], c.ffn_halo[:, j, :], valid[:, 0:1], None, ALU.mult, None,
                         [c.r_halo, c.r], [r_up[s]])
                us = u[s]
                p.ts("dve", us, upb[s][:, 0:ntok], fwv[:, j, 2:3], fb[:, j:j + 1], ALU.mult, ALU.add, [r_up[s], c.r], [r_u[s]])
                p.stt(us, upb[s][:, 1:ntok + 1], fwv[:, j, 1:2], us, ALU.mult, ALU.add, [r_up[s], r_u[s], c.r], [r_u[s]])
                p.stt(us, upb[s][:, 2:ntok + 2], fwv[:, j, 0:1], us, ALU.mult, ALU.add, [r_up[s], r_u[s], c.r], [r_u[s]])
                p.act(us, us, AF.Silu, [r_u[s]], [r_u[s]])
                p.tt("pool", ab[s], us, gb[s], ALU.mult, [r_u[s], r_gb[s]], [r_ab[s]])
                p.dma("sp", d["actT"][j * 128:(j + 1) * 128, t0:t0 + ntok], ab[s], [r_ab[s]], [], r_ab[s])
        p.barrier()
        for hv in halves:
            p.reset()
            h0 = hv[0][0]
            nt = sum(n for _, n in hv)
            aT = p.alloc(FC * nt, BF16).rearrange("p (k t) -> p k t", k=FC)
            rk = [Res("fd_a%d" % k) for k in range(FC)]
            for k in range(FC):
                p.dma("sp", aT[:, k, :], d["actT"][k * 128:(k + 1) * 128, t0 + h0:t0 + h0 + nt], [], [rk[k]], rk[k])
            ra = (lambda k, c0, n, rk=rk: [rk[k]])
            ws = WStream(p, 2, FC, 128, "ffd")
            xst = Stage(p, 3, 512, F32, "fd_x")
            wd = d["w_down%d" % L]
            rel_tiles = [(c0 - h0, n) for (c0, n) in hv]

            def cons(tag, fo, ti, c0, n, ps, rps, tb=t0 + h0, xst=xst):
                f0 = tag * 128
                t, r = xst.next()
                p.dma("sp", t[:, 0:n], d[xin][f0:f0 + 128, tb + c0:tb + c0 + n], [], [r], r)
                p.tt("dve", t[:, 0:n], ps[:, 0:n], t[:, 0:n], ALU.add, [rps, r], [r])
                p.dma("sp", d[xout][f0:f0 + 128, tb + c0:tb + c0 + n], t[:, 0:n], [r], [], r)

            linear_fm(p, ws, [(wd[i], i) for i in range(16)], aT, ra, rel_tiles, cons)
            p.barrier()


def qkv_phase(p, c, d):
    p.reset()
    hT = p.alloc(KC * NCR, BF16).rearrange("p (k t) -> p k t", k=KC)
    rh = {i: Res("qk_h%d" % i) for i in range((NCR + 255) // 256)}
    nb_ = norm_bufs(p)
    ws = WStream(p, 2, KC, 512, "qkv")
    st_bf = Stage(p, 4, 512, BF16, "qk_st")
    w = d["w_qkv"]
    valid = c.small["valid"]

    def mk(dst, base):
        def cons(tag, fo, ti, c0, n, ps, rps):
            t, r = st_bf.next()
            p.copy("act", t[:, 0:n], ps[:, 0:n], [rps], [r])
            f0 = (tag * 4 + fo) * 128
            p.dma("sp", d[dst][f0:f0 + 128, base + c0:base + c0 + n], t[:, 0:n], [r], [], r)
        return cons

    def mk_v(base):
        def cons_v(tag, cc, ps, rps):
            t, r = st_bf.next()
            if base + cc >= NOWN:
                p.ts("dve", t[:, 0:512], ps[:, 0:512], valid[:, 0:1], None, ALU.mult, None, [rps, c.r], [r])
            else:
                p.copy("dve", t[:, 0:512], ps[:, 0:512], [rps], [r])
            p.dma("sp", d["V1"][base + cc:base + cc + 128, tag * 512:(tag + 1) * 512], t[:, 0:512], [r], [], r)
        return cons_v

    norm_phase(p, c, d["x2T"], 0, NCR, c.small["mixn1"], hT, rh, 0, nb_)
    linear_fm(p, ws, [(w[i], i) for i in range(4)], hT, rh, CR_TILES, mk("q1T", 0))
    linear_fm(p, ws, [(w[4 + i], i) for i in range(4)], hT, rh, CR_TILES, mk("k1T", 0))
    linear_tm(p, ws, [(w[8 + i], i) for i in range(4)], hT, rh, 0, NCR, mk_v(0))
    nrest = NW - NCR
    norm_phase(p, c, d["x2T"], NCR, nrest, c.small["mixn1"], hT, rh, 0, nb_)
    rest_tiles = [(0, 512), (512, 512), (1024, 512), (1536, 384)]
    linear_fm(p, ws, [(w[4 + i], i) for i in range(4)], hT, rh, rest_tiles, mk("k1T", NCR))
    linear_tm(p, ws, [(w[8 + i], i) for i in range(4)], hT, rh, 0, nrest, mk_v(NCR))
    p.barrier()


def sb_phase(p, c, d):
    p.reset()
    S = 4
    p.rot = list(range(8 - S))
    scale = HD ** -0.5
    NQT = NCR // 128
    qT = [p.alloc(NCR, BF16) for _ in range(2)]
    kT = [p.alloc(NW, BF16) for _ in range(2)]
    Vh = [p.alloc(32 * 128, BF16).rearrange("p (s d) -> p s d", s=32) for _ in range(2)]
    r_q = [Res("sb_q%d" % i) for i in range(2)]
    r_k = [Res("sb_k%d" % i) for i in range(2)]
    r_v = [Res("sb_v%d" % i) for i in range(2)]
    yst = [p.alloc(NCR, BF16) for _ in range(2)]
    r_y = [Res("sb_y%d" % i) for i in range(2)]
    streams = []
    for si in range(S):
        streams.append(dict(
            om=[p.alloc(512, F32) for _ in range(2)], r_om=[Res("sb_om%d_%d" % (si, i)) for i in range(2)],
            Cx=[p.alloc(513, F32) for _ in range(2)], r_cx=[Res("sb_cx%d_%d" % (si, i)) for i in range(2)],
            ab=[p.alloc(512, BF16) for _ in range(2)], r_ab=[Res("sb_ab%d_%d" % (si, i)) for i in range(2)],
            aT=[p.alloc(512, BF16) for _ in range(2)], r_aT=[Res("sb_aT%d_%d" % (si, i)) for i in range(2)],
            acc=8 - S + si, n=0))
    regcache = {}

    def qgen(h, hs, qt, st):
        i0 = qt * 128
        qtile = qT[hs][:, i0:i0 + 128]
        nkeys = NW - i0
        po, rpo = p.fixed(st["acc"])
        nsub_total = nkeys // 128
        sub_done = 0
        k0 = i0
        prev = None
        while k0 < NW:
            nk = min(512, NW - k0)
            s = st["n"] % 2
            st["n"] += 1
            om, r_om = st["om"][s], st["r_om"][s]
            Cx, r_cx = st["Cx"][s], st["r_cx"][s]
            ab, r_ab = st["ab"][s], st["r_ab"][s]
            aT, r_aT = st["aT"][s], st["r_aT"][s]
            pz, rpz = p.bank()
            p.mm(pz[:, 0:nk], qtile, kT[hs][:, k0:k0 + nk], True, True, [r_q[hs], r_k[hs]], [rpz])
            yield
            p.act(om[:, 0:nk], pz[:, 0:nk], AF.Sigmoid, [rpz], [r_om], scale=-scale)
            if k0 == i0:
                def sel_f(e, o=om[:, 0:128]):
                    if "one" not in regcache:
                        regcache["one"] = e.to_reg(1.0)
                    return e.affine_select(out=o, in_=o, pattern=[[1, 128]], base=0, channel_multiplier=-1,
                                           compare_op=ALU.is_gt, fill=regcache["one"])
                p.add("pool", sel_f, [r_om], [r_om])
                p.memset("dve", Cx[:, 0:1], 1.0, [r_cx])
            else:
                pcx, prcx, pnk = prev
                p.copy("dve", Cx[:, 0:1], pcx[:, pnk:pnk + 1], [prcx], [r_cx])
            yield
            p.add("dve", lambda e, o=Cx[:, 1:nk + 1], a=om[:, 0:nk], z=c.zeros[:, 0:nk], ini=Cx[:, 0:1]:
                  e.tensor_tensor_scan(out=o, data0=a, data1=z, initial=ini, op0=ALU.mult, op1=ALU.add),
                  [r_om, r_cx, c.r], [r_cx])
            prev = (Cx, r_cx, nk)
            yield
            p.tt("pool", ab[:, 0:nk], Cx[:, 0:nk], Cx[:, 1:nk + 1], ALU.subtract, [r_cx], [r_ab])
            yield
            pt_ps, rpt = p.bank()
            ptv = pt_ps[:, 0:256].bitcast(BF16)
            nsub = nk // 128
            for j in range(nsub):
                p.tr(ptv[:, j * 128:(j + 1) * 128], ab[:, j * 128:(j + 1) * 128], c.ident_bf, [r_ab, c.r], [rpt])
            yield
            p.copy("act", aT[:, 0:nk], ptv[:, 0:nk], [rpt], [r_aT])
            yield
            for j in range(nsub):
                sbi = (k0 // 128) + j
                p.mm(po[:, 0:128], Vh[hs][:, sbi, :], aT[:, j * 128:(j + 1) * 128], sub_done == 0,
                     sub_done == nsub_total - 1, [r_aT, r_v[hs]], [rpo])
                sub_done += 1
            k0 += nk
            yield
        p.copy("dve", yst[hs][:, i0:i0 + 128], po[:, 0:128], [rpo], [r_y[hs]])
        yield

    for h in range(16):
        hs = h % 2
        p.dma("sp", qT[hs], d["q1T"][h * 128:(h + 1) * 128, 0:NCR], [], [r_q[hs]], r_q[hs])
        p.dma("sp", kT[hs], d["k1T"][h * 128:(h + 1) * 128, :], [], [r_k[hs]], r_k[hs])
        p.dma("sp", Vh[hs], d["V1"][:, h * 128:(h + 1) * 128].rearrange("(s q) e -> q s e", q=128), [], [r_v[hs]], r_v[hs])
        run_interleaved(range(NQT), S, lambda qt, sl, h=h, hs=hs: qgen(h, hs, qt, streams[sl]))
        p.dma("sp", d["y1T"][h * 128:(h + 1) * 128, 0:NCR], yst[hs], [r_y[hs]], [], r_y[hs])
    p.barrier()


def final_norm_phase(p, c, d, xin):
    p.reset()
    g = c.small["finaln"]
    xv = d[xin].rearrange("(k p) t -> p k t", p=128)
    ov = d["outT"].rearrange("(k p) t -> p k t", p=128)
    xt = [p.alloc(KC * 256, F32).rearrange("p (k t) -> p k t", k=KC) for _ in range(2)]
    sqf = [p.alloc(KC * 256, F32).rearrange("p (k t) -> p k t", k=KC) for _ in range(2)]
    rs = [p.alloc(256, F32) for _ in range(2)]
    r_xt = [Res("fn_xt%d" % i) for i in range(2)]
    r_sq = [Res("fn_sq%d" % i) for i in range(2)]
    r_rs = [Res("fn_rs%d" % i) for i in range(2)]
    for it in range(NOWN // 256):
        s = it % 2
        t = it * 256
        p.dma("sp", xt[s], xv[:, :, t:t + 256], [], [r_xt[s]], r_xt[s])
        p.act(sqf[s], xt[s], AF.Square, [r_xt[s]], [r_sq[s]])
        ps, rps = p.bank()
        for k in range(KC):
            p.mm(ps[:, 0:256], c.ones_f, sqf[s][:, k, :], k == 0, k == KC - 1, [r_sq[s], c.r], [rps])
        p.act(rs[s], ps[:, 0:256], AF.Sqrt, [rps, c.r], [r_rs[s]], scale=1.0 / D, bias=c.eps_rms)
        p.add("dve", lambda e, o=rs[s]: e.reciprocal(out=o, in_=o), [r_rs[s]], [r_rs[s]])
        for k in range(KC):
            p.stt(sqf[s][:, k, :], xt[s][:, k, :], g[:, k:k + 1], rs[s], ALU.mult, ALU.mult,
                  [r_xt[s], r_rs[s], c.r], [r_sq[s]])
        p.dma("sp", ov[:, :, t:t + 256], sqf[s], [r_sq[s]], [], r_sq[s])
    p.barrier()


SMALL = {"mixn0": 16, "ffnn0": 16, "mixn1": 16, "ffnn1": 16, "finaln": 16, "conv_w": 8 * 31, "conv_b": 8,
         "ln_g": 8, "ln_b": 8, "fconv_w0": FC * 3, "fconv_b0": FC, "fconv_w1": FC * 3, "fconv_b1": FC,
         "valid": 1, "blkvalid": 16}
W_SHAPES = {"w_in": [10, 128, KC * 512], "w_out": [4, 128, KC * 512], "w_up0": [22, 128, KC * 256],
            "w_gate0": [22, 128, KC * 256], "w_down0": [16, 128, FC * 128], "w_qkv": [12, 128, KC * 512],
            "w_o": [4, 128, KC * 512], "w_up1": [22, 128, KC * 256], "w_gate1": [22, 128, KC * 256],
            "w_down1": [16, 128, FC * 128]}
T4 = [(0, 512), (512, 512), (1024, 512), (1536, 512)]


def build_fused(debug=False):
    nc = bass.Bass("TRN2", target_bir_lowering=False)
    d = {}
    d["xT"] = nc.dram_tensor("xT", [D, NW], F32, kind="ExternalInput").ap()
    for k, shp in W_SHAPES.items():
        d[k] = nc.dram_tensor(k, shp, F32, kind="ExternalInput").ap()
    small = {}
    for k, n in SMALL.items():
        small[k] = nc.dram_tensor("s_" + k, [128, n], F32, kind="ExternalInput").ap()

    def scr(name, shape, dt, out=False):
        d[name] = nc.dram_tensor(name, shape, dt, kind="ExternalOutput" if (out or debug) else "Internal").ap()

    scr("sT", [1024, NW + 32], F32)
    scr("qT", [1024, NW], BF16)
    scr("kT", [1024, NW], BF16)
    scr("V", [NW, 1024], BF16)
    scr("yT", [D, NW], BF16)
    scr("x1T", [D, NW], F32)
    scr("actT", [DFF, NW], BF16)
    scr("x2T", [D, NW], F32)
    scr("q1T", [D, NCR], BF16)
    scr("k1T", [D, NW], BF16)
    scr("V1", [NW, D], BF16)
    scr("y1T", [D, NCR], BF16)
    scr("x3T", [D, NCR], F32)
    scr("x4T", [D, NCR], F32)
    scr("outT", [D, NOWN], F32, out=True)

    def body(p):
        c = setup_consts(p, small)
        inproj_phase(p, c, d)
        moba_phase(p, c, d)
        outproj_phase(p, c, d, "yT", "w_out", "xT", "x1T", [(0, T4), (2048, T4)])
        ffn_phase(p, c, d, 0, "x1T", "x2T",
                  [(2048, T4, "save", [T4[0:2], T4[2:4]]), (0, T4, "use", [T4[0:2], T4[2:4]])])
        qkv_phase(p, c, d)
        sb_phase(p, c, d)
        outproj_phase(p, c, d, "y1T", "w_o", "x2T", "x3T", [(0, CR_TILES)])
        ffn_phase(p, c, d, 1, "x3T", "x4T", [(0, CR_TILES, "cr", [CR_TILES[0:2], CR_TILES[2:5]])])
        final_norm_phase(p, c, d, "x4T")

    build_program(nc, body)
    return nc


def wblk(W, cb):
    K, Fo = W.shape
    kc = K // 128
    nb = Fo // cb
    return np.ascontiguousarray(W.reshape(kc, 128, nb, cb).transpose(2, 1, 0, 3).reshape(nb, 128, kc * cb))


def pvec(v):
    n = v.shape[0] // 128
    return np.ascontiguousarray(v.reshape(n, 128).T)


def prep(inputs):
    f = lambda a: np.asarray(a, dtype=np.float32)
    x = f(inputs["x"])
    w_in = f(inputs["even_w_in"])[0]
    cols = []
    for i in range(4):
        for j in (2 * i, 2 * i + 1):
            cols.append(np.arange(j * 128, (j + 1) * 128))
        for j in (2 * i, 2 * i + 1):
            cols.append(np.arange(1024 + j * 128, 1024 + (j + 1) * 128))
    cols.append(np.arange(2048, 5120))
    cols = np.concatenate(cols)

    def cw(a, n):
        return np.ascontiguousarray(a.T.reshape(n, 128, a.shape[0]).transpose(1, 0, 2).reshape(128, n * a.shape[0]))

    shared = {
        "w_in": wblk(w_in[:, cols], 512),
        "w_out": wblk(f(inputs["even_w_out"])[0], 512),
        "w_qkv": wblk(f(inputs["odd_w_qkv"])[0], 512),
        "w_o": wblk(f(inputs["odd_w_o"])[0], 512),
        "s_mixn0": pvec(f(inputs["mix_norm"])[0]),
        "s_mixn1": pvec(f(inputs["mix_norm"])[1]),
        "s_finaln": pvec(f(inputs["final_norm"])),
        "s_conv_w": cw(f(inputs["even_conv_w"])[0], 8),
        "s_conv_b": pvec(f(inputs["even_conv_b"])[0]),
        "s_ln_g": pvec(f(inputs["even_ln_g"])[0]),
        "s_ln_b": pvec(f(inputs["even_ln_b"])[0]),
    }
    for L in range(2):
        shared["w_up%d" % L] = wblk(f(inputs["ffn_w_up"])[L], 256)
        shared["w_gate%d" % L] = wblk(f(inputs["ffn_w_gate"])[L], 256)
        shared["w_down%d" % L] = wblk(f(inputs["ffn_w_down"])[L], 128)
        shared["s_ffnn%d" % L] = pvec(f(inputs["ffn_norm"])[L])
        shared["s_fconv_w%d" % L] = cw(f(inputs["ffn_conv_w"])[L], FC)
        shared["s_fconv_b%d" % L] = pvec(f(inputs["ffn_conv_b"])[L])
    maps = []
    for core in range(8):
        b, half = core // 2, core % 2
        win = np.zeros((NW, D), np.float32)
        if half == 1:
            win[:] = x[b]
        else:
            win[NOWN:] = x[b, :NOWN]
        m = dict(shared)
        m["xT"] = np.ascontiguousarray(win[::-1].T)
        m["s_valid"] = np.full((128, 1), float(half), np.float32)
        bv = np.ones((128, 16), np.float32)
        if half == 0:
            bv[:, 8:] = 0.0
        m["s_blkvalid"] = bv
        maps.append(m)
    return maps


def assemble(resB):
    out = np.zeros((4, 4096, D), np.float32)
    for core in range(8):
        b, half = core // 2, core % 2
        oT = resB[core]["outT"]
        o = oT.T[::-1]
        out[b, half * NOWN:(half + 1) * NOWN] = o
    return out


def kernel(**inputs):
    nc = build_fused()
    maps = prep(inputs)
    r = run_bass_kernel_spmd(nc, maps, core_ids=list(range(8)))
    return assemble(r.results)
```

```python
import numpy as np
import ml_dtypes
from contextlib import ExitStack
import concourse.bass as bass
import concourse.mybir as mybir
from concourse.bass_utils import run_bass_kernel_spmd

F32 = mybir.dt.float32
BF16 = mybir.dt.bfloat16
AF = mybir.ActivationFunctionType
ALU = mybir.AluOpType
AX = mybir.AxisListType

D = 2048
KC = 16
NW = 4096
NOWN = 2048
NCR = 2176
NU = 2304
DFF = 5632
FC = 44
HD = 128
RMS_EPS = 1e-6
LN_EPS = 1e-5
CR_TILES = [(0, 512), (512, 512), (1024, 512), (1536, 512), (2048, 128)]
W_TILES = [(i * 512, 512) for i in range(8)]
SBUF_WORDS = 49 * 1024


class Res:
    __slots__ = ("name", "last_w", "readers", "sem", "sem_total", "last_dma")

    def __init__(self, name):
        self.name = name
        self.last_w = None
        self.readers = []
        self.sem = None
        self.sem_total = 0
        self.last_dma = None


class Ins:
    __slots__ = ("eng", "fn", "deps", "is_dma", "res", "sem", "sem_val", "need_sig", "sig_val")

    def __init__(self, eng, fn):
        self.eng = eng
        self.fn = fn
        self.deps = []
        self.is_dma = False
        self.res = None
        self.sem = None
        self.sem_val = 0
        self.need_sig = False
        self.sig_val = 0


class Prog:
    ENG = ["pe", "act", "dve", "pool", "sp"]
    COMPUTE = ["pe", "act", "dve", "pool"]

    def __init__(self, nc, es):
        self.nc = nc
        self.es = es
        self.streams = {e: [] for e in self.ENG}
        self.esem = {e: es.enter_context(nc.semaphore("prog_" + e)) for e in self.COMPUTE}
        self.dma_res = []
        self.sem_pool = []
        self.nsem = 0
        self.big = es.enter_context(nc.sbuf_tensor("bigbuf", [128, SBUF_WORDS], F32))
        self.top = 0
        self.floor = 0
        self.banks = [es.enter_context(nc.psum_tensor("bank%d" % i, [128, 512], F32)) for i in range(8)]
        self.rbanks = [Res("bank%d" % i) for i in range(8)]
        self.nbank = 0
        self.rot = list(range(8))
        self.free_sems = []

    def alloc(self, n, dtype):
        words = (n * (2 if dtype == BF16 else 4) + 3) // 4
        words = (words + 7) // 8 * 8
        a = self.big[:, self.top:self.top + words]
        self.top += words
        assert self.top <= SBUF_WORDS, "SBUF overflow %d" % self.top
        if dtype == BF16:
            return a.bitcast(BF16)[:, 0:n]
        return a[:, 0:n]

    def reset(self):
        self.top = self.floor
        self.rot = list(range(8))

    def bank(self):
        i = self.rot[self.nbank % len(self.rot)]
        self.nbank += 1
        return self.banks[i], self.rbanks[i]

    def fixed(self, i):
        return self.banks[i], self.rbanks[i]

    def res(self, name):
        return Res(name)

    def add(self, eng, fn, reads=(), writes=(), dma=None, ndma=1, extra=()):
        ins = Ins(eng, fn)
        deps = []
        for r in reads:
            if r.last_w is not None:
                deps.append(r.last_w)
        for w in writes:
            lw = w.last_w
            if lw is not None:
                if not (eng == "pe" and lw.eng == "pe" and not lw.is_dma and not w.readers):
                    deps.append(lw)
            for rd in w.readers:
                deps.append(rd)
        deps.extend(extra)
        if dma is not None:
            ins.is_dma = True
            ins.res = dma
            if dma.sem is None:
                if self.sem_pool:
                    dma.sem, dma.sem_total = self.sem_pool.pop(0)
                else:
                    dma.sem = self.es.enter_context(self.nc.semaphore("d%d" % self.nsem))
                    dma.sem_total = 0
                    self.nsem += 1
                self.dma_res.append(dma)
            if dma.last_dma is not None:
                deps.append(dma.last_dma)
            dma.sem_total += 16 * ndma
            ins.sem = dma.sem
            ins.sem_val = dma.sem_total
            dma.last_dma = ins
        for r in reads:
            r.readers.append(ins)
        for w in writes:
            w.last_w = ins
            w.readers = []
        seen = set()
        for d in deps:
            if d is ins or id(d) in seen:
                continue
            seen.add(id(d))
            ins.deps.append(d)
        self.streams[eng].append(ins)
        return ins

    def barrier(self):
        lasts = []
        for e in self.ENG:
            for ins in reversed(self.streams[e]):
                if not ins.is_dma and ins.fn is not None:
                    lasts.append(ins)
                    break
        dmas = [r.last_dma for r in self.dma_res if r.last_dma is not None]
        for e in self.ENG:
            self.add(e, None, extra=lasts + dmas)
        for r in self.dma_res:
            self.sem_pool.append((r.sem, r.sem_total))
            r.sem = None
            r.last_dma = None
        self.dma_res = []

    def mm(self, out, lhsT, rhs, start, stop, reads, writes):
        return self.add("pe", lambda e: e.matmul(out, lhsT=lhsT, rhs=rhs, start=start, stop=stop), reads, writes)

    def tr(self, out, in_, ident, reads, writes):
        return self.add("pe", lambda e: e.transpose(out, in_, ident), reads, writes)

    def act(self, out, in_, func, reads, writes, scale=1.0, bias=None, accum_out=None):
        def f(e):
            kw = {}
            if bias is not None:
                kw["bias"] = bias
            if accum_out is not None:
                kw["accum_out"] = accum_out
            return e.activation(out=out, in_=in_, func=func, scale=scale, **kw)
        return self.add("act", f, reads, writes)

    def tt(self, eng, out, in0, in1, op, reads, writes):
        return self.add(eng, lambda e: e.tensor_tensor(out=out, in0=in0, in1=in1, op=op), reads, writes)

    def ts(self, eng, out, in0, s1, s2, op0, op1, reads, writes):
        if op1 is None:
            return self.add(eng, lambda e: e.tensor_scalar(out=out, in0=in0, scalar1=s1, scalar2=None, op0=op0), reads, writes)
        return self.add(eng, lambda e: e.tensor_scalar(out=out, in0=in0, scalar1=s1, scalar2=s2, op0=op0, op1=op1), reads, writes)

    def stt(self, out, in0, scalar, in1, op0, op1, reads, writes):
        return self.add("dve", lambda e: e.scalar_tensor_tensor(out=out, in0=in0, scalar=scalar, in1=in1, op0=op0, op1=op1), reads, writes)

    def copy(self, eng, out, in_, reads, writes):
        if eng == "act":
            return self.add("act", lambda e: e.activation(out=out, in_=in_, func=AF.Copy), reads, writes)
        return self.add(eng, lambda e: e.tensor_copy(out=out, in_=in_), reads, writes)

    def memset(self, eng, ap, val, writes):
        return self.add(eng, lambda e: e.memset(ap, val), (), writes)

    def dma(self, q, out, in_, reads, writes, res, **kw):
        return self.add(q, lambda e: e.dma_start(out=out, in_=in_, **kw), reads, writes, dma=res)

    def _prepare(self):
        for e in self.ENG:
            for ins in self.streams[e]:
                for d in ins.deps:
                    if not d.is_dma:
                        d.need_sig = True
        for e in self.ENG:
            c = 0
            for ins in self.streams[e]:
                if not ins.is_dma and ins.need_sig:
                    assert ins.fn is not None
                    c += 1
                    ins.sig_val = c

    def _emit_one(self, e, eng):
        waited = {}
        for ins in self.streams[e]:
            need = {}
            for d in ins.deps:
                if d.is_dma:
                    key = ("d", id(d.sem))
                    sem = d.sem
                    val = d.sem_val
                else:
                    key = ("e", d.eng)
                    sem = self.esem[d.eng]
                    val = d.sig_val
                if waited.get(key, 0) >= val:
                    continue
                if key not in need or need[key][1] < val:
                    need[key] = (sem, val)
            for key, (sem, val) in need.items():
                eng.wait_ge(sem, val)
                waited[key] = val
            if ins.fn is None:
                continue
            r = ins.fn(eng)
            if ins.is_dma:
                r.then_inc(ins.sem, 16)
            elif ins.need_sig:
                r.then_inc(self.esem[ins.eng], 1)


def build_program(nc, body):
    with ExitStack() as es:
        p = Prog(nc, es)
        body(p)
        p.barrier()
        p._prepare()
        block = es.enter_context(nc.Block())

        def sect(name):
            def f(eng):
                p._emit_one(name, eng)
            return f

        block.tensor(sect("pe"))
        block.scalar(sect("act"))
        block.vector(sect("dve"))
        block.gpsimd(sect("pool"))
        block.sync(sect("sp"))
    return nc


class Consts:
    pass


def setup_consts(p, small_dram):
    c = Consts()
    c.r = Res("consts")
    c.ones_bf = p.alloc(128, BF16)
    c.ones_f = p.alloc(128, F32)
    c.ident_bf = p.alloc(128, BF16)
    c.zeros = p.alloc(512, F32)
    c.causal_ge = p.alloc(128, F32)
    c.eps_rms = p.alloc(1, F32)
    c.eps_ln = p.alloc(1, F32)
    identf = p.alloc(128, F32)
    c.ffn_halo = p.alloc(FC * 2, F32).rearrange("p (j k) -> p j k", j=FC)
    c.r_halo = Res("ffn_halo")
    p.memset("pool", c.ones_bf, 1.0, [c.r])
    p.memset("pool", c.ones_f, 1.0, [c.r])
    p.memset("pool", c.zeros, 0.0, [c.r])
    p.memset("pool", c.eps_rms, RMS_EPS, [c.r])
    p.memset("pool", c.eps_ln, LN_EPS, [c.r])
    p.memset("pool", c.causal_ge, 0.0, [c.r])
    p.add("pool", lambda e: e.affine_select(out=c.causal_ge, in_=c.causal_ge, pattern=[[1, 128]], base=0,
                                            channel_multiplier=-1, compare_op=ALU.is_ge, fill=-1e5), [c.r], [c.r])
    p.memset("pool", identf, 0.0, [c.r])
    p.add("pool", lambda e: e.affine_select(out=identf, in_=identf, pattern=[[1, 128]], base=0,
                                            channel_multiplier=-1, compare_op=ALU.not_equal, fill=1.0), [c.r], [c.r])
    p.copy("pool", c.ident_bf, identf, [c.r], [c.r])
    c.small = {}
    for name, ap in small_dram.items():
        n = ap.shape[1]
        t = p.alloc(n, F32)
        p.dma("sp", t, ap, [], [c.r], c.r)
        c.small[name] = t
    p.floor = p.top
    return c


def norm_bufs(p, nt=256):
    xt = [p.alloc(KC * nt, F32).rearrange("p (k t) -> p k t", k=KC) for _ in range(2)]
    sq = [p.alloc(KC * nt, BF16).rearrange("p (k t) -> p k t", k=KC) for _ in range(2)]
    rs = [p.alloc(nt, F32) for _ in range(2)]
    r_xt = [Res("n_xt%d" % i) for i in range(2)]
    r_sq = [Res("n_sq%d" % i) for i in range(2)]
    r_rs = [Res("n_rs%d" % i) for i in range(2)]
    return xt, sq, rs, r_xt, r_sq, r_rs, nt


def norm_phase(p, c, x_dram, col0, ntok, g, hT, rh, hcol0, bufs=None):
    xv = x_dram.rearrange("(k p) t -> p k t", p=128)
    xt, sq, rs, r_xt, r_sq, r_rs, NT = bufs if bufs is not None else norm_bufs(p)
    assert ntok % 128 == 0
    t = 0
    it = 0
    while t < ntok:
        n = min(NT, ntok - t)
        s = it % 2
        it += 1
        p.dma("sp", xt[s][:, :, 0:n], xv[:, :, col0 + t:col0 + t + n], [], [r_xt[s]], r_xt[s])
        p.act(sq[s][:, :, 0:n], xt[s][:, :, 0:n], AF.Square, [r_xt[s]], [r_sq[s]])
        ps, rps = p.bank()
        for k in range(KC):
            p.mm(ps[:, 0:n], c.ones_bf, sq[s][:, k, 0:n], k == 0, k == KC - 1, [r_sq[s], c.r], [rps])
        p.act(rs[s][:, 0:n], ps[:, 0:n], AF.Sqrt, [rps, c.r], [r_rs[s]], scale=1.0 / D, bias=c.eps_rms)
        p.add("dve", lambda e, o=rs[s][:, 0:n]: e.reciprocal(out=o, in_=o), [r_rs[s]], [r_rs[s]])
        hc = hcol0 + t
        rr = rh[hc // 256]
        for k in range(KC):
            p.stt(hT[:, k, hc:hc + n], xt[s][:, k, 0:n], g[:, k:k + 1], rs[s][:, 0:n], ALU.mult, ALU.mult,
                  [r_xt[s], r_rs[s], c.r], [rr])
        t += n


def rh_reads(rh, c0, n):
    return [rh[i] for i in range(c0 // 256, (c0 + n - 1) // 256 + 1)]


class WStream:
    def __init__(self, p, nslots, kc, cb, name):
        self.p = p
        self.kc = kc
        self.cb = cb
        self.slots = [p.alloc(kc * cb, BF16) for _ in range(nslots)]
        self.res = [Res("%s_w%d" % (name, i)) for i in range(nslots)]
        self.n = 0

    def load(self, blk_ap):
        s = self.n % len(self.slots)
        self.n += 1
        self.p.dma("pool", self.slots[s], blk_ap, [], [self.res[s]], self.res[s], max_dma_last_dim=8192)
        return self.slots[s].rearrange("p (k f) -> p k f", k=self.kc), self.res[s]


def linear_fm(p, ws, blocks, hT, rh, tok_tiles, consume, tok_outer=False):
    nfo = ws.cb // 128
    pending = None
    loaded = []
    for bi, (bap, tag) in enumerate(blocks):
        if bi == 0:
            loaded.append(ws.load(bap))
        if bi + 1 < len(blocks):
            loaded.append(ws.load(blocks[bi + 1][0]))
        wv, rw = loaded[bi]
        order = ([(ti, fo) for ti in range(len(tok_tiles)) for fo in range(nfo)] if tok_outer
                 else [(ti, fo) for fo in range(nfo) for ti in range(len(tok_tiles))])
        for ti, fo in order:
            c0, n = tok_tiles[ti]
            ps, rps = p.bank()
            for k in range(ws.kc):
                rr = rh(k, c0, n) if callable(rh) else rh_reads(rh, c0, n)
                p.mm(ps[:, 0:n], wv[:, k, fo * 128:(fo + 1) * 128], hT[:, k, c0:c0 + n], k == 0, k == ws.kc - 1,
                     [rw] + rr, [rps])
            consume(tag, fo, ti, c0, n, ps, rps)


def linear_tm(p, ws, blocks, hT, rh, tok0, ntok, consume):
    loaded = []
    for bi, (bap, tag) in enumerate(blocks):
        if bi == 0:
            loaded.append(ws.load(bap))
        if bi + 1 < len(blocks):
            loaded.append(ws.load(blocks[bi + 1][0]))
        wv, rw = loaded[bi]
        for c in range(tok0, tok0 + ntok, 128):
            ps, rps = p.bank()
            for k in range(ws.kc):
                p.mm(ps[:, 0:ws.cb], hT[:, k, c:c + 128], wv[:, k, :], k == 0, k == ws.kc - 1,
                     [rw] + rh_reads(rh, c, 128), [rps])
            consume(tag, c, ps, rps)


class Stage:
    def __init__(self, p, nslots, n, dtype, name):
        self.t = [p.alloc(n, dtype) for _ in range(nslots)]
        self.r = [Res("%s%d" % (name, i)) for i in range(nslots)]
        self.i = 0

    def next(self):
        s = self.i % len(self.t)
        self.i += 1
        return self.t[s], self.r[s]


def inproj_phase(p, c, d):
    p.reset()
    NCH = 2048
    hT = p.alloc(KC * NCH, BF16).rearrange("p (k t) -> p k t", k=KC)
    rh = {i: Res("hT%d" % i) for i in range(NCH // 256)}
    ws = WStream(p, 2, KC, 512, "inp")
    st_bf = Stage(p, 4, 512, BF16, "st_bf")
    st_sg = Stage(p, 2, 512, F32, "st_sg")
    st_s = Stage(p, 2, 512, F32, "st_s")
    g = c.small["mixn0"]
    w = d["w_in"]
    tiles = [(0, 512), (512, 512), (1024, 512), (1536, 512)]
    nb_ = norm_bufs(p)
    rz = Res("zpad")
    for jj in range(8):
        p.dma("sp", d["sT"][jj * 128:(jj + 1) * 128, NW:NW + 32], c.zeros[:, 0:32], [c.r], [], rz)
    for base in (0, NCH):
        def cons_q(tag, fo, ti, c0, n, ps, rps, base=base):
            t, r = st_bf.next()
            p.copy("act", t[:, 0:n], ps[:, 0:n], [rps], [r])
            f0 = (tag[1] * 4 + fo) * 128
            dst = d["qT"] if tag[0] == "q" else d["kT"]
            p.dma("sp", dst[f0:f0 + 128, base + c0:base + c0 + n], t[:, 0:n], [r], [], r)

        def cons_v(tag, cc, ps, rps, base=base):
            t, r = st_bf.next()
            p.copy("dve", t[:, 0:512], ps[:, 0:512], [rps], [r])
            f0 = tag[1] * 512
            p.dma("sp", d["V"][base + cc:base + cc + 128, f0:f0 + 512], t[:, 0:512], [r], [], r)

        glu = {}

        def cons_u(tag, fo, ti, c0, n, ps, rps, base=base, glu=glu):
            glu[fo] = (ps, rps)
            if fo == 3:
                for j in range(2):
                    pa, ra = glu[j]
                    pg, rg = glu[2 + j]
                    sg, rsg = st_sg.next()
                    p.act(sg[:, 0:n], pg[:, 0:n], AF.Sigmoid, [rg], [rsg])
                    s, rs_ = st_s.next()
                    p.tt("dve", s[:, 0:n], pa[:, 0:n], sg[:, 0:n], ALU.mult, [ra, rsg], [rs_])
                    f0 = (tag[1] * 2 + j) * 128
                    p.dma("sp", d["sT"][f0:f0 + 128, base + c0:base + c0 + n], s[:, 0:n], [rs_], [], rs_)

        norm_phase(p, c, d["xT"], base, NCH, g, hT, rh, 0, nb_)
        linear_fm(p, ws, [(w[i], ("u", i)) for i in range(4)], hT, rh, tiles, cons_u, tok_outer=True)
        linear_fm(p, ws, [(w[4 + i], ("q", i)) for i in range(2)], hT, rh, tiles, cons_q)
        linear_fm(p, ws, [(w[6 + i], ("k", i)) for i in range(2)], hT, rh, tiles, cons_q)
        linear_tm(p, ws, [(w[8 + i], ("v", i)) for i in range(2)], hT, rh, 0, NCH, cons_v)
    p.barrier()


def conformer_gen(p, c, d):
    W = 31
    sb = [p.alloc(512 + 32, F32) for _ in range(3)]
    r_sb = [Res("cf_s%d" % i) for i in range(3)]
    h = p.alloc(8 * 512, F32).rearrange("p (j t) -> p j t", j=8)
    sq = p.alloc(8 * 512, F32).rearrange("p (j t) -> p j t", j=8)
    r_h = [Res("cf_h%d" % j) for j in range(8)]
    r_sq = [Res("cf_sq%d" % j) for j in range(8)]
    mean = p.alloc(512, F32)
    msq = p.alloc(512, F32)
    rstd = p.alloc(512, F32)
    r_mean, r_msq, r_rstd = Res("cf_mean"), Res("cf_msq"), Res("cf_rstd")
    tmp = [p.alloc(512, F32) for _ in range(2)]
    r_tmp = [Res("cf_tmp%d" % i) for i in range(2)]
    yst = Stage(p, 2, 512, BF16, "cf_y")
    cw, cb, lg, lb = c.small["conv_w"], c.small["conv_b"], c.small["ln_g"], c.small["ln_b"]
    cwv = cw.rearrange("p (j k) -> p j k", j=8)
    ld = 0
    for (c0, n) in W_TILES:
        for j in range(8):
            s = ld % 3
            ld += 1
            p.dma("sp", sb[s][:, 0:n + 30], d["sT"][j * 128:(j + 1) * 128, c0:c0 + n + 30], [], [r_sb[s]], r_sb[s])
            hj = h[:, j, 0:n]
            p.ts("dve", hj, sb[s][:, 30:30 + n], cwv[:, j, 0:1], cb[:, j:j + 1], ALU.mult, ALU.add,
                 [r_sb[s], c.r], [r_h[j]])
            for k in range(1, W):
                p.stt(hj, sb[s][:, 30 - k:30 - k + n], cwv[:, j, k:k + 1], hj, ALU.mult, ALU.add,
                      [r_sb[s], r_h[j], c.r], [r_h[j]])
                if k % 2 == 0:
                    yield
            p.act(sq[:, j, 0:n], hj, AF.Square, [r_h[j]], [r_sq[j]])
        ps1, rp1 = p.bank()
        for j in range(8):
            p.mm(ps1[:, 0:n], c.ones_f, h[:, j, 0:n], j == 0, j == 7, [r_h[j], c.r], [rp1])
        ps2, rp2 = p.bank()
        for j in range(8):
            p.mm(ps2[:, 0:n], c.ones_f, sq[:, j, 0:n], j == 0, j == 7, [r_sq[j], c.r], [rp2])
        p.add("act", lambda e, o=mean[:, 0:n], i=ps1[:, 0:n]: e.activation(out=o, in_=i, func=AF.Copy, scale=1.0 / 1024),
              [rp1], [r_mean])
        p.tt("dve", msq[:, 0:n], mean[:, 0:n], mean[:, 0:n], ALU.mult, [r_mean], [r_msq])
        p.stt(rstd[:, 0:n], ps2[:, 0:n], 1.0 / 1024, msq[:, 0:n], ALU.mult, ALU.subtract, [rp2, r_msq], [r_rstd])
        p.act(rstd[:, 0:n], rstd[:, 0:n], AF.Sqrt, [r_rstd, c.r], [r_rstd], bias=c.eps_ln)
        p.add("dve", lambda e, o=rstd[:, 0:n]: e.reciprocal(out=o, in_=o), [r_rstd], [r_rstd])
        yield
        for j in range(8):
            if j % 2 == 0:
                yield
            t = tmp[j % 2][:, 0:n]
            rt = r_tmp[j % 2]
            p.tt("dve", t, h[:, j, 0:n], mean[:, 0:n], ALU.subtract, [r_h[j], r_mean], [rt])
            p.tt("dve", t, t, rstd[:, 0:n], ALU.mult, [rt, r_rstd], [rt])
            y, ry = yst.next()
            p.act(y[:, 0:n], t, AF.Silu, [rt, c.r], [ry], scale=lg[:, j:j + 1], bias=lb[:, j:j + 1])
            p.dma("sp", d["yT"][j * 128:(j + 1) * 128, c0:c0 + n], y[:, 0:n], [ry], [], ry)
    yield


def run_interleaved(items, S, mk, stagger=True, side=None):
    free = list(range(S))
    active = []
    it = iter(items)
    done = False
    sweeps = 0
    while True:
        while free and not done and (not stagger or not active or sweeps >= 2):
            sweeps = 0
            x = next(it, None)
            if x is None:
                done = True
                break
            sl = free.pop(0)
            active.append((mk(x, sl), sl))
            if stagger and free and not done:
                break
        if not active:
            break
        sweeps += 1
        if side is not None and side[0] is not None:
            try:
                next(side[0])
            except StopIteration:
                side[0] = None
        for g, sl in list(active):
            try:
                next(g)
            except StopIteration:
                active.remove((g, sl))
                free.append(sl)


NSTREAM = 3


def moba_phase(p, c, d, with_conformer=True):
    p.reset()
    S = 3
    side = [conformer_gen(p, c, d)] if with_conformer else [None]
    if with_conformer:
        next(side[0])
    p.rot = list(range(8 - S))
    BIG = 1.0e30
    NEGB = 3.0e4
    scale = HD ** -0.5
    NQT = NW // 128
    qT = [p.alloc(NW, BF16) for _ in range(2)]
    kT = [p.alloc(NW, BF16) for _ in range(2)]
    Vh = [p.alloc(32 * 128, BF16).rearrange("p (s d) -> p s d", s=32) for _ in range(2)]
    r_q = [Res("mb_q%d" % i) for i in range(2)]
    r_k = [Res("mb_k%d" % i) for i in range(2)]
    r_v = [Res("mb_v%d" % i) for i in range(2)]
    kmf = [p.alloc(16, F32) for _ in range(2)]
    kmb = [p.alloc(16, BF16) for _ in range(2)]
    r_km = [Res("mb_km%d" % i) for i in range(2)]
    vbias = p.alloc(16, F32)
    r_vb = Res("mb_vb")
    bv = c.small["blkvalid"]
    p.ts("dve", vbias, bv, -1.0, BIG, ALU.add, ALU.mult, [c.r], [r_vb])
    yst = [p.alloc(NW, BF16) for _ in range(2)]
    r_y = [Res("mb_y%d" % i) for i in range(2)]
    streams = []
    for si in range(S):
        st = dict(
            gm=p.alloc(16, F32), top8=p.alloc(8, F32), sel=p.alloc(16, F32), r_g=Res("mb_g%d" % si),
            pexp=[p.alloc(512, BF16) for _ in range(2)], r_pe=[Res("mb_p%d_%d" % (si, i)) for i in range(2)],
            dtmp=p.alloc(128, F32), r_dt=Res("mb_dt%d" % si),
            pT=[p.alloc(512, BF16) for _ in range(2)], r_pT=[Res("mb_pT%d_%d" % (si, i)) for i in range(2)],
            rsum=p.alloc(24, F32), r_rs=Res("mb_rs%d" % si), rinv=p.alloc(1, F32),
            ob=p.alloc(128, BF16), r_ob=Res("mb_ob%d" % si), acc=8 - S + si, n=0)
        streams.append(st)

    def qgen(h, hs, qt, st):
        i0 = qt * 128
        nb = i0 // 256
        qtile = qT[hs][:, i0:i0 + 128]
        gm, top8, sel, r_g = st["gm"], st["top8"], st["sel"], st["r_g"]
        npast = 15 - nb
        p.memset("pool", gm, -BIG, [r_g])
        if npast > 0:
            psg, rpg = p.bank()
            p.mm(psg[:, 0:16], qtile, kmb[hs], True, True, [r_q[hs], r_km[hs]], [rpg])
            p.tt("dve", gm[:, nb + 1:16], psg[:, nb + 1:16], vbias[:, nb + 1:16], ALU.add, [rpg, r_vb], [r_g])
        p.add("dve", lambda e, o=top8, i=gm: e.max(out=o, in_=i), [r_g], [r_g])
        p.ts("dve", sel, gm, top8[:, 2:3], None, ALU.is_ge, None, [r_g], [r_g])
        p.tt("dve", sel, sel, bv, ALU.mult, [r_g, c.r], [r_g])
        p.ts("dve", sel, sel, -1.0, NEGB, ALU.add, ALU.mult, [r_g], [r_g])
        yield
        segs = [(i0, 128, "diag")]
        k = i0 + 128
        if qt % 2 == 0:
            segs.append((k, 128, "own"))
            k += 128
        while k < NW:
            segs.append((k, 256, k // 256))
            k += 256
        tiles = []
        cur = []
        curn = 0
        for sg in segs:
            if curn + sg[1] > 512:
                tiles.append(cur)
                cur = []
                curn = 0
            cur.append(sg)
            curn += sg[1]
        tiles.append(cur)
        po, rpo = p.fixed(st["acc"])
        nsub_total = sum(s_[1] for s_ in segs) // 128
        sub_done = 0
        rcol = 0
        rsm, rrs = st["rsum"], st["r_rs"]
        for tl in tiles:
            k0 = tl[0][0]
            nk = sum(s_[1] for s_ in tl)
            pz, rpz = p.bank()
            p.mm(pz[:, 0:nk], qtile, kT[hs][:, k0:k0 + nk], True, True, [r_q[hs], r_k[hs]], [rpz])
            px = st["n"] % 2
            st["n"] += 1
            pexp, r_pe = st["pexp"][px], st["r_pe"][px]
            pT, r_pT = st["pT"][px], st["r_pT"][px]
            yield
            col = 0
            for (ks, kn, kind) in tl:
                if kind == "diag":
                    dt_ = st["dtmp"]
                    p.tt("dve", dt_, pz[:, col:col + 128], c.causal_ge, ALU.add, [rpz, c.r], [st["r_dt"]])
                    p.act(pexp[:, col:col + 128], dt_, AF.Exp, [st["r_dt"]], [r_pe, rrs], scale=scale,
                          accum_out=rsm[:, rcol:rcol + 1])
                elif kind == "own":
                    p.act(pexp[:, col:col + kn], pz[:, col:col + kn], AF.Exp, [rpz], [r_pe, rrs], scale=scale,
                          accum_out=rsm[:, rcol:rcol + 1])
                else:
                    p.act(pexp[:, col:col + kn], pz[:, col:col + kn], AF.Exp, [rpz, r_g], [r_pe, rrs],
                          scale=scale, bias=sel[:, kind:kind + 1], accum_out=rsm[:, rcol:rcol + 1])
                rcol += 1
                col += kn
            yield
            pt_ps, rpt = p.bank()
            ptv = pt_ps[:, 0:256].bitcast(BF16)
            nsub = nk // 128
            for j in range(nsub):
                p.tr(ptv[:, j * 128:(j + 1) * 128], pexp[:, j * 128:(j + 1) * 128], c.ident_bf, [r_pe, c.r], [rpt])
            yield
            p.copy("dve" if (st["n"] % 2) else "act", pT[:, 0:nk], ptv[:, 0:nk], [rpt], [r_pT])
            yield
            for j in range(nsub):
                sbi = (k0 // 128) + j
                p.mm(po[:, 0:128], pT[:, j * 128:(j + 1) * 128], Vh[hs][:, sbi, :], sub_done == 0,
                     sub_done == nsub_total - 1, [r_pT, r_v[hs]], [rpo])
                sub_done += 1
            yield
        rinv, ob, r_ob = st["rinv"], st["ob"], st["r_ob"]
        p.add("dve", lambda e, o=rinv, i=rsm[:, 0:rcol]: e.tensor_reduce(out=o, in_=i, axis=AX.X, op=ALU.add),
              [rrs], [rrs])
        p.add("dve", lambda e, o=rinv: e.reciprocal(out=o, in_=o), [rrs], [rrs])
        p.ts("dve", ob, po[:, 0:128], rinv[:, 0:1], None, ALU.mult, None, [rpo, rrs], [r_ob])
        pt2, rpt2 = p.bank()
        pt2v = pt2[:, 0:64].bitcast(BF16)
        p.tr(pt2v, ob, c.ident_bf, [r_ob, c.r], [rpt2])
        p.copy("act", yst[hs][:, i0:i0 + 128], pt2v, [rpt2], [r_y[hs]])
        yield

    for h in range(8):
        hs = h % 2
        p.dma("sp", qT[hs], d["qT"][h * 128:(h + 1) * 128, 0:NW], [], [r_q[hs]], r_q[hs])
        p.dma("sp", kT[hs], d["kT"][h * 128:(h + 1) * 128, :], [], [r_k[hs]], r_k[hs])
        p.dma("sp", Vh[hs], d["V"][:, h * 128:(h + 1) * 128].rearrange("(s q) e -> q s e", q=128), [], [r_v[hs]], r_v[hs])
        p.add("dve", lambda e, o=kmf[hs], i=kT[hs].rearrange("p (n s) -> p n s", n=16): e.tensor_reduce(out=o, in_=i, axis=AX.X, op=ALU.add),
              [r_k[hs]], [r_km[hs]])
        p.ts("dve", kmb[hs], kmf[hs], 1.0 / 256, None, ALU.mult, None, [r_km[hs]], [r_km[hs]])
        run_interleaved(range(NQT), S, lambda qt, sl, h=h, hs=hs: qgen(h, hs, qt, streams[sl]), side=side)
        p.dma("sp", d["yT"][1024 + h * 128:1024 + (h + 1) * 128, 0:NW], yst[hs], [r_y[hs]], [], r_y[hs])
    while side[0] is not None:
        try:
            next(side[0])
        except StopIteration:
            side[0] = None
    p.barrier()


def load_hT_from_dram(p, src, nrow_chunks, t0, ntok, name):
    hT = p.alloc(nrow_chunks * ntok, BF16).rearrange("p (k t) -> p k t", k=nrow_chunks)
    rk = [Res("%s%d" % (name, k)) for k in range(nrow_chunks)]
    for k in range(nrow_chunks):
        p.dma("sp", hT[:, k, :], src[k * 128:(k + 1) * 128, t0:t0 + ntok], [], [rk[k]], rk[k])
    return hT, (lambda k, c0, n: [rk[k]])


def outproj_phase(p, c, d, yname, wname, xin, xout, chunks):
    for (t0, tiles) in chunks:
        p.reset()
        ntok = sum(n for _, n in tiles)
        hT, rh = load_hT_from_dram(p, d[yname], KC, t0, ntok, "op_y")
        ws = WStream(p, 2, KC, 512, "op")
        xst = Stage(p, 3, 512, F32, "op_x")
        w = d[wname]

        def cons(tag, fo, ti, c0, n, ps, rps, t0=t0, xst=xst):
            f0 = (tag * 4 + fo) * 128
            t, r = xst.next()
            p.dma("sp", t[:, 0:n], d[xin][f0:f0 + 128, t0 + c0:t0 + c0 + n], [], [r], r)
            p.tt("dve", t[:, 0:n], ps[:, 0:n], t[:, 0:n], ALU.add, [rps, r], [r])
            p.dma("sp", d[xout][f0:f0 + 128, t0 + c0:t0 + c0 + n], t[:, 0:n], [r], [], r)

        linear_fm(p, ws, [(w[i], i) for i in range(4)], hT, rh, tiles, cons)
        p.barrier()


def ffn_phase(p, c, d, L, xin, xout, chunks):
    g = c.small["ffnn%d" % L]
    fw, fb = c.small["fconv_w%d" % L], c.small["fconv_b%d" % L]
    fwv = fw.rearrange("p (j k) -> p j k", j=FC)
    valid = c.small["valid"]
    wu, wg = d["w_up%d" % L], d["w_gate%d" % L]
    nblk = DFF // 256
    for (t0, tiles, halo, halves) in chunks:
        p.reset()
        ntok = sum(n for _, n in tiles)
        hT = p.alloc(KC * ntok, BF16).rearrange("p (k t) -> p k t", k=KC)
        rh = {i: Res("ff_h%d" % i) for i in range((ntok + 255) // 256)}
        norm_phase(p, c, d[xin], t0, ntok, g, hT, rh, 0, norm_bufs(p, 128))
        wsu = WStream(p, 2, KC, 256, "ffu")
        wsg = WStream(p, 2, KC, 256, "ffg")
        NB = ntok + 2
        upb = [p.alloc(NB, F32) for _ in range(2)]
        gb = [p.alloc(ntok, F32) for _ in range(2)]
        u = [p.alloc(ntok, F32) for _ in range(2)]
        ab = [p.alloc(ntok, BF16) for _ in range(2)]
        r_up = [Res("ff_up%d" % i) for i in range(2)]
        r_gb = [Res("ff_gb%d" % i) for i in range(2)]
        r_u = [Res("ff_u%d" % i) for i in range(2)]
        r_ab = [Res("ff_ab%d" % i) for i in range(2)]
        lu = [wsu.load(wu[0])]
        lg_ = [wsg.load(wg[0])]
        for b in range(nblk):
            if b + 1 < nblk:
                lu.append(wsu.load(wu[b + 1]))
                lg_.append(wsg.load(wg[b + 1]))
            for fo in range(2):
                j = b * 2 + fo
                s = j % 2
                wv, rw = lu[b]
                for (c0, n) in tiles:
                    ps, rps = p.bank()
                    for k in range(KC):
                        p.mm(ps[:, 0:n], wv[:, k, fo * 128:(fo + 1) * 128], hT[:, k, c0:c0 + n], k == 0, k == KC - 1,
                             [rw] + rh_reads(rh, c0, n), [rps])
                    p.copy("act", upb[s][:, c0:c0 + n], ps[:, 0:n], [rps], [r_up[s]])
                wv, rw = lg_[b]
                for (c0, n) in tiles:
                    ps, rps = p.bank()
                    for k in range(KC):
                        p.mm(ps[:, 0:n], wv[:, k, fo * 128:(fo + 1) * 128], hT[:, k, c0:c0 + n], k == 0, k == KC - 1,
                             [rw] + rh_reads(rh, c0, n), [rps])
                    p.copy("act", gb[s][:, c0:c0 + n], ps[:, 0:n], [rps], [r_gb[s]])
                if halo == "cr":
                    p.ts("dve", upb[s][:, NOWN:NOWN + 2], upb[s][:, NOWN:NOWN + 2], valid[:, 0:1], None, ALU.mult, None,
                         [r_up[s], c.r], [r_up[s]])
                    p.memset("pool", upb[s][:, ntok:NB], 0.0, [r_up[s]])
                elif halo == "save":
                    p.memset("pool", upb[s][:, ntok:NB], 0.0, [r_up[s]])
                    p.copy("pool", c.ffn_halo[:, j, :], upb[s][:, 0:2], [r_up[s]], [c.r_halo])
                else:
                    p.ts("dve", upb[s][:, ntok:NB], c.ffn_halo[:, j, :], valid[:, 0:1], None, ALU.mult, None,
                         [c.r_halo, c.r], [r_up[s]])
                us = u[s]
                p.ts("dve", us, upb[s][:, 0:ntok], fwv[:, j, 2:3], fb[:, j:j + 1], ALU.mult, ALU.add, [r_up[s], c.r], [r_u[s]])
                p.stt(us, upb[s][:, 1:ntok + 1], fwv[:, j, 1:2], us, ALU.mult, ALU.add, [r_up[s], r_u[s], c.r], [r_u[s]])
                p.stt(us, upb[s][:, 2:ntok + 2], fwv[:, j, 0:1], us, ALU.mult, ALU.add, [r_up[s], r_u[s], c.r], [r_u[s]])
                p.act(us, us, AF.Silu, [r_u[s]], [r_u[s]])
                p.tt("pool", ab[s], us, gb[s], ALU.mult, [r_u[s], r_gb[s]], [r_ab[s]])
                p.dma("sp", d["actT"][j * 128:(j + 1) * 128, t0:t0 + ntok], ab[s], [r_ab[s]], [], r_ab[s])
        p.barrier()
        for hv in halves:
            p.reset()
            h0 = hv[0][0]
            nt = sum(n for _, n in hv)
            aT = p.alloc(FC * nt, BF16).rearrange("p (k t) -> p k t", k=FC)
            rk = [Res("fd_a%d" % k) for k in range(FC)]
            for k in range(FC):
                p.dma("sp", aT[:, k, :], d["actT"][k * 128:(k + 1) * 128, t0 + h0:t0 + h0 + nt], [], [rk[k]], rk[k])
            ra = (lambda k, c0, n, rk=rk: [rk[k]])
            ws = WStream(p, 2, FC, 128, "ffd")
            xst = Stage(p, 3, 512, F32, "fd_x")
            wd = d["w_down%d" % L]
            rel_tiles = [(c0 - h0, n) for (c0, n) in hv]

            def cons(tag, fo, ti, c0, n, ps, rps, tb=t0 + h0, xst=xst):
                f0 = tag * 128
                t, r = xst.next()
                p.dma("sp", t[:, 0:n], d[xin][f0:f0 + 128, tb + c0:tb + c0 + n], [], [r], r)
                p.tt("dve", t[:, 0:n], ps[:, 0:n], t[:, 0:n], ALU.add, [rps, r], [r])
                p.dma("sp", d[xout][f0:f0 + 128, tb + c0:tb + c0 + n], t[:, 0:n], [r], [], r)

            linear_fm(p, ws, [(wd[i], i) for i in range(16)], aT, ra, rel_tiles, cons)
            p.barrier()


def qkv_phase(p, c, d):
    p.reset()
    hT = p.alloc(KC * NCR, BF16).rearrange("p (k t) -> p k t", k=KC)
    rh = {i: Res("qk_h%d" % i) for i in range((NCR + 255) // 256)}
    nb_ = norm_bufs(p)
    ws = WStream(p, 2, KC, 512, "qkv")
    st_bf = Stage(p, 4, 512, BF16, "qk_st")
    w = d["w_qkv"]
    valid = c.small["valid"]

    def mk(dst, base):
        def cons(tag, fo, ti, c0, n, ps, rps):
            t, r = st_bf.next()
            p.copy("act", t[:, 0:n], ps[:, 0:n], [rps], [r])
            f0 = (tag * 4 + fo) * 128
            p.dma("sp", d[dst][f0:f0 + 128, base + c0:base + c0 + n], t[:, 0:n], [r], [], r)
        return cons

    def mk_v(base):
        def cons_v(tag, cc, ps, rps):
            t, r = st_bf.next()
            if base + cc >= NOWN:
                p.ts("dve", t[:, 0:512], ps[:, 0:512], valid[:, 0:1], None, ALU.mult, None, [rps, c.r], [r])
            else:
                p.copy("dve", t[:, 0:512], ps[:, 0:512], [rps], [r])
            p.dma("sp", d["V1"][base + cc:base + cc + 128, tag * 512:(tag + 1) * 512], t[:, 0:512], [r], [], r)
        return cons_v

    norm_phase(p, c, d["x2T"], 0, NCR, c.small["mixn1"], hT, rh, 0, nb_)
    linear_fm(p, ws, [(w[i], i) for i in range(4)], hT, rh, CR_TILES, mk("q1T", 0))
    linear_fm(p, ws, [(w[4 + i], i) for i in range(4)], hT, rh, CR_TILES, mk("k1T", 0))
    linear_tm(p, ws, [(w[8 + i], i) for i in range(4)], hT, rh, 0, NCR, mk_v(0))
    nrest = NW - NCR
    norm_phase(p, c, d["x2T"], NCR, nrest, c.small["mixn1"], hT, rh, 0, nb_)
    rest_tiles = [(0, 512), (512, 512), (1024, 512), (1536, 384)]
    linear_fm(p, ws, [(w[4 + i], i) for i in range(4)], hT, rh, rest_tiles, mk("k1T", NCR))
    linear_tm(p, ws, [(w[8 + i], i) for i in range(4)], hT, rh, 0, nrest, mk_v(NCR))
    p.barrier()


def sb_phase(p, c, d):
    p.reset()
    S = 4
    p.rot = list(range(8 - S))
    scale = HD ** -0.5
    NQT = NCR // 128
    qT = [p.alloc(NCR, BF16) for _ in range(2)]
    kT = [p.alloc(NW, BF16) for _ in range(2)]
    Vh = [p.alloc(32 * 128, BF16).rearrange("p (s d) -> p s d", s=32) for _ in range(2)]
    r_q = [Res("sb_q%d" % i) for i in range(2)]
    r_k = [Res("sb_k%d" % i) for i in range(2)]
    r_v = [Res("sb_v%d" % i) for i in range(2)]
    yst = [p.alloc(NCR, BF16) for _ in range(2)]
    r_y = [Res("sb_y%d" % i) for i in range(2)]
    streams = []
    for si in range(S):
        streams.append(dict(
            om=[p.alloc(512, F32) for _ in range(2)], r_om=[Res("sb_om%d_%d" % (si, i)) for i in range(2)],
            Cx=[p.alloc(513, F32) for _ in range(2)], r_cx=[Res("sb_cx%d_%d" % (si, i)) for i in range(2)],
            ab=[p.alloc(512, BF16) for _ in range(2)], r_ab=[Res("sb_ab%d_%d" % (si, i)) for i in range(2)],
            aT=[p.alloc(512, BF16) for _ in range(2)], r_aT=[Res("sb_aT%d_%d" % (si, i)) for i in range(2)],
            acc=8 - S + si, n=0))
    regcache = {}

    def qgen(h, hs, qt, st):
        i0 = qt * 128
        qtile = qT[hs][:, i0:i0 + 128]
        nkeys = NW - i0
        po, rpo = p.fixed(st["acc"])
        nsub_total = nkeys // 128
        sub_done = 0
        k0 = i0
        prev = None
        while k0 < NW:
            nk = min(512, NW - k0)
            s = st["n"] % 2
            st["n"] += 1
            om, r_om = st["om"][s], st["r_om"][s]
            Cx, r_cx = st["Cx"][s], st["r_cx"][s]
            ab, r_ab = st["ab"][s], st["r_ab"][s]
            aT, r_aT = st["aT"][s], st["r_aT"][s]
            pz, rpz = p.bank()
            p.mm(pz[:, 0:nk], qtile, kT[hs][:, k0:k0 + nk], True, True, [r_q[hs], r_k[hs]], [rpz])
            yield
            p.act(om[:, 0:nk], pz[:, 0:nk], AF.Sigmoid, [rpz], [r_om], scale=-scale)
            if k0 == i0:
                def sel_f(e, o=om[:, 0:128]):
                    if "one" not in regcache:
                        regcache["one"] = e.to_reg(1.0)
                    return e.affine_select(out=o, in_=o, pattern=[[1, 128]], base=0, channel_multiplier=-1,
                                           compare_op=ALU.is_gt, fill=regcache["one"])
                p.add("pool", sel_f, [r_om], [r_om])
                p.memset("dve", Cx[:, 0:1], 1.0, [r_cx])
            else:
                pcx, prcx, pnk = prev
                p.copy("dve", Cx[:, 0:1], pcx[:, pnk:pnk + 1], [prcx], [r_cx])
            yield
            p.add("dve", lambda e, o=Cx[:, 1:nk + 1], a=om[:, 0:nk], z=c.zeros[:, 0:nk], ini=Cx[:, 0:1]:
                  e.tensor_tensor_scan(out=o, data0=a, data1=z, initial=ini, op0=ALU.mult, op1=ALU.add),
                  [r_om, r_cx, c.r], [r_cx])
            prev = (Cx, r_cx, nk)
            yield
            p.tt("pool", ab[:, 0:nk], Cx[:, 0:nk], Cx[:, 1:nk + 1], ALU.subtract, [r_cx], [r_ab])
            yield
            pt_ps, rpt = p.bank()
            ptv = pt_ps[:, 0:256].bitcast(BF16)
            nsub = nk // 128
            for j in range(nsub):
                p.tr(ptv[:, j * 128:(j + 1) * 128], ab[:, j * 128:(j + 1) * 128], c.ident_bf, [r_ab, c.r], [rpt])
            yield
            p.copy("act", aT[:, 0:nk], ptv[:, 0:nk], [rpt], [r_aT])
            yield
            for j in range(nsub):
                sbi = (k0 // 128) + j
                p.mm(po[:, 0:128], Vh[hs][:, sbi, :], aT[:, j * 128:(j + 1) * 128], sub_done == 0,
                     sub_done == nsub_total - 1, [r_aT, r_v[hs]], [rpo])
                sub_done += 1
            k0 += nk
            yield
        p.copy("dve", yst[hs][:, i0:i0 + 128], po[:, 0:128], [rpo], [r_y[hs]])
        yield

    for h in range(16):
        hs = h % 2
        p.dma("sp", qT[hs], d["q1T"][h * 128:(h + 1) * 128, 0:NCR], [], [r_q[hs]], r_q[hs])
        p.dma("sp", kT[hs], d["k1T"][h * 128:(h + 1) * 128, :], [], [r_k[hs]], r_k[hs])
        p.dma("sp", Vh[hs], d["V1"][:, h * 128:(h + 1) * 128].rearrange("(s q) e -> q s e", q=128), [], [r_v[hs]], r_v[hs])
        run_interleaved(range(NQT), S, lambda qt, sl, h=h, hs=hs: qgen(h, hs, qt, streams[sl]))
        p.dma("sp", d["y1T"][h * 128:(h + 1) * 128, 0:NCR], yst[hs], [r_y[hs]], [], r_y[hs])
    p.barrier()


def final_norm_phase(p, c, d, xin):
    p.reset()
    g = c.small["finaln"]
    xv = d[xin].rearrange("(k p) t -> p k t", p=128)
    ov = d["outT"].rearrange("(k p) t -> p k t", p=128)
    xt = [p.alloc(KC * 256, F32).rearrange("p (k t) -> p k t", k=KC) for _ in range(2)]
    sqf = [p.alloc(KC * 256, F32).rearrange("p (k t) -> p k t", k=KC) for _ in range(2)]
    rs = [p.alloc(256, F32) for _ in range(2)]
    r_xt = [Res("fn_xt%d" % i) for i in range(2)]
    r_sq = [Res("fn_sq%d" % i) for i in range(2)]
    r_rs = [Res("fn_rs%d" % i) for i in range(2)]
    for it in range(NOWN // 256):
        s = it % 2
        t = it * 256
        p.dma("sp", xt[s], xv[:, :, t:t + 256], [], [r_xt[s]], r_xt[s])
        p.act(sqf[s], xt[s], AF.Square, [r_xt[s]], [r_sq[s]])
        ps, rps = p.bank()
        for k in range(KC):
            p.mm(ps[:, 0:256], c.ones_f, sqf[s][:, k, :], k == 0, k == KC - 1, [r_sq[s], c.r], [rps])
        p.act(rs[s], ps[:, 0:256], AF.Sqrt, [rps, c.r], [r_rs[s]], scale=1.0 / D, bias=c.eps_rms)
        p.add("dve", lambda e, o=rs[s]: e.reciprocal(out=o, in_=o), [r_rs[s]], [r_rs[s]])
        for k in range(KC):
            p.stt(sqf[s][:, k, :], xt[s][:, k, :], g[:, k:k + 1], rs[s], ALU.mult, ALU.mult,
                  [r_xt[s], r_rs[s], c.r], [r_sq[s]])
        p.dma("sp", ov[:, :, t:t + 256], sqf[s], [r_sq[s]], [], r_sq[s])
    p.barrier()


SMALL = {"mixn0": 16, "ffnn0": 16, "mixn1": 16, "ffnn1": 16, "finaln": 16, "conv_w": 8 * 31, "conv_b": 8,
         "ln_g": 8, "ln_b": 8, "fconv_w0": FC * 3, "fconv_b0": FC, "fconv_w1": FC * 3, "fconv_b1": FC,
         "valid": 1, "blkvalid": 16}
W_SHAPES = {"w_in": [10, 128, KC * 512], "w_out": [4, 128, KC * 512], "w_up0": [22, 128, KC * 256],
            "w_gate0": [22, 128, KC * 256], "w_down0": [16, 128, FC * 128], "w_qkv": [12, 128, KC * 512],
            "w_o": [4, 128, KC * 512], "w_up1": [22, 128, KC * 256], "w_gate1": [22, 128, KC * 256],
            "w_down1": [16, 128, FC * 128]}
T4 = [(0, 512), (512, 512), (1024, 512), (1536, 512)]


def build_fused(debug=False):
    nc = bass.Bass("TRN2", target_bir_lowering=False)
    d = {}
    d["xT"] = nc.dram_tensor("xT", [D, NW], F32, kind="ExternalInput").ap()
    for k, shp in W_SHAPES.items():
        d[k] = nc.dram_tensor(k, shp, F32, kind="ExternalInput").ap()
    small = {}
    for k, n in SMALL.items():
        small[k] = nc.dram_tensor("s_" + k, [128, n], F32, kind="ExternalInput").ap()

    def scr(name, shape, dt, out=False):
        d[name] = nc.dram_tensor(name, shape, dt, kind="ExternalOutput" if (out or debug) else "Internal").ap()

    scr("sT", [1024, NW + 32], F32)
    scr("qT", [1024, NW], BF16)
    scr("kT", [1024, NW], BF16)
    scr("V", [NW, 1024], BF16)
    scr("yT", [D, NW], BF16)
    scr("x1T", [D, NW], F32)
    scr("actT", [DFF, NW], BF16)
    scr("x2T", [D, NW], F32)
    scr("q1T", [D, NCR], BF16)
    scr("k1T", [D, NW], BF16)
    scr("V1", [NW, D], BF16)
    scr("y1T", [D, NCR], BF16)
    scr("x3T", [D, NCR], F32)
    scr("x4T", [D, NCR], F32)
    scr("outT", [D, NOWN], F32, out=True)

    def body(p):
        c = setup_consts(p, small)
        inproj_phase(p, c, d)
        moba_phase(p, c, d)
        outproj_phase(p, c, d, "yT", "w_out", "xT", "x1T", [(0, T4), (2048, T4)])
        ffn_phase(p, c, d, 0, "x1T", "x2T",
                  [(2048, T4, "save", [T4[0:2], T4[2:4]]), (0, T4, "use", [T4[0:2], T4[2:4]])])
        qkv_phase(p, c, d)
        sb_phase(p, c, d)
        outproj_phase(p, c, d, "y1T", "w_o", "x2T", "x3T", [(0, CR_TILES)])
        ffn_phase(p, c, d, 1, "x3T", "x4T", [(0, CR_TILES, "cr", [CR_TILES[0:2], CR_TILES[2:5]])])
        final_norm_phase(p, c, d, "x4T")

    build_program(nc, body)
    return nc


def wblk(W, cb):
    K, Fo = W.shape
    kc = K // 128
    nb = Fo // cb
    return np.ascontiguousarray(W.reshape(kc, 128, nb, cb).transpose(2, 1, 0, 3).reshape(nb, 128, kc * cb))


def pvec(v):
    n = v.shape[0] // 128
    return np.ascontiguousarray(v.reshape(n, 128).T)


def prep(inputs):
    f = lambda a: np.asarray(a, dtype=np.float32)
    x = f(inputs["x"])
    w_in = f(inputs["even_w_in"])[0]
    cols = []
    for i in range(4):
        for j in (2 * i, 2 * i + 1):
            cols.append(np.arange(j * 128, (j + 1) * 128))
        for j in (2 * i, 2 * i + 1):
            cols.append(np.arange(1024 + j * 128, 1024 + (j + 1) * 128))
    cols.append(np.arange(2048, 5120))
    cols = np.concatenate(cols)

    def cw(a, n):
        return np.ascontiguousarray(a.T.reshape(n, 128, a.shape[0]).transpose(1, 0, 2).reshape(128, n * a.shape[0]))

    shared = {
        "w_in": wblk(w_in[:, cols], 512),
        "w_out": wblk(f(inputs["even_w_out"])[0], 512),
        "w_qkv": wblk(f(inputs["odd_w_qkv"])[0], 512),
        "w_o": wblk(f(inputs["odd_w_o"])[0], 512),
        "s_mixn0": pvec(f(inputs["mix_norm"])[0]),
        "s_mixn1": pvec(f(inputs["mix_norm"])[1]),
        "s_finaln": pvec(f(inputs["final_norm"])),
        "s_conv_w": cw(f(inputs["even_conv_w"])[0], 8),
        "s_conv_b": pvec(f(inputs["even_conv_b"])[0]),
        "s_ln_g": pvec(f(inputs["even_ln_g"])[0]),
        "s_ln_b": pvec(f(inputs["even_ln_b"])[0]),
    }
    for L in range(2):
        shared["w_up%d" % L] = wblk(f(inputs["ffn_w_up"])[L], 256)
        shared["w_gate%d" % L] = wblk(f(inputs["ffn_w_gate"])[L], 256)
        shared["w_down%d" % L] = wblk(f(inputs["ffn_w_down"])[L], 128)
        shared["s_ffnn%d" % L] = pvec(f(inputs["ffn_norm"])[L])
        shared["s_fconv_w%d" % L] = cw(f(inputs["ffn_conv_w"])[L], FC)
        shared["s_fconv_b%d" % L] = pvec(f(inputs["ffn_conv_b"])[L])
    maps = []
    for core in range(8):
        b, half = core // 2, core % 2
        win = np.zeros((NW, D), np.float32)
        if half == 1:
            win[:] = x[b]
        else:
            win[NOWN:] = x[b, :NOWN]
        m = dict(shared)
        m["xT"] = np.ascontiguousarray(win[::-1].T)
        m["s_valid"] = np.full((128, 1), float(half), np.float32)
        bv = np.ones((128, 16), np.float32)
        if half == 0:
            bv[:, 8:] = 0.0
        m["s_blkvalid"] = bv
        maps.append(m)
    return maps


def assemble(resB):
    out = np.zeros((4, 4096, D), np.float32)
    for core in range(8):
        b, half = core // 2, core % 2
        oT = resB[core]["outT"]
        o = oT.T[::-1]
        out[b, half * NOWN:(half + 1) * NOWN] = o
    return out


def kernel(**inputs):
    nc = build_fused()
    maps = prep(inputs)
    r = run_bass_kernel_spmd(nc, maps, core_ids=list(range(8)))
    return assemble(r.results)
```

```python
import numpy as np
import ml_dtypes
from contextlib import ExitStack
import concourse.bass as bass
import concourse.mybir as mybir
from concourse.bass_utils import run_bass_kernel_spmd

F32 = mybir.dt.float32
BF16 = mybir.dt.bfloat16
AF = mybir.ActivationFunctionType
ALU = mybir.AluOpType
AX = mybir.AxisListType

D = 2048
KC = 16
NW = 4096
NOWN = 2048
NCR = 2176
NU = 2304
DFF = 5632
FC = 44
HD = 128
RMS_EPS = 1e-6
LN_EPS = 1e-5
CR_TILES = [(0, 512), (512, 512), (1024, 512), (1536, 512), (2048, 128)]
W_TILES = [(i * 512, 512) for i in range(8)]
SBUF_WORDS = 49 * 1024


class Res:
    __slots__ = ("name", "last_w", "readers", "sem", "sem_total", "last_dma")

    def __init__(self, name):
        self.name = name
        self.last_w = None
        self.readers = []
        self.sem = None
        self.sem_total = 0
        self.last_dma = None


class Ins:
    __slots__ = ("eng", "fn", "deps", "is_dma", "res", "sem", "sem_val", "need_sig", "sig_val")

    def __init__(self, eng, fn):
        self.eng = eng
        self.fn = fn
        self.deps = []
        self.is_dma = False
        self.res = None
        self.sem = None
        self.sem_val = 0
        self.need_sig = False
        self.sig_val = 0


class Prog:
    ENG = ["pe", "act", "dve", "pool", "sp"]
    COMPUTE = ["pe", "act", "dve", "pool"]

    def __init__(self, nc, es):
        self.nc = nc
        self.es = es
        self.streams = {e: [] for e in self.ENG}
        self.esem = {e: es.enter_context(nc.semaphore("prog_" + e)) for e in self.COMPUTE}
        self.dma_res = []
        self.sem_pool = []
        self.nsem = 0
        self.big = es.enter_context(nc.sbuf_tensor("bigbuf", [128, SBUF_WORDS], F32))
        self.top = 0
        self.floor = 0
        self.banks = [es.enter_context(nc.psum_tensor("bank%d" % i, [128, 512], F32)) for i in range(8)]
        self.rbanks = [Res("bank%d" % i) for i in range(8)]
        self.nbank = 0
        self.rot = list(range(8))
        self.free_sems = []

    def alloc(self, n, dtype):
        words = (n * (2 if dtype == BF16 else 4) + 3) // 4
        words = (words + 7) // 8 * 8
        a = self.big[:, self.top:self.top + words]
        self.top += words
        assert self.top <= SBUF_WORDS, "SBUF overflow %d" % self.top
        if dtype == BF16:
            return a.bitcast(BF16)[:, 0:n]
        return a[:, 0:n]

    def reset(self):
        self.top = self.floor
        self.rot = list(range(8))

    def bank(self):
        i = self.rot[self.nbank % len(self.rot)]
        self.nbank += 1
        return self.banks[i], self.rbanks[i]

    def fixed(self, i):
        return self.banks[i], self.rbanks[i]

    def res(self, name):
        return Res(name)

    def add(self, eng, fn, reads=(), writes=(), dma=None, ndma=1, extra=()):
        ins = Ins(eng, fn)
        deps = []
        for r in reads:
            if r.last_w is not None:
                deps.append(r.last_w)
        for w in writes:
            lw = w.last_w
            if lw is not None:
                if not (eng == "pe" and lw.eng == "pe" and not lw.is_dma and not w.readers):
                    deps.append(lw)
            for rd in w.readers:
                deps.append(rd)
        deps.extend(extra)
        if dma is not None:
            ins.is_dma = True
            ins.res = dma
            if dma.sem is None:
                if self.sem_pool:
                    dma.sem, dma.sem_total = self.sem_pool.pop(0)
                else:
                    dma.sem = self.es.enter_context(self.nc.semaphore("d%d" % self.nsem))
                    dma.sem_total = 0
                    self.nsem += 1
                self.dma_res.append(dma)
            if dma.last_dma is not None:
                deps.append(dma.last_dma)
            dma.sem_total += 16 * ndma
            ins.sem = dma.sem
            ins.sem_val = dma.sem_total
            dma.last_dma = ins
        for r in reads:
            r.readers.append(ins)
        for w in writes:
            w.last_w = ins
            w.readers = []
        seen = set()
        for d in deps:
            if d is ins or id(d) in seen:
                continue
            seen.add(id(d))
            ins.deps.append(d)
        self.streams[eng].append(ins)
        return ins

    def barrier(self):
        lasts = []
        for e in self.ENG:
            for ins in reversed(self.streams[e]):
                if not ins.is_dma and ins.fn is not None:
                    lasts.append(ins)
                    break
        dmas = [r.last_dma for r in self.dma_res if r.last_dma is not None]
        for e in self.ENG:
            self.add(e, None, extra=lasts + dmas)
        for r in self.dma_res:
            self.sem_pool.append((r.sem, r.sem_total))
            r.sem = None
            r.last_dma = None
        self.dma_res = []

    def mm(self, out, lhsT, rhs, start, stop, reads, writes):
        return self.add("pe", lambda e: e.matmul(out, lhsT=lhsT, rhs=rhs, start=start, stop=stop), reads, writes)

    def tr(self, out, in_, ident, reads, writes):
        return self.add("pe", lambda e: e.transpose(out, in_, ident), reads, writes)

    def act(self, out, in_, func, reads, writes, scale=1.0, bias=None, accum_out=None):
        def f(e):
            kw = {}
            if bias is not None:
                kw["bias"] = bias
            if accum_out is not None:
                kw["accum_out"] = accum_out
            return e.activation(out=out, in_=in_, func=func, scale=scale, **kw)
        return self.add("act", f, reads, writes)

    def tt(self, eng, out, in0, in1, op, reads, writes):
        return self.add(eng, lambda e: e.tensor_tensor(out=out, in0=in0, in1=in1, op=op), reads, writes)

    def ts(self, eng, out, in0, s1, s2, op0, op1, reads, writes):
        if op1 is None:
            return self.add(eng, lambda e: e.tensor_scalar(out=out, in0=in0, scalar1=s1, scalar2=None, op0=op0), reads, writes)
        return self.add(eng, lambda e: e.tensor_scalar(out=out, in0=in0, scalar1=s1, scalar2=s2, op0=op0, op1=op1), reads, writes)

    def stt(self, out, in0, scalar, in1, op0, op1, reads, writes):
        return self.add("dve", lambda e: e.scalar_tensor_tensor(out=out, in0=in0, scalar=scalar, in1=in1, op0=op0, op1=op1), reads, writes)

    def copy(self, eng, out, in_, reads, writes):
        if eng == "act":
            return self.add("act", lambda e: e.activation(out=out, in_=in_, func=AF.Copy), reads, writes)
        return self.add(eng, lambda e: e.tensor_copy(out=out, in_=in_), reads, writes)

    def memset(self, eng, ap, val, writes):
        return self.add(eng, lambda e: e.memset(ap, val), (), writes)

    def dma(self, q, out, in_, reads, writes, res, **kw):
        return self.add(q, lambda e: e.dma_start(out=out, in_=in_, **kw), reads, writes, dma=res)

    def _prepare(self):
        for e in self.ENG:
            for ins in self.streams[e]:
                for d in ins.deps:
                    if not d.is_dma:
                        d.need_sig = True
        for e in self.ENG:
            c = 0
            for ins in self.streams[e]:
                if not ins.is_dma and ins.need_sig:
                    assert ins.fn is not None
                    c += 1
                    ins.sig_val = c

    def _emit_one(self, e, eng):
        waited = {}
        for ins in self.streams[e]:
            need = {}
            for d in ins.deps:
                if d.is_dma:
                    key = ("d", id(d.sem))
                    sem = d.sem
                    val = d.sem_val
                else:
                    key = ("e", d.eng)
                    sem = self.esem[d.eng]
                    val = d.sig_val
                if waited.get(key, 0) >= val:
                    continue
                if key not in need or need[key][1] < val:
                    need[key] = (sem, val)
            for key, (sem, val) in need.items():
                eng.wait_ge(sem, val)
                waited[key] = val
            if ins.fn is None:
                continue
            r = ins.fn(eng)
            if ins.is_dma:
                r.then_inc(ins.sem, 16)
            elif ins.need_sig:
                r.then_inc(self.esem[ins.eng], 1)


def build_program(nc, body):
    with ExitStack() as es:
        p = Prog(nc, es)
        body(p)
        p.barrier()
        p._prepare()
        block = es.enter_context(nc.Block())

        def sect(name):
            def f(eng):
                p._emit_one(name, eng)
            return f

        block.tensor(sect("pe"))
        block.scalar(sect("act"))
        block.vector(sect("dve"))
        block.gpsimd(sect("pool"))
        block.sync(sect("sp"))
    return nc


class Consts:
    pass


def setup_consts(p, small_dram):
    c = Consts()
    c.r = Res("consts")
    c.ones_bf = p.alloc(128, BF16)
    c.ones_f = p.alloc(128, F32)
    c.ident_bf = p.alloc(128, BF16)
    c.zeros = p.alloc(512, F32)
    c.causal_ge = p.alloc(128, F32)
    c.eps_rms = p.alloc(1, F32)
    c.eps_ln = p.alloc(1, F32)
    identf = p.alloc(128, F32)
    c.ffn_halo = p.alloc(FC * 2, F32).rearrange("p (j k) -> p j k", j=FC)
    c.r_halo = Res("ffn_halo")
    p.memset("pool", c.ones_bf, 1.0, [c.r])
    p.memset("pool", c.ones_f, 1.0, [c.r])
    p.memset("pool", c.zeros, 0.0, [c.r])
    p.memset("pool", c.eps_rms, RMS_EPS, [c.r])
    p.memset("pool", c.eps_ln, LN_EPS, [c.r])
    p.memset("pool", c.causal_ge, 0.0, [c.r])
    p.add("pool", lambda e: e.affine_select(out=c.causal_ge, in_=c.causal_ge, pattern=[[1, 128]], base=0,
                                            channel_multiplier=-1, compare_op=ALU.is_ge, fill=-1e5), [c.r], [c.r])
    p.memset("pool", identf, 0.0, [c.r])
    p.add("pool", lambda e: e.affine_select(out=identf, in_=identf, pattern=[[1, 128]], base=0,
                                            channel_multiplier=-1, compare_op=ALU.not_equal, fill=1.0), [c.r], [c.r])
    p.copy("pool", c.ident_bf, identf, [c.r], [c.r])
    c.small = {}
    for name, ap in small_dram.items():
        n = ap.shape[1]
        t = p.alloc(n, F32)
        p.dma("sp", t, ap, [], [c.r], c.r)
        c.small[name] = t
    p.floor = p.top
    return c


def norm_bufs(p):
    xt = [p.alloc(KC * 256, F32).rearrange("p (k t) -> p k t", k=KC) for _ in range(2)]
    sq = [p.alloc(KC * 256, BF16).rearrange("p (k t) -> p k t", k=KC) for _ in range(2)]
    rs = [p.alloc(256, F32) for _ in range(2)]
    r_xt = [Res("n_xt%d" % i) for i in range(2)]
    r_sq = [Res("n_sq%d" % i) for i in range(2)]
    r_rs = [Res("n_rs%d" % i) for i in range(2)]
    return xt, sq, rs, r_xt, r_sq, r_rs


def norm_phase(p, c, x_dram, col0, ntok, g, hT, rh, hcol0, bufs=None):
    xv = x_dram.rearrange("(k p) t -> p k t", p=128)
    xt, sq, rs, r_xt, r_sq, r_rs = bufs if bufs is not None else norm_bufs(p)
    assert ntok % 128 == 0
    t = 0
    it = 0
    while t < ntok:
        n = min(256, ntok - t)
        s = it % 2
        it += 1
        p.dma("sp", xt[s][:, :, 0:n], xv[:, :, col0 + t:col0 + t + n], [], [r_xt[s]], r_xt[s])
        p.act(sq[s][:, :, 0:n], xt[s][:, :, 0:n], AF.Square, [r_xt[s]], [r_sq[s]])
        ps, rps = p.bank()
        for k in range(KC):
            p.mm(ps[:, 0:n], c.ones_bf, sq[s][:, k, 0:n], k == 0, k == KC - 1, [r_sq[s], c.r], [rps])
        p.act(rs[s][:, 0:n], ps[:, 0:n], AF.Sqrt, [rps, c.r], [r_rs[s]], scale=1.0 / D, bias=c.eps_rms)
        p.add("dve", lambda e, o=rs[s][:, 0:n]: e.reciprocal(out=o, in_=o), [r_rs[s]], [r_rs[s]])
        hc = hcol0 + t
        rr = rh[hc // 256]
        for k in range(KC):
            p.stt(hT[:, k, hc:hc + n], xt[s][:, k, 0:n], g[:, k:k + 1], rs[s][:, 0:n], ALU.mult, ALU.mult,
                  [r_xt[s], r_rs[s], c.r], [rr])
        t += n


def rh_reads(rh, c0, n):
    return [rh[i] for i in range(c0 // 256, (c0 + n - 1) // 256 + 1)]


class WStream:
    def __init__(self, p, nslots, kc, cb, name):
        self.p = p
        self.kc = kc
        self.cb = cb
        self.slots = [p.alloc(kc * cb, BF16) for _ in range(nslots)]
        self.res = [Res("%s_w%d" % (name, i)) for i in range(nslots)]
        self.n = 0

    def load(self, blk_ap):
        s = self.n % len(self.slots)
        self.n += 1
        self.p.dma("pool", self.slots[s], blk_ap, [], [self.res[s]], self.res[s], max_dma_last_dim=8192)
        return self.slots[s].rearrange("p (k f) -> p k f", k=self.kc), self.res[s]


def linear_fm(p, ws, blocks, hT, rh, tok_tiles, consume, tok_outer=False):
    nfo = ws.cb // 128
    pending = None
    loaded = []
    for bi, (bap, tag) in enumerate(blocks):
        if bi == 0:
            loaded.append(ws.load(bap))
        if bi + 1 < len(blocks):
            loaded.append(ws.load(blocks[bi + 1][0]))
        wv, rw = loaded[bi]
        order = ([(ti, fo) for ti in range(len(tok_tiles)) for fo in range(nfo)] if tok_outer
                 else [(ti, fo) for fo in range(nfo) for ti in range(len(tok_tiles))])
        for ti, fo in order:
            c0, n = tok_tiles[ti]
            ps, rps = p.bank()
            for k in range(ws.kc):
                rr = rh(k, c0, n) if callable(rh) else rh_reads(rh, c0, n)
                p.mm(ps[:, 0:n], wv[:, k, fo * 128:(fo + 1) * 128], hT[:, k, c0:c0 + n], k == 0, k == ws.kc - 1,
                     [rw] + rr, [rps])
            consume(tag, fo, ti, c0, n, ps, rps)


def linear_tm(p, ws, blocks, hT, rh, tok0, ntok, consume):
    loaded = []
    for bi, (bap, tag) in enumerate(blocks):
        if bi == 0:
            loaded.append(ws.load(bap))
        if bi + 1 < len(blocks):
            loaded.append(ws.load(blocks[bi + 1][0]))
        wv, rw = loaded[bi]
        for c in range(tok0, tok0 + ntok, 128):
            ps, rps = p.bank()
            for k in range(ws.kc):
                p.mm(ps[:, 0:ws.cb], hT[:, k, c:c + 128], wv[:, k, :], k == 0, k == ws.kc - 1,
                     [rw] + rh_reads(rh, c, 128), [rps])
            consume(tag, c, ps, rps)


class Stage:
    def __init__(self, p, nslots, n, dtype, name):
        self.t = [p.alloc(n, dtype) for _ in range(nslots)]
        self.r = [Res("%s%d" % (name, i)) for i in range(nslots)]
        self.i = 0

    def next(self):
        s = self.i % len(self.t)
        self.i += 1
        return self.t[s], self.r[s]


def inproj_phase(p, c, d):
    p.reset()
    NCH = 2048
    hT = p.alloc(KC * NCH, BF16).rearrange("p (k t) -> p k t", k=KC)
    rh = {i: Res("hT%d" % i) for i in range(NCH // 256)}
    ws = WStream(p, 2, KC, 512, "inp")
    st_bf = Stage(p, 4, 512, BF16, "st_bf")
    st_sg = Stage(p, 2, 512, F32, "st_sg")
    st_s = Stage(p, 2, 512, F32, "st_s")
    g = c.small["mixn0"]
    w = d["w_in"]
    tiles = [(0, 512), (512, 512), (1024, 512), (1536, 512)]
    nb_ = norm_bufs(p)
    rz = Res("zpad")
    for jj in range(8):
        p.dma("sp", d["sT"][jj * 128:(jj + 1) * 128, NW:NW + 32], c.zeros[:, 0:32], [c.r], [], rz)
    for base in (0, NCH):
        def cons_q(tag, fo, ti, c0, n, ps, rps, base=base):
            t, r = st_bf.next()
            p.copy("act", t[:, 0:n], ps[:, 0:n], [rps], [r])
            f0 = (tag[1] * 4 + fo) * 128
            dst = d["qT"] if tag[0] == "q" else d["kT"]
            p.dma("sp", dst[f0:f0 + 128, base + c0:base + c0 + n], t[:, 0:n], [r], [], r)

        def cons_v(tag, cc, ps, rps, base=base):
            t, r = st_bf.next()
            p.copy("dve", t[:, 0:512], ps[:, 0:512], [rps], [r])
            f0 = tag[1] * 512
            p.dma("sp", d["V"][base + cc:base + cc + 128, f0:f0 + 512], t[:, 0:512], [r], [], r)

        glu = {}

        def cons_u(tag, fo, ti, c0, n, ps, rps, base=base, glu=glu):
            glu[fo] = (ps, rps)
            if fo == 3:
                for j in range(2):
                    pa, ra = glu[j]
                    pg, rg = glu[2 + j]
                    sg, rsg = st_sg.next()
                    p.act(sg[:, 0:n], pg[:, 0:n], AF.Sigmoid, [rg], [rsg])
                    s, rs_ = st_s.next()
                    p.tt("dve", s[:, 0:n], pa[:, 0:n], sg[:, 0:n], ALU.mult, [ra, rsg], [rs_])
                    f0 = (tag[1] * 2 + j) * 128
                    p.dma("sp", d["sT"][f0:f0 + 128, base + c0:base + c0 + n], s[:, 0:n], [rs_], [], rs_)

        norm_phase(p, c, d["xT"], base, NCH, g, hT, rh, 0, nb_)
        linear_fm(p, ws, [(w[i], ("u", i)) for i in range(4)], hT, rh, tiles, cons_u, tok_outer=True)
        linear_fm(p, ws, [(w[4 + i], ("q", i)) for i in range(2)], hT, rh, tiles, cons_q)
        linear_fm(p, ws, [(w[6 + i], ("k", i)) for i in range(2)], hT, rh, tiles, cons_q)
        linear_tm(p, ws, [(w[8 + i], ("v", i)) for i in range(2)], hT, rh, 0, NCH, cons_v)
    p.barrier()


def conformer_gen(p, c, d):
    W = 31
    sb = [p.alloc(512 + 32, F32) for _ in range(3)]
    r_sb = [Res("cf_s%d" % i) for i in range(3)]
    h = p.alloc(8 * 512, F32).rearrange("p (j t) -> p j t", j=8)
    sq = p.alloc(8 * 512, F32).rearrange("p (j t) -> p j t", j=8)
    r_h = [Res("cf_h%d" % j) for j in range(8)]
    r_sq = [Res("cf_sq%d" % j) for j in range(8)]
    mean = p.alloc(512, F32)
    msq = p.alloc(512, F32)
    rstd = p.alloc(512, F32)
    r_mean, r_msq, r_rstd = Res("cf_mean"), Res("cf_msq"), Res("cf_rstd")
    tmp = [p.alloc(512, F32) for _ in range(2)]
    r_tmp = [Res("cf_tmp%d" % i) for i in range(2)]
    yst = Stage(p, 2, 512, BF16, "cf_y")
    cw, cb, lg, lb = c.small["conv_w"], c.small["conv_b"], c.small["ln_g"], c.small["ln_b"]
    cwv = cw.rearrange("p (j k) -> p j k", j=8)
    ld = 0
    for (c0, n) in W_TILES:
        for j in range(8):
            s = ld % 3
            ld += 1
            p.dma("sp", sb[s][:, 0:n + 30], d["sT"][j * 128:(j + 1) * 128, c0:c0 + n + 30], [], [r_sb[s]], r_sb[s])
            hj = h[:, j, 0:n]
            p.ts("dve", hj, sb[s][:, 30:30 + n], cwv[:, j, 0:1], cb[:, j:j + 1], ALU.mult, ALU.add,
                 [r_sb[s], c.r], [r_h[j]])
            for k in range(1, W):
                p.stt(hj, sb[s][:, 30 - k:30 - k + n], cwv[:, j, k:k + 1], hj, ALU.mult, ALU.add,
                      [r_sb[s], r_h[j], c.r], [r_h[j]])
                if k % 2 == 0:
                    yield
            p.act(sq[:, j, 0:n], hj, AF.Square, [r_h[j]], [r_sq[j]])
        ps1, rp1 = p.bank()
        for j in range(8):
            p.mm(ps1[:, 0:n], c.ones_f, h[:, j, 0:n], j == 0, j == 7, [r_h[j], c.r], [rp1])
        ps2, rp2 = p.bank()
        for j in range(8):
            p.mm(ps2[:, 0:n], c.ones_f, sq[:, j, 0:n], j == 0, j == 7, [r_sq[j], c.r], [rp2])
        p.add("act", lambda e, o=mean[:, 0:n], i=ps1[:, 0:n]: e.activation(out=o, in_=i, func=AF.Copy, scale=1.0 / 1024),
              [rp1], [r_mean])
        p.tt("dve", msq[:, 0:n], mean[:, 0:n], mean[:, 0:n], ALU.mult, [r_mean], [r_msq])
        p.stt(rstd[:, 0:n], ps2[:, 0:n], 1.0 / 1024, msq[:, 0:n], ALU.mult, ALU.subtract, [rp2, r_msq], [r_rstd])
        p.act(rstd[:, 0:n], rstd[:, 0:n], AF.Sqrt, [r_rstd, c.r], [r_rstd], bias=c.eps_ln)
        p.add("dve", lambda e, o=rstd[:, 0:n]: e.reciprocal(out=o, in_=o), [r_rstd], [r_rstd])
        yield
        for j in range(8):
            if j % 2 == 0:
                yield
            t = tmp[j % 2][:, 0:n]
            rt = r_tmp[j % 2]
            p.tt("dve", t, h[:, j, 0:n], mean[:, 0:n], ALU.subtract, [r_h[j], r_mean], [rt])
            p.tt("dve", t, t, rstd[:, 0:n], ALU.mult, [rt, r_rstd], [rt])
            y, ry = yst.next()
            p.act(y[:, 0:n], t, AF.Silu, [rt, c.r], [ry], scale=lg[:, j:j + 1], bias=lb[:, j:j + 1])
            p.dma("sp", d["yT"][j * 128:(j + 1) * 128, c0:c0 + n], y[:, 0:n], [ry], [], ry)
    yield


def run_interleaved(items, S, mk, stagger=True, side=None):
    free = list(range(S))
    active = []
    it = iter(items)
    done = False
    sweeps = 0
    while True:
        while free and not done and (not stagger or not active or sweeps >= 2):
            sweeps = 0
            x = next(it, None)
            if x is None:
                done = True
                break
            sl = free.pop(0)
            active.append((mk(x, sl), sl))
            if stagger and free and not done:
                break
        if not active:
            break
        sweeps += 1
        if side is not None and side[0] is not None:
            try:
                next(side[0])
            except StopIteration:
                side[0] = None
        for g, sl in list(active):
            try:
                next(g)
            except StopIteration:
                active.remove((g, sl))
                free.append(sl)


NSTREAM = 3


def moba_phase(p, c, d, with_conformer=True):
    p.reset()
    S = 3
    side = [conformer_gen(p, c, d)] if with_conformer else [None]
    if with_conformer:
        next(side[0])
    p.rot = list(range(8 - S))
    BIG = 1.0e30
    NEGB = 3.0e4
    scale = HD ** -0.5
    NQT = NW // 128
    qT = [p.alloc(NW, BF16) for _ in range(2)]
    kT = [p.alloc(NW, BF16) for _ in range(2)]
    Vh = [p.alloc(32 * 128, BF16).rearrange("p (s d) -> p s d", s=32) for _ in range(2)]
    r_q = [Res("mb_q%d" % i) for i in range(2)]
    r_k = [Res("mb_k%d" % i) for i in range(2)]
    r_v = [Res("mb_v%d" % i) for i in range(2)]
    kmf = [p.alloc(16, F32) for _ in range(2)]
    kmb = [p.alloc(16, BF16) for _ in range(2)]
    r_km = [Res("mb_km%d" % i) for i in range(2)]
    vbias = p.alloc(16, F32)
    r_vb = Res("mb_vb")
    bv = c.small["blkvalid"]
    p.ts("dve", vbias, bv, -1.0, BIG, ALU.add, ALU.mult, [c.r], [r_vb])
    yst = [p.alloc(NW, BF16) for _ in range(2)]
    r_y = [Res("mb_y%d" % i) for i in range(2)]
    streams = []
    for si in range(S):
        st = dict(
            gm=p.alloc(16, F32), top8=p.alloc(8, F32), sel=p.alloc(16, F32), r_g=Res("mb_g%d" % si),
            pexp=[p.alloc(512, BF16) for _ in range(2)], r_pe=[Res("mb_p%d_%d" % (si, i)) for i in range(2)],
            dtmp=p.alloc(128, F32), r_dt=Res("mb_dt%d" % si),
            pT=[p.alloc(512, BF16) for _ in range(2)], r_pT=[Res("mb_pT%d_%d" % (si, i)) for i in range(2)],
            rsum=p.alloc(24, F32), r_rs=Res("mb_rs%d" % si), rinv=p.alloc(1, F32),
            ob=p.alloc(128, BF16), r_ob=Res("mb_ob%d" % si), acc=8 - S + si, n=0)
        streams.append(st)

    def qgen(h, hs, qt, st):
        i0 = qt * 128
        nb = i0 // 256
        qtile = qT[hs][:, i0:i0 + 128]
        gm, top8, sel, r_g = st["gm"], st["top8"], st["sel"], st["r_g"]
        npast = 15 - nb
        p.memset("pool", gm, -BIG, [r_g])
        if npast > 0:
            psg, rpg = p.bank()
            p.mm(psg[:, 0:16], qtile, kmb[hs], True, True, [r_q[hs], r_km[hs]], [rpg])
            p.tt("dve", gm[:, nb + 1:16], psg[:, nb + 1:16], vbias[:, nb + 1:16], ALU.add, [rpg, r_vb], [r_g])
        p.add("dve", lambda e, o=top8, i=gm: e.max(out=o, in_=i), [r_g], [r_g])
        p.ts("dve", sel, gm, top8[:, 2:3], None, ALU.is_ge, None, [r_g], [r_g])
        p.tt("dve", sel, sel, bv, ALU.mult, [r_g, c.r], [r_g])
        p.ts("dve", sel, sel, -1.0, NEGB, ALU.add, ALU.mult, [r_g], [r_g])
        yield
        segs = [(i0, 128, "diag")]
        k = i0 + 128
        if qt % 2 == 0:
            segs.append((k, 128, "own"))
            k += 128
        while k < NW:
            segs.append((k, 256, k // 256))
            k += 256
        tiles = []
        cur = []
        curn = 0
        for sg in segs:
            if curn + sg[1] > 512:
                tiles.append(cur)
                cur = []
                curn = 0
            cur.append(sg)
            curn += sg[1]
        tiles.append(cur)
        po, rpo = p.fixed(st["acc"])
        nsub_total = sum(s_[1] for s_ in segs) // 128
        sub_done = 0
        rcol = 0
        rsm, rrs = st["rsum"], st["r_rs"]
        for tl in tiles:
            k0 = tl[0][0]
            nk = sum(s_[1] for s_ in tl)
            pz, rpz = p.bank()
            p.mm(pz[:, 0:nk], qtile, kT[hs][:, k0:k0 + nk], True, True, [r_q[hs], r_k[hs]], [rpz])
            px = st["n"] % 2
            st["n"] += 1
            pexp, r_pe = st["pexp"][px], st["r_pe"][px]
            pT, r_pT = st["pT"][px], st["r_pT"][px]
            yield
            col = 0
            for (ks, kn, kind) in tl:
                if kind == "diag":
                    dt_ = st["dtmp"]
                    p.tt("dve", dt_, pz[:, col:col + 128], c.causal_ge, ALU.add, [rpz, c.r], [st["r_dt"]])
                    p.act(pexp[:, col:col + 128], dt_, AF.Exp, [st["r_dt"]], [r_pe, rrs], scale=scale,
                          accum_out=rsm[:, rcol:rcol + 1])
                elif kind == "own":
                    p.act(pexp[:, col:col + kn], pz[:, col:col + kn], AF.Exp, [rpz], [r_pe, rrs], scale=scale,
                          accum_out=rsm[:, rcol:rcol + 1])
                else:
                    p.act(pexp[:, col:col + kn], pz[:, col:col + kn], AF.Exp, [rpz, r_g], [r_pe, rrs],
                          scale=scale, bias=sel[:, kind:kind + 1], accum_out=rsm[:, rcol:rcol + 1])
                rcol += 1
                col += kn
            yield
            pt_ps, rpt = p.bank()
            ptv = pt_ps[:, 0:256].bitcast(BF16)
            nsub = nk // 128
            for j in range(nsub):
                p.tr(ptv[:, j * 128:(j + 1) * 128], pexp[:, j * 128:(j + 1) * 128], c.ident_bf, [r_pe, c.r], [rpt])
            yield
            p.copy("act", pT[:, 0:nk], ptv[:, 0:nk], [rpt], [r_pT])
            yield
            for j in range(nsub):
                sbi = (k0 // 128) + j
                p.mm(po[:, 0:128], pT[:, j * 128:(j + 1) * 128], Vh[hs][:, sbi, :], sub_done == 0,
                     sub_done == nsub_total - 1, [r_pT, r_v[hs]], [rpo])
                sub_done += 1
            yield
        rinv, ob, r_ob = st["rinv"], st["ob"], st["r_ob"]
        p.add("dve", lambda e, o=rinv, i=rsm[:, 0:rcol]: e.tensor_reduce(out=o, in_=i, axis=AX.X, op=ALU.add),
              [rrs], [rrs])
        p.add("dve", lambda e, o=rinv: e.reciprocal(out=o, in_=o), [rrs], [rrs])
        p.ts("dve", ob, po[:, 0:128], rinv[:, 0:1], None, ALU.mult, None, [rpo, rrs], [r_ob])
        pt2, rpt2 = p.bank()
        pt2v = pt2[:, 0:64].bitcast(BF16)
        p.tr(pt2v, ob, c.ident_bf, [r_ob, c.r], [rpt2])
        p.copy("act", yst[hs][:, i0:i0 + 128], pt2v, [rpt2], [r_y[hs]])
        yield

    for h in range(8):
        hs = h % 2
        p.dma("sp", qT[hs], d["qT"][h * 128:(h + 1) * 128, 0:NW], [], [r_q[hs]], r_q[hs])
        p.dma("sp", kT[hs], d["kT"][h * 128:(h + 1) * 128, :], [], [r_k[hs]], r_k[hs])
        p.dma("sp", Vh[hs], d["V"][:, h * 128:(h + 1) * 128].rearrange("(s q) e -> q s e", q=128), [], [r_v[hs]], r_v[hs])
        p.add("dve", lambda e, o=kmf[hs], i=kT[hs].rearrange("p (n s) -> p n s", n=16): e.tensor_reduce(out=o, in_=i, axis=AX.X, op=ALU.add),
              [r_k[hs]], [r_km[hs]])
        p.ts("dve", kmb[hs], kmf[hs], 1.0 / 256, None, ALU.mult, None, [r_km[hs]], [r_km[hs]])
        run_interleaved(range(NQT), S, lambda qt, sl, h=h, hs=hs: qgen(h, hs, qt, streams[sl]), side=side)
        p.dma("sp", d["yT"][1024 + h * 128:1024 + (h + 1) * 128, 0:NW], yst[hs], [r_y[hs]], [], r_y[hs])
    while side[0] is not None:
        try:
            next(side[0])
        except StopIteration:
            side[0] = None
    p.barrier()


def load_hT_from_dram(p, src, nrow_chunks, t0, ntok, name):
    hT = p.alloc(nrow_chunks * ntok, BF16).rearrange("p (k t) -> p k t", k=nrow_chunks)
    rk = [Res("%s%d" % (name, k)) for k in range(nrow_chunks)]
    for k in range(nrow_chunks):
        p.dma("sp", hT[:, k, :], src[k * 128:(k + 1) * 128, t0:t0 + ntok], [], [rk[k]], rk[k])
    return hT, (lambda k, c0, n: [rk[k]])


def outproj_phase(p, c, d, yname, wname, xin, xout, chunks):
    for (t0, tiles) in chunks:
        p.reset()
        ntok = sum(n for _, n in tiles)
        hT, rh = load_hT_from_dram(p, d[yname], KC, t0, ntok, "op_y")
        ws = WStream(p, 2, KC, 512, "op")
        xst = Stage(p, 3, 512, F32, "op_x")
        w = d[wname]

        def cons(tag, fo, ti, c0, n, ps, rps, t0=t0, xst=xst):
            f0 = (tag * 4 + fo) * 128
            t, r = xst.next()
            p.dma("sp", t[:, 0:n], d[xin][f0:f0 + 128, t0 + c0:t0 + c0 + n], [], [r], r)
            p.tt("dve", t[:, 0:n], ps[:, 0:n], t[:, 0:n], ALU.add, [rps, r], [r])
            p.dma("sp", d[xout][f0:f0 + 128, t0 + c0:t0 + c0 + n], t[:, 0:n], [r], [], r)

        linear_fm(p, ws, [(w[i], i) for i in range(4)], hT, rh, tiles, cons)
        p.barrier()


def ffn_phase(p, c, d, L, xin, xout, chunks):
    g = c.small["ffnn%d" % L]
    fw, fb = c.small["fconv_w%d" % L], c.small["fconv_b%d" % L]
    fwv = fw.rearrange("p (j k) -> p j k", j=FC)
    valid = c.small["valid"]
    wu, wg = d["w_up%d" % L], d["w_gate%d" % L]
    nblk = DFF // 256
    for (t0, tiles, halo, halves) in chunks:
        p.reset()
        ntok = sum(n for _, n in tiles)
        hT = p.alloc(KC * ntok, BF16).rearrange("p (k t) -> p k t", k=KC)
        rh = {i: Res("ff_h%d" % i) for i in range((ntok + 255) // 256)}
        mark = p.top
        norm_phase(p, c, d[xin], t0, ntok, g, hT, rh, 0)
        p.barrier()
        p.top = mark
        wsu = WStream(p, 2, KC, 256, "ffu")
        wsg = WStream(p, 2, KC, 256, "ffg")
        NB = ntok + 2
        upb = [p.alloc(NB, F32) for _ in range(2)]
        gb = [p.alloc(ntok, F32) for _ in range(2)]
        u = [p.alloc(ntok, F32) for _ in range(2)]
        ab = [p.alloc(ntok, BF16) for _ in range(2)]
        r_up = [Res("ff_up%d" % i) for i in range(2)]
        r_gb = [Res("ff_gb%d" % i) for i in range(2)]
        r_u = [Res("ff_u%d" % i) for i in range(2)]
        r_ab = [Res("ff_ab%d" % i) for i in range(2)]
        lu = [wsu.load(wu[0])]
        lg_ = [wsg.load(wg[0])]
        for b in range(nblk):
            if b + 1 < nblk:
                lu.append(wsu.load(wu[b + 1]))
                lg_.append(wsg.load(wg[b + 1]))
            for fo in range(2):
                j = b * 2 + fo
                s = j % 2
                wv, rw = lu[b]
                for (c0, n) in tiles:
                    ps, rps = p.bank()
                    for k in range(KC):
                        p.mm(ps[:, 0:n], wv[:, k, fo * 128:(fo + 1) * 128], hT[:, k, c0:c0 + n], k == 0, k == KC - 1,
                             [rw] + rh_reads(rh, c0, n), [rps])
                    p.copy("act", upb[s][:, c0:c0 + n], ps[:, 0:n], [rps], [r_up[s]])
                wv, rw = lg_[b]
                for (c0, n) in tiles:
                    ps, rps = p.bank()
                    for k in range(KC):
                        p.mm(ps[:, 0:n], wv[:, k, fo * 128:(fo + 1) * 128], hT[:, k, c0:c0 + n], k == 0, k == KC - 1,
                             [rw] + rh_reads(rh, c0, n), [rps])
                    p.copy("act", gb[s][:, c0:c0 + n], ps[:, 0:n], [rps], [r_gb[s]])
                if halo == "cr":
                    p.ts("dve", upb[s][:, NOWN:NOWN + 2], upb[s][:, NOWN:NOWN + 2], valid[:, 0:1], None, ALU.mult, None,
                         [r_up[s], c.r], [r_up[s]])
                    p.memset("pool", upb[s][:, ntok:NB], 0.0, [r_up[s]])
                elif halo == "save":
                    p.memset("pool", upb[s][:, ntok:NB], 0.0, [r_up[s]])
                    p.copy("pool", c.ffn_halo[:, j, :], upb[s][:, 0:2], [r_up[s]], [c.r_halo])
                else:
                    p.ts("dve", upb[s][:, ntok:NB], c.ffn_halo[:, j, :], valid[:, 0:1], None, ALU.mult, None,
                         [c.r_halo, c.r], [r_up[s]])
                us = u[s]
                p.ts("dve", us, upb[s][:, 0:ntok], fwv[:, j, 2:3], fb[:, j:j + 1], ALU.mult, ALU.add, [r_up[s], c.r], [r_u[s]])
                p.stt(us, upb[s][:, 1:ntok + 1], fwv[:, j, 1:2], us, ALU.mult, ALU.add, [r_up[s], r_u[s], c.r], [r_u[s]])
                p.stt(us, upb[s][:, 2:ntok + 2], fwv[:, j, 0:1], us, ALU.mult, ALU.add, [r_up[s], r_u[s], c.r], [r_u[s]])
                p.act(us, us, AF.Silu, [r_u[s]], [r_u[s]])
                p.tt("pool", ab[s], us, gb[s], ALU.mult, [r_u[s], r_gb[s]], [r_ab[s]])
                p.dma("sp", d["actT"][j * 128:(j + 1) * 128, t0:t0 + ntok], ab[s], [r_ab[s]], [], r_ab[s])
        p.barrier()
        for hv in halves:
            p.reset()
            h0 = hv[0][0]
            nt = sum(n for _, n in hv)
            aT = p.alloc(FC * nt, BF16).rearrange("p (k t) -> p k t", k=FC)
            rk = [Res("fd_a%d" % k) for k in range(FC)]
            for k in range(FC):
                p.dma("sp", aT[:, k, :], d["actT"][k * 128:(k + 1) * 128, t0 + h0:t0 + h0 + nt], [], [rk[k]], rk[k])
            ra = (lambda k, c0, n, rk=rk: [rk[k]])
            ws = WStream(p, 2, FC, 128, "ffd")
            xst = Stage(p, 3, 512, F32, "fd_x")
            wd = d["w_down%d" % L]
            rel_tiles = [(c0 - h0, n) for (c0, n) in hv]

            def cons(tag, fo, ti, c0, n, ps, rps, tb=t0 + h0, xst=xst):
                f0 = tag * 128
                t, r = xst.next()
                p.dma("sp", t[:, 0:n], d[xin][f0:f0 + 128, tb + c0:tb + c0 + n], [], [r], r)
                p.tt("dve", t[:, 0:n], ps[:, 0:n], t[:, 0:n], ALU.add, [rps, r], [r])
                p.dma("sp", d[xout][f0:f0 + 128, tb + c0:tb + c0 + n], t[:, 0:n], [r], [], r)

            linear_fm(p, ws, [(wd[i], i) for i in range(16)], aT, ra, rel_tiles, cons)
            p.barrier()


def qkv_phase(p, c, d):
    p.reset()
    hT = p.alloc(KC * NCR, BF16).rearrange("p (k t) -> p k t", k=KC)
    rh = {i: Res("qk_h%d" % i) for i in range((NCR + 255) // 256)}
    nb_ = norm_bufs(p)
    ws = WStream(p, 2, KC, 512, "qkv")
    st_bf = Stage(p, 4, 512, BF16, "qk_st")
    w = d["w_qkv"]
    valid = c.small["valid"]

    def mk(dst, base):
        def cons(tag, fo, ti, c0, n, ps, rps):
            t, r = st_bf.next()
            p.copy("act", t[:, 0:n], ps[:, 0:n], [rps], [r])
            f0 = (tag * 4 + fo) * 128
            p.dma("sp", d[dst][f0:f0 + 128, base + c0:base + c0 + n], t[:, 0:n], [r], [], r)
        return cons

    def mk_v(base):
        def cons_v(tag, cc, ps, rps):
            t, r = st_bf.next()
            if base + cc >= NOWN:
                p.ts("dve", t[:, 0:512], ps[:, 0:512], valid[:, 0:1], None, ALU.mult, None, [rps, c.r], [r])
            else:
                p.copy("dve", t[:, 0:512], ps[:, 0:512], [rps], [r])
            p.dma("sp", d["V1"][base + cc:base + cc + 128, tag * 512:(tag + 1) * 512], t[:, 0:512], [r], [], r)
        return cons_v

    norm_phase(p, c, d["x2T"], 0, NCR, c.small["mixn1"], hT, rh, 0, nb_)
    linear_fm(p, ws, [(w[i], i) for i in range(4)], hT, rh, CR_TILES, mk("q1T", 0))
    linear_fm(p, ws, [(w[4 + i], i) for i in range(4)], hT, rh, CR_TILES, mk("k1T", 0))
    linear_tm(p, ws, [(w[8 + i], i) for i in range(4)], hT, rh, 0, NCR, mk_v(0))
    nrest = NW - NCR
    norm_phase(p, c, d["x2T"], NCR, nrest, c.small["mixn1"], hT, rh, 0, nb_)
    rest_tiles = [(0, 512), (512, 512), (1024, 512), (1536, 384)]
    linear_fm(p, ws, [(w[4 + i], i) for i in range(4)], hT, rh, rest_tiles, mk("k1T", NCR))
    linear_tm(p, ws, [(w[8 + i], i) for i in range(4)], hT, rh, 0, nrest, mk_v(NCR))
    p.barrier()


def sb_phase(p, c, d):
    p.reset()
    S = 4
    p.rot = list(range(8 - S))
    scale = HD ** -0.5
    NQT = NCR // 128
    qT = [p.alloc(NCR, BF16) for _ in range(2)]
    kT = [p.alloc(NW, BF16) for _ in range(2)]
    Vh = [p.alloc(32 * 128, BF16).rearrange("p (s d) -> p s d", s=32) for _ in range(2)]
    r_q = [Res("sb_q%d" % i) for i in range(2)]
    r_k = [Res("sb_k%d" % i) for i in range(2)]
    r_v = [Res("sb_v%d" % i) for i in range(2)]
    yst = [p.alloc(NCR, BF16) for _ in range(2)]
    r_y = [Res("sb_y%d" % i) for i in range(2)]
    streams = []
    for si in range(S):
        streams.append(dict(
            om=[p.alloc(512, F32) for _ in range(2)], r_om=[Res("sb_om%d_%d" % (si, i)) for i in range(2)],
            Cx=[p.alloc(513, F32) for _ in range(2)], r_cx=[Res("sb_cx%d_%d" % (si, i)) for i in range(2)],
            ab=[p.alloc(512, BF16) for _ in range(2)], r_ab=[Res("sb_ab%d_%d" % (si, i)) for i in range(2)],
            aT=[p.alloc(512, BF16) for _ in range(2)], r_aT=[Res("sb_aT%d_%d" % (si, i)) for i in range(2)],
            acc=8 - S + si, n=0))
    regcache = {}

    def qgen(h, hs, qt, st):
        i0 = qt * 128
        qtile = qT[hs][:, i0:i0 + 128]
        nkeys = NW - i0
        po, rpo = p.fixed(st["acc"])
        nsub_total = nkeys // 128
        sub_done = 0
        k0 = i0
        prev = None
        while k0 < NW:
            nk = min(512, NW - k0)
            s = st["n"] % 2
            st["n"] += 1
            om, r_om = st["om"][s], st["r_om"][s]
            Cx, r_cx = st["Cx"][s], st["r_cx"][s]
            ab, r_ab = st["ab"][s], st["r_ab"][s]
            aT, r_aT = st["aT"][s], st["r_aT"][s]
            pz, rpz = p.bank()
            p.mm(pz[:, 0:nk], qtile, kT[hs][:, k0:k0 + nk], True, True, [r_q[hs], r_k[hs]], [rpz])
            yield
            p.act(om[:, 0:nk], pz[:, 0:nk], AF.Sigmoid, [rpz], [r_om], scale=-scale)
            if k0 == i0:
                def sel_f(e, o=om[:, 0:128]):
                    if "one" not in regcache:
                        regcache["one"] = e.to_reg(1.0)
                    return e.affine_select(out=o, in_=o, pattern=[[1, 128]], base=0, channel_multiplier=-1,
                                           compare_op=ALU.is_gt, fill=regcache["one"])
                p.add("pool", sel_f, [r_om], [r_om])
                p.memset("dve", Cx[:, 0:1], 1.0, [r_cx])
            else:
                pcx, prcx, pnk = prev
                p.copy("dve", Cx[:, 0:1], pcx[:, pnk:pnk + 1], [prcx], [r_cx])
            yield
            p.add("dve", lambda e, o=Cx[:, 1:nk + 1], a=om[:, 0:nk], z=c.zeros[:, 0:nk], ini=Cx[:, 0:1]:
                  e.tensor_tensor_scan(out=o, data0=a, data1=z, initial=ini, op0=ALU.mult, op1=ALU.add),
                  [r_om, r_cx, c.r], [r_cx])
            prev = (Cx, r_cx, nk)
            yield
            p.tt("pool", ab[:, 0:nk], Cx[:, 0:nk], Cx[:, 1:nk + 1], ALU.subtract, [r_cx], [r_ab])
            yield
            pt_ps, rpt = p.bank()
            ptv = pt_ps[:, 0:256].bitcast(BF16)
            nsub = nk // 128
            for j in range(nsub):
                p.tr(ptv[:, j * 128:(j + 1) * 128], ab[:, j * 128:(j + 1) * 128], c.ident_bf, [r_ab, c.r], [rpt])
            yield
            p.copy("act", aT[:, 0:nk], ptv[:, 0:nk], [rpt], [r_aT])
            yield
            for j in range(nsub):
                sbi = (k0 // 128) + j
                p.mm(po[:, 0:128], Vh[hs][:, sbi, :], aT[:, j * 128:(j + 1) * 128], sub_done == 0,
                     sub_done == nsub_total - 1, [r_aT, r_v[hs]], [rpo])
                sub_done += 1
            k0 += nk
            yield
        p.copy("dve", yst[hs][:, i0:i0 + 128], po[:, 0:128], [rpo], [r_y[hs]])
        yield

    for h in range(16):
        hs = h % 2
        p.dma("sp", qT[hs], d["q1T"][h * 128:(h + 1) * 128, 0:NCR], [], [r_q[hs]], r_q[hs])
        p.dma("sp", kT[hs], d["k1T"][h * 128:(h + 1) * 128, :], [], [r_k[hs]], r_k[hs])
        p.dma("sp", Vh[hs], d["V1"][:, h * 128:(h + 1) * 128].rearrange("(s q) e -> q s e", q=128), [], [r_v[hs]], r_v[hs])
        run_interleaved(range(NQT), S, lambda qt, sl, h=h, hs=hs: qgen(h, hs, qt, streams[sl]))
        p.dma("sp", d["y1T"][h * 128:(h + 1) * 128, 0:NCR], yst[hs], [r_y[hs]], [], r_y[hs])
    p.barrier()


def final_norm_phase(p, c, d, xin):
    p.reset()
    g = c.small["finaln"]
    xv = d[xin].rearrange("(k p) t -> p k t", p=128)
    ov = d["outT"].rearrange("(k p) t -> p k t", p=128)
    xt = [p.alloc(KC * 256, F32).rearrange("p (k t) -> p k t", k=KC) for _ in range(2)]
    sqf = [p.alloc(KC * 256, F32).rearrange("p (k t) -> p k t", k=KC) for _ in range(2)]
    rs = [p.alloc(256, F32) for _ in range(2)]
    r_xt = [Res("fn_xt%d" % i) for i in range(2)]
    r_sq = [Res("fn_sq%d" % i) for i in range(2)]
    r_rs = [Res("fn_rs%d" % i) for i in range(2)]
    for it in range(NOWN // 256):
        s = it % 2
        t = it * 256
        p.dma("sp", xt[s], xv[:, :, t:t + 256], [], [r_xt[s]], r_xt[s])
        p.act(sqf[s], xt[s], AF.Square, [r_xt[s]], [r_sq[s]])
        ps, rps = p.bank()
        for k in range(KC):
            p.mm(ps[:, 0:256], c.ones_f, sqf[s][:, k, :], k == 0, k == KC - 1, [r_sq[s], c.r], [rps])
        p.act(rs[s], ps[:, 0:256], AF.Sqrt, [rps, c.r], [r_rs[s]], scale=1.0 / D, bias=c.eps_rms)
        p.add("dve", lambda e, o=rs[s]: e.reciprocal(out=o, in_=o), [r_rs[s]], [r_rs[s]])
        for k in range(KC):
            p.stt(sqf[s][:, k, :], xt[s][:, k, :], g[:, k:k + 1], rs[s], ALU.mult, ALU.mult,
                  [r_xt[s], r_rs[s], c.r], [r_sq[s]])
        p.dma("sp", ov[:, :, t:t + 256], sqf[s], [r_sq[s]], [], r_sq[s])
    p.barrier()


SMALL = {"mixn0": 16, "ffnn0": 16, "mixn1": 16, "ffnn1": 16, "finaln": 16, "conv_w": 8 * 31, "conv_b": 8,
         "ln_g": 8, "ln_b": 8, "fconv_w0": FC * 3, "fconv_b0": FC, "fconv_w1": FC * 3, "fconv_b1": FC,
         "valid": 1, "blkvalid": 16}
W_SHAPES = {"w_in": [10, 128, KC * 512], "w_out": [4, 128, KC * 512], "w_up0": [22, 128, KC * 256],
            "w_gate0": [22, 128, KC * 256], "w_down0": [16, 128, FC * 128], "w_qkv": [12, 128, KC * 512],
            "w_o": [4, 128, KC * 512], "w_up1": [22, 128, KC * 256], "w_gate1": [22, 128, KC * 256],
            "w_down1": [16, 128, FC * 128]}
T4 = [(0, 512), (512, 512), (1024, 512), (1536, 512)]


def build_fused(debug=False):
    nc = bass.Bass("TRN2", target_bir_lowering=False)
    d = {}
    d["xT"] = nc.dram_tensor("xT", [D, NW], F32, kind="ExternalInput").ap()
    for k, shp in W_SHAPES.items():
        d[k] = nc.dram_tensor(k, shp, F32, kind="ExternalInput").ap()
    small = {}
    for k, n in SMALL.items():
        small[k] = nc.dram_tensor("s_" + k, [128, n], F32, kind="ExternalInput").ap()

    def scr(name, shape, dt, out=False):
        d[name] = nc.dram_tensor(name, shape, dt, kind="ExternalOutput" if (out or debug) else "Internal").ap()

    scr("sT", [1024, NW + 32], F32)
    scr("qT", [1024, NW], BF16)
    scr("kT", [1024, NW], BF16)
    scr("V", [NW, 1024], BF16)
    scr("yT", [D, NW], BF16)
    scr("x1T", [D, NW], F32)
    scr("actT", [DFF, NW], BF16)
    scr("x2T", [D, NW], F32)
    scr("q1T", [D, NCR], BF16)
    scr("k1T", [D, NW], BF16)
    scr("V1", [NW, D], BF16)
    scr("y1T", [D, NCR], BF16)
    scr("x3T", [D, NCR], F32)
    scr("x4T", [D, NCR], F32)
    scr("outT", [D, NOWN], F32, out=True)

    def body(p):
        c = setup_consts(p, small)
        inproj_phase(p, c, d)
        moba_phase(p, c, d)
        outproj_phase(p, c, d, "yT", "w_out", "xT", "x1T", [(0, T4), (2048, T4)])
        ffn_phase(p, c, d, 0, "x1T", "x2T",
                  [(2048, T4, "save", [T4[0:2], T4[2:4]]), (0, T4, "use", [T4[0:2], T4[2:4]])])
        qkv_phase(p, c, d)
        sb_phase(p, c, d)
        outproj_phase(p, c, d, "y1T", "w_o", "x2T", "x3T", [(0, CR_TILES)])
        ffn_phase(p, c, d, 1, "x3T", "x4T", [(0, CR_TILES, "cr", [CR_TILES[0:2], CR_TILES[2:5]])])
        final_norm_phase(p, c, d, "x4T")

    build_program(nc, body)
    return nc


def wblk(W, cb):
    K, Fo = W.shape
    kc = K // 128
    nb = Fo // cb
    return np.ascontiguousarray(W.reshape(kc, 128, nb, cb).transpose(2, 1, 0, 3).reshape(nb, 128, kc * cb))


def pvec(v):
    n = v.shape[0] // 128
    return np.ascontiguousarray(v.reshape(n, 128).T)


def prep(inputs):
    f = lambda a: np.asarray(a, dtype=np.float32)
    x = f(inputs["x"])
    w_in = f(inputs["even_w_in"])[0]
    cols = []
    for i in range(4):
        for j in (2 * i, 2 * i + 1):
            cols.append(np.arange(j * 128, (j + 1) * 128))
        for j in (2 * i, 2 * i + 1):
            cols.append(np.arange(1024 + j * 128, 1024 + (j + 1) * 128))
    cols.append(np.arange(2048, 5120))
    cols = np.concatenate(cols)

    def cw(a, n):
        return np.ascontiguousarray(a.T.reshape(n, 128, a.shape[0]).transpose(1, 0, 2).reshape(128, n * a.shape[0]))

    shared = {
        "w_in": wblk(w_in[:, cols], 512),
        "w_out": wblk(f(inputs["even_w_out"])[0], 512),
        "w_qkv": wblk(f(inputs["odd_w_qkv"])[0], 512),
        "w_o": wblk(f(inputs["odd_w_o"])[0], 512),
        "s_mixn0": pvec(f(inputs["mix_norm"])[0]),
        "s_mixn1": pvec(f(inputs["mix_norm"])[1]),
        "s_finaln": pvec(f(inputs["final_norm"])),
        "s_conv_w": cw(f(inputs["even_conv_w"])[0], 8),
        "s_conv_b": pvec(f(inputs["even_conv_b"])[0]),
        "s_ln_g": pvec(f(inputs["even_ln_g"])[0]),
        "s_ln_b": pvec(f(inputs["even_ln_b"])[0]),
    }
    for L in range(2):
        shared["w_up%d" % L] = wblk(f(inputs["ffn_w_up"])[L], 256)
        shared["w_gate%d" % L] = wblk(f(inputs["ffn_w_gate"])[L], 256)
        shared["w_down%d" % L] = wblk(f(inputs["ffn_w_down"])[L], 128)
        shared["s_ffnn%d" % L] = pvec(f(inputs["ffn_norm"])[L])
        shared["s_fconv_w%d" % L] = cw(f(inputs["ffn_conv_w"])[L], FC)
        shared["s_fconv_b%d" % L] = pvec(f(inputs["ffn_conv_b"])[L])
    maps = []
    for core in range(8):
        b, half = core // 2, core % 2
        win = np.zeros((NW, D), np.float32)
        if half == 1:
            win[:] = x[b]
        else:
            win[NOWN:] = x[b, :NOWN]
        m = dict(shared)
        m["xT"] = np.ascontiguousarray(win[::-1].T)
        m["s_valid"] = np.full((128, 1), float(half), np.float32)
        bv = np.ones((128, 16), np.float32)
        if half == 0:
            bv[:, 8:] = 0.0
        m["s_blkvalid"] = bv
        maps.append(m)
    return maps


def assemble(resB):
    out = np.zeros((4, 4096, D), np.float32)
    for core in range(8):
        b, half = core // 2, core % 2
        oT = resB[core]["outT"]
        o = oT.T[::-1]
        out[b, half * NOWN:(half + 1) * NOWN] = o
    return out


def kernel(**inputs):
    nc = build_fused()
    maps = prep(inputs)
    r = run_bass_kernel_spmd(nc, maps, core_ids=list(range(8)))
    return assemble(r.results)
```
